# Optimizing a Trainium2 kernel written in Bass

```python
import jax
import jax.numpy as jnp
from jax import lax
import numpy as np


D_MODEL = 1024
BATCH = 2
SEQ = 8192
DEPTH = 4

N_MIXERS = 3
N_MEM = 256
CHUNK = 64
GDN_DK = 128
GDN_DV = 128
GDN_HEADS = D_MODEL // GDN_DV
GDN_QK = GDN_HEADS * GDN_DK
GDN_V = GDN_HEADS * GDN_DV
CONV_K = 4
MLSTM_HEADS = 8
MLSTM_DV = D_MODEL // MLSTM_HEADS
MLSTM_DQK = MLSTM_DV // 2
MLSTM_QK = MLSTM_HEADS * MLSTM_DQK
MLSTM_V = MLSTM_HEADS * MLSTM_DV
FOX_DH = 64
FOX_HEADS = D_MODEL // FOX_DH
FOX_W = FOX_HEADS * FOX_DH
FOX_BLOCK = 128
MEM_HEADS = 4
MEM_DH = D_MODEL // 8
MEM_W = MEM_HEADS * MEM_DH
MIX_W = D_MODEL
D_FF = 4 * D_MODEL
ALPHA = (2 * DEPTH) ** 0.25
BETA_INIT = (8 * DEPTH) ** -0.25
LN_EPS = 1e-5
NORM_EPS = 1e-6

kernel_name = 'hybrid_gdn_mlstm_fox_memory_deepnorm'


def split_cols(t, sizes):
    return jnp.split(t, np.cumsum(sizes)[:-1].tolist(), axis=-1)


def layer_norm(x, g, b):
    xf = x.astype(jnp.float32)
    mu = jnp.mean(xf, -1, keepdims=True)
    var = jnp.mean(jnp.square(xf - mu), -1, keepdims=True)
    return ((xf - mu) * lax.rsqrt(var + LN_EPS)).astype(x.dtype) * g + b


def rms_norm(x, g):
    xf = x.astype(jnp.float32)
    return xf * lax.rsqrt(jnp.mean(jnp.square(xf), -1, keepdims=True) + NORM_EPS) * g


def head_layer_norm(x, g):
    mu = jnp.mean(x, -1, keepdims=True)
    var = jnp.mean(jnp.square(x - mu), -1, keepdims=True)
    return (x - mu) * lax.rsqrt(var + NORM_EPS) * g


def l2_normalize(x):
    return x * lax.rsqrt(jnp.sum(jnp.square(x), -1, keepdims=True) + NORM_EPS)


def causal_dwconv(x, w):
    k, c = w.shape
    return lax.conv_general_dilated(x, w[:, None, :].astype(x.dtype), window_strides=(1,),
                                    padding=[(k - 1, 0)], dimension_numbers=('NWC', 'WIO', 'NWC'),
                                    feature_group_count=c)


def to_chunks(t):
    bsz, seq, h = t.shape[:3]
    t = t.reshape(bsz, seq // CHUNK, CHUNK, h, *t.shape[3:])
    return jnp.moveaxis(t, 3, 1)


def from_chunks(t):
    n, bsz, h, c, d = t.shape
    return jnp.transpose(t, (1, 0, 3, 2, 4)).reshape(bsz, n * c, h, d)


def gated_delta_rule(q, k, v, g, beta):
    dk = q.shape[-1]
    dv = v.shape[-1]
    q = to_chunks(q) * dk ** -0.5
    k = to_chunks(k)
    v = to_chunks(v)
    g = to_chunks(g)
    beta = to_chunks(beta)
    gc = jnp.cumsum(g, -1)
    idx = jnp.arange(CHUNK)
    incl = idx[:, None] >= idx[None, :]
    strict = idx[:, None] > idx[None, :]
    decay = jnp.exp(jnp.where(incl, gc[..., :, None] - gc[..., None, :], -jnp.inf))
    kb = k * beta[..., None]
    a_mat = jnp.where(strict, jnp.einsum('bhncd,bhnsd->bhncs', kb, k) * decay, 0.0)
    rhs = jnp.concatenate([v * beta[..., None], kb * jnp.exp(gc)[..., None]], -1)
    sol = lax.linalg.triangular_solve(a_mat, rhs, left_side=True, lower=True, unit_diagonal=True)
    u, w = sol[..., :dv], sol[..., dv:]
    attn = jnp.einsum('bhncd,bhnsd->bhncs', q, k) * decay
    qg = q * jnp.exp(gc)[..., None]
    g_last = gc[..., -1:]
    kg = k * jnp.exp(g_last - gc)[..., None]
    d_last = jnp.exp(g_last[..., 0])
    xs = tuple(jnp.moveaxis(t, 2, 0) for t in (u, w, qg, kg, attn, d_last))
    bsz, h = q.shape[:2]
    s0 = jnp.zeros((bsz, h, dk, dv), jnp.float32)

    def step(state, inp):
        u_c, w_c, qg_c, kg_c, a_c, dl_c = inp
        v_new = u_c - jnp.einsum('bhcd,bhde->bhce', w_c, state)
        o_c = jnp.einsum('bhcd,bhde->bhce', qg_c, state) + jnp.einsum('bhcs,bhse->bhce', a_c, v_new)
        state = state * dl_c[..., None, None] + jnp.einsum('bhcd,bhce->bhde', kg_c, v_new)
        return state, o_c

    _, o = lax.scan(step, s0, xs)
    return from_chunks(o)


def mlstm_chunkwise(q, k, v, i_log, f_log):
    dqk = q.shape[-1]
    dv = v.shape[-1]
    q = to_chunks(q) * dqk ** -0.5
    k = to_chunks(k)
    v = to_chunks(v)
    i_log = to_chunks(i_log)
    f_log = to_chunks(f_log)
    b = jnp.cumsum(f_log, -1)
    idx = jnp.arange(CHUNK)
    incl = idx[:, None] >= idx[None, :]
    log_d = jnp.where(incl, b[..., :, None] - b[..., None, :] + i_log[..., None, :], -jnp.inf)
    m_intra = jnp.max(log_d, -1)
    qk = jnp.einsum('bhncd,bhnsd->bhncs', q, k)
    b_last = b[..., -1]
    log_w = b_last[..., None] - b + i_log
    m_w = jnp.max(log_w, -1)
    xs = tuple(jnp.moveaxis(t, 2, 0) for t in (q, k, v, b, log_d, m_intra, qk, b_last, log_w, m_w))
    bsz, h = q.shape[:2]
    init = (jnp.zeros((bsz, h, dqk, dv), jnp.float32), jnp.zeros((bsz, h, dqk), jnp.float32),
            jnp.zeros((bsz, h), jnp.float32))

    def step(carry, inp):
        c_st, n_st, m_prev = carry
        q_c, k_c, v_c, b_c, logd_c, mi_c, qk_c, bl_c, lw_c, mw_c = inp
        a_inter = b_c + m_prev[..., None]
        m_t = jnp.maximum(a_inter, mi_c)
        w_inter = jnp.exp(a_inter - m_t)
        p = qk_c * jnp.exp(logd_c - m_t[..., None])
        num = w_inter[..., None] * jnp.einsum('bhcd,bhde->bhce', q_c, c_st) + jnp.einsum('bhcs,bhse->bhce', p, v_c)
        den = w_inter * jnp.einsum('bhcd,bhd->bhc', q_c, n_st) + jnp.sum(p, -1)
        h_c = num / jnp.maximum(jnp.abs(den), jnp.exp(-m_t))[..., None]
        m_new = jnp.maximum(bl_c + m_prev, mw_c)
        keep = jnp.exp(bl_c + m_prev - m_new)
        kw = k_c * jnp.exp(lw_c - m_new[..., None])[..., None]
        c_st = keep[..., None, None] * c_st + jnp.einsum('bhcd,bhce->bhde', kw, v_c)
        n_st = keep[..., None] * n_st + jnp.sum(kw, -2)
        return (c_st, n_st, m_new), h_c

    _, hs = lax.scan(step, init, xs)
    return from_chunks(hs)


def forgetting_attention(q, k, v, f_log):
    bsz, seq, h, d = q.shape
    c = jnp.moveaxis(jnp.cumsum(f_log, axis=1), 1, 2)
    scale = d ** -0.5
    outs = []
    for blk in range(seq // FOX_BLOCK):
        lo, hi = blk * FOX_BLOCK, (blk + 1) * FOX_BLOCK
        logits = jnp.einsum('bqhd,bkhd->bhqk', q[:, lo:hi], k[:, :hi]) * scale
        logits = logits + c[:, :, lo:hi, None] - c[:, :, None, :hi]
        causal = (lo + jnp.arange(FOX_BLOCK))[:, None] >= jnp.arange(hi)[None, :]
        p = jax.nn.softmax(jnp.where(causal, logits, -jnp.inf), axis=-1)
        outs.append(jnp.einsum('bhqk,bkhd->bqhd', p, v[:, :hi]))
    return jnp.concatenate(outs, axis=1)


def gdn_mixer(x, w_in, conv_w, a_log, dt_bias, norm_g):
    bsz, seq, _ = x.shape
    proj = x @ w_in
    qkv, z, a, bg, memq = split_cols(proj, (2 * GDN_QK + GDN_V, GDN_V, GDN_HEADS, GDN_HEADS, MEM_W))
    qkv = jax.nn.silu(causal_dwconv(qkv, conv_w)).astype(jnp.float32)
    q, k, v = split_cols(qkv, (GDN_QK, GDN_QK, GDN_V))
    q = l2_normalize(q.reshape(bsz, seq, GDN_HEADS, GDN_DK))
    k = l2_normalize(k.reshape(bsz, seq, GDN_HEADS, GDN_DK))
    v = v.reshape(bsz, seq, GDN_HEADS, GDN_DV)
    g = -jnp.exp(a_log.astype(jnp.float32)) * jax.nn.softplus((a + dt_bias).astype(jnp.float32))
    beta = jax.nn.sigmoid(bg.astype(jnp.float32))
    o = gated_delta_rule(q, k, v, g, beta)
    o = rms_norm(o, norm_g) * jax.nn.silu(z.reshape(bsz, seq, GDN_HEADS, GDN_DV).astype(jnp.float32))
    return o.reshape(bsz, seq, GDN_V).astype(x.dtype), memq


def mlstm_mixer(x, w_in, b_gate, norm_g):
    bsz, seq, _ = x.shape
    proj = x @ w_in
    q, k, v, og, ig, fg, memq = split_cols(
        proj, (MLSTM_QK, MLSTM_QK, MLSTM_V, MLSTM_V, MLSTM_HEADS, MLSTM_HEADS, MEM_W))
    q = q.reshape(bsz, seq, MLSTM_HEADS, MLSTM_DQK).astype(jnp.float32)
    k = k.reshape(bsz, seq, MLSTM_HEADS, MLSTM_DQK).astype(jnp.float32)
    v = v.reshape(bsz, seq, MLSTM_HEADS, MLSTM_DV).astype(jnp.float32)
    i_log = (ig + b_gate[0]).astype(jnp.float32)
    f_log = jax.nn.log_sigmoid((fg + b_gate[1]).astype(jnp.float32))
    h = mlstm_chunkwise(q, k, v, i_log, f_log)
    h = head_layer_norm(h, norm_g) * jax.nn.sigmoid(og.reshape(bsz, seq, MLSTM_HEADS, MLSTM_DV).astype(jnp.float32))
    return h.reshape(bsz, seq, MLSTM_V).astype(x.dtype), memq


def fox_mixer(x, w_in, b_f, qk_g):
    bsz, seq, _ = x.shape
    proj = x @ w_in
    q, k, v, og, fg, memq = split_cols(proj, (FOX_W, FOX_W, FOX_W, FOX_W, FOX_HEADS, MEM_W))

    def heads(t):
        return t.reshape(bsz, seq, FOX_HEADS, FOX_DH).astype(jnp.float32)

    q = rms_norm(heads(q), qk_g[0])
    k = rms_norm(heads(k), qk_g[1])
    f_log = jax.nn.log_sigmoid((fg + b_f).astype(jnp.float32))
    o = forgetting_attention(q, k, heads(v), f_log) * jax.nn.sigmoid(heads(og))
    return o.reshape(bsz, seq, FOX_W).astype(x.dtype), memq


def memory_attention(memq, mem, w_kv):
    bsz, seq, _ = memq.shape
    kv = mem @ w_kv
    k, v = split_cols(kv, (MEM_W, MEM_W))
    k = k.reshape(bsz, -1, MEM_HEADS, MEM_DH)
    v = v.reshape(bsz, -1, MEM_HEADS, MEM_DH)
    q = memq.reshape(bsz, seq, MEM_HEADS, MEM_DH)
    logits = jnp.einsum('bshd,bmhd->bhsm', q, k).astype(jnp.float32) * MEM_DH ** -0.5
    p = jax.nn.softmax(logits, axis=-1).astype(v.dtype)
    return jnp.einsum('bhsm,bmhd->bshd', p, v).reshape(bsz, seq, MEM_W)


def _normal(key, shape, scale):
    return jax.random.normal(key, shape, jnp.float32) * scale


def setup_inputs(seed: int = 0) -> dict:
    key = jax.random.key(seed)
    ks = jax.random.split(key, 24)
    n_a, n_b, n_c = ((DEPTH + 2 - kind) // N_MIXERS for kind in range(N_MIXERS))
    gdn_cols = 2 * GDN_QK + 2 * GDN_V + 2 * GDN_HEADS + MEM_W
    mlstm_cols = 2 * MLSTM_QK + 2 * MLSTM_V + 2 * MLSTM_HEADS + MEM_W
    fox_cols = 4 * FOX_W + FOX_HEADS + MEM_W
    dt = jnp.exp(jax.random.uniform(ks[4], (n_a, GDN_HEADS), jnp.float32, np.log(1e-3), np.log(1e-1)))
    f_bias = jnp.linspace(3.0, 6.0, MLSTM_HEADS, dtype=jnp.float32) + _normal(ks[8], (n_b, MLSTM_HEADS), 0.1)
    i_bias = _normal(ks[7], (n_b, MLSTM_HEADS), 0.1)
    return {
        'x': _normal(ks[0], (BATCH, SEQ, D_MODEL), 1.0),
        'mem': _normal(ks[1], (BATCH, N_MEM, D_MODEL), 1.0),
        'gdn_w_in': _normal(ks[2], (n_a, D_MODEL, gdn_cols), D_MODEL ** -0.5),
        'gdn_conv_w': _normal(ks[3], (n_a, CONV_K, 2 * GDN_QK + GDN_V), CONV_K ** -0.5),
        'gdn_a_log': jnp.log(jax.random.uniform(ks[5], (n_a, GDN_HEADS), jnp.float32, 1.0, 16.0)),
        'gdn_dt_bias': dt + jnp.log(-jnp.expm1(-dt)),
        'gdn_norm_g': 1.0 + _normal(ks[6], (n_a, GDN_DV), 0.02),
        'mlstm_w_in': _normal(ks[9], (n_b, D_MODEL, mlstm_cols), D_MODEL ** -0.5),
        'mlstm_b_gate': jnp.stack([i_bias, f_bias], axis=1),
        'mlstm_norm_g': 1.0 + _normal(ks[10], (n_b, MLSTM_HEADS, MLSTM_DV), 0.02),
        'fox_w_in': _normal(ks[11], (n_c, D_MODEL, fox_cols), D_MODEL ** -0.5),
        'fox_b_f': jax.random.uniform(ks[12], (n_c, FOX_HEADS), jnp.float32, 1.0, 4.0),
        'fox_qk_g': 1.0 + _normal(ks[13], (n_c, 2, FOX_DH), 0.02),
        'mem_w_kv': _normal(ks[14], (DEPTH, D_MODEL, 2 * MEM_W), D_MODEL ** -0.5),
        'w_out': _normal(ks[15], (DEPTH, MIX_W + MEM_W, D_MODEL), (MIX_W + MEM_W) ** -0.5 * BETA_INIT),
        'ln1_g': 1.0 + _normal(ks[16], (DEPTH, D_MODEL), 0.02),
        'ln1_b': _normal(ks[17], (DEPTH, D_MODEL), 0.02),
        'w_up': _normal(ks[18], (DEPTH, D_MODEL, D_FF), D_MODEL ** -0.5),
        'w_down': _normal(ks[19], (DEPTH, D_FF, D_MODEL), D_FF ** -0.5 * BETA_INIT),
        'ln2_g': 1.0 + _normal(ks[20], (DEPTH, D_MODEL), 0.02),
        'ln2_b': _normal(ks[21], (DEPTH, D_MODEL), 0.02),
    }


def reference(x, mem, gdn_w_in, gdn_conv_w, gdn_a_log, gdn_dt_bias, gdn_norm_g,
              mlstm_w_in, mlstm_b_gate, mlstm_norm_g, fox_w_in, fox_b_f, fox_qk_g,
              mem_w_kv, w_out, ln1_g, ln1_b, w_up, w_down, ln2_g, ln2_b):
    for i in range(DEPTH):
        kind, j = i % N_MIXERS, i // N_MIXERS
        if kind == 0:
            mix, memq = gdn_mixer(x, gdn_w_in[j], gdn_conv_w[j], gdn_a_log[j], gdn_dt_bias[j], gdn_norm_g[j])
        elif kind == 1:
            mix, memq = mlstm_mixer(x, mlstm_w_in[j], mlstm_b_gate[j], mlstm_norm_g[j])
        else:
            mix, memq = fox_mixer(x, fox_w_in[j], fox_b_f[j], fox_qk_g[j])
        mem_out = memory_attention(memq, mem, mem_w_kv[i])
        y = jnp.concatenate([mix, mem_out], axis=-1) @ w_out[i]
        x = layer_norm(ALPHA * x + y, ln1_g[i], ln1_b[i])
        hdn = jnp.square(jax.nn.relu(x @ w_up[i]))
        x = layer_norm(ALPHA * x + hdn @ w_down[i], ln2_g[i], ln2_b[i])
    return x
```

```python
from contextlib import ExitStack
import numpy as np
import concourse.bass as bass
import concourse.mybir as mybir

F32 = mybir.dt.float32
BF16 = mybir.dt.bfloat16
AF = mybir.ActivationFunctionType
ALU = mybir.AluOpType
AX = mybir.AxisListType

ENGS = ['pe', 'act', 'dve', 'pool', 'sp']
SYNC_SAME_ENGINE = True
NDSEM = 32


class Buf:
    __slots__ = ('name', 'last_w', 'readers', 'sem', 'cnt')

    def __init__(self, name):
        self.name = name
        self.last_w = None
        self.readers = []
        self.sem = None
        self.cnt = 0


class Op:
    __slots__ = ('eng', 'fn', 'deps', 'is_dma', 'tok', 'signal', 'seq', 'pe_buf', 'inc')

    def __init__(self, eng, fn):
        self.eng = eng
        self.fn = fn
        self.deps = []
        self.is_dma = False
        self.tok = None
        self.signal = False
        self.seq = None
        self.pe_buf = None
        self.inc = 1


class Prog:
    def __init__(self, nc, stack):
        self.nc = nc
        self.stack = stack
        self.ops = {e: [] for e in ENGS}
        self.esem = {e: stack.enter_context(nc.semaphore('es_' + e)) for e in ENGS}
        self.ecnt = {e: 0 for e in ENGS}
        self.waited = {e: {} for e in ENGS}
        self.bufs = []
        self.dsem = [stack.enter_context(nc.semaphore('ds_%d' % i)) for i in range(NDSEM)]
        self.dlast = [None] * NDSEM
        self.dcount = 0
        self.nops = 0
        self.serial = False
        self.prev = None

    def buf(self, name):
        b = Buf(name)
        self.bufs.append(b)
        return b

    def _deps(self, op, reads, writes):
        if self.serial:
            if self.prev is not None:
                op.deps.append(self.prev)
            self.prev = op
        for b in reads:
            if b.last_w is not None:
                op.deps.append(b.last_w)
        for b in writes:
            if b.last_w is not None:
                op.deps.append(b.last_w)
            op.deps.extend(b.readers)
        for b in reads:
            b.readers.append(op)
        for b in writes:
            b.last_w = op
            b.readers = []

    def op(self, eng, fn, reads=(), writes=(), inc=1):
        o = Op(eng, fn)
        o.inc = inc
        self._deps(o, reads, writes)
        self.ops[eng].append(o)
        self.nops += 1
        return o

    def mm(self, fn, reads, out):
        o = Op('pe', fn)
        o.pe_buf = out
        self._deps(o, reads, [out])
        o.deps = [d for d in o.deps if not (d.eng == 'pe' and d.pe_buf is out)]
        self.ops['pe'].append(o)
        self.nops += 1
        return o

    def dma(self, queue, out, in_, reads, wbuf, **kw):
        n = self.dcount
        self.dcount += 1
        i = n % NDSEM
        sem = self.dsem[i]
        val = 16 * (n // NDSEM + 1)

        def fn(eng, out=out, in_=in_, kw=kw):
            return eng.dma_start(out=out, in_=in_, **kw)
        o = Op(queue, fn)
        o.is_dma = True
        o.tok = (sem, val)
        if self.dlast[i] is not None:
            o.deps.append(self.dlast[i])
        self.dlast[i] = o
        if not isinstance(wbuf, (list, tuple)):
            wbuf = [wbuf]
        self._deps(o, reads, wbuf)
        self.ops[queue].append(o)
        self.nops += 1
        return o

    def barrier(self):
        lasts = []
        for e in ENGS:
            if self.ops[e]:
                for o in reversed(self.ops[e]):
                    if not o.is_dma:
                        lasts.append(o)
                        break
        dl = [o for o in self.dlast if o is not None]
        for e in ENGS:
            o = Op(e, lambda eng: eng.nop())
            o.deps = list(lasts) + dl
            self.ops[e].append(o)
        for b in self.bufs:
            b.last_w = None
            b.readers = []

    def emit(self):
        nc = self.nc
        for e in ENGS:
            for o in self.ops[e]:
                for d in o.deps:
                    if not d.is_dma:
                        if d.eng == e and not SYNC_SAME_ENGINE:
                            continue
                        d.signal = True
        for e in ENGS:
            for o in self.ops[e]:
                if o.signal and not o.is_dma and o.seq is None:
                    self.ecnt[e] += o.inc
                    o.seq = self.ecnt[e]
        engobj = {'pe': nc.tensor, 'act': nc.scalar, 'dve': nc.vector, 'pool': nc.gpsimd, 'sp': nc.sync}
        stats = {'waits': 0}

        def run(e, eng):
            waited = self.waited[e]
            for o in self.ops[e]:
                need = {}
                for d in o.deps:
                    if d.is_dma:
                        sem, val = d.tok
                    else:
                        if d.eng == e and not SYNC_SAME_ENGINE:
                            continue
                        sem, val = self.esem[d.eng], d.seq
                    k = id(sem)
                    if waited.get(k, (None, 0))[1] >= val:
                        continue
                    if k not in need or need[k][1] < val:
                        need[k] = (sem, val)
                for k, (sem, val) in need.items():
                    eng.wait_ge(sem, val)
                    waited[k] = (sem, val)
                    stats['waits'] += 1
                ins = o.fn(eng)
                if o.is_dma:
                    ins.then_inc(o.tok[0], 16)
                elif o.signal:
                    ins.then_inc(self.esem[e], o.inc)
            self.ops[e] = []

        with nc.Block() as block:
            @block.tensor
            def _(eng):
                run('pe', eng)

            @block.scalar
            def _(eng):
                run('act', eng)

            @block.vector
            def _(eng):
                run('dve', eng)

            @block.gpsimd
            def _(eng):
                run('pool', eng)

            @block.sync
            def _(eng):
                run('sp', eng)
        return stats


D = 1024
DFF = 4096
ALPHA = 8 ** 0.25
LN_EPS = 1e-5
NORM_EPS = 1e-6
TT = 512


class Ctx:
    def __init__(self, nc, st, P):
        self.nc, self.st, self.P = nc, st, P
        self.ps = []
        for i in range(8):
            t = st.enter_context(nc.psum_tensor("ps%d" % i, [128, 512], F32))
            self.ps.append((t, P.buf("ps%d" % i)))
        self.psi = 0
        self.nring = 8
        self.cache = {}
        self.pcache = {}
        self.pools = {}
        self.sst = st
        self.stage_no = 0
        self.wr = []
        self.wi = 0
        self.ones32 = self.sbp("ones32", [128, 128], F32)
        self.onesb = self.sbp("onesb", [128, 128], BF16)
        b = P.buf("consts")
        self.bconst = b
        P.op('pool', lambda e: e.memset(self.ones32[:], 1.0), [], [b])
        P.op('pool', lambda e: e.memset(self.onesb[:], 1.0), [], [b])
        self.eps_ln = self.sbp('eps_ln', [128, 1], F32)
        self.eps_n = self.sbp('eps_n', [128, 1], F32)
        P.op('pool', lambda e: e.memset(self.eps_ln[:], LN_EPS), [], [b])
        P.op('pool', lambda e: e.memset(self.eps_n[:], NORM_EPS), [], [b])

    def sb(self, name, shape, dt):
        if name in self.pcache:
            return self.pcache[name]
        if name not in self.cache:
            self.cache[name] = self.sst.enter_context(self.nc.sbuf_tensor("%s_s%d" % (name, self.stage_no), shape, dt))
        return self.cache[name]

    def sbp(self, name, shape, dt):
        if name not in self.pcache:
            self.pcache[name] = self.st.enter_context(self.nc.sbuf_tensor(name, shape, dt))
        return self.pcache[name]

    def stage_begin(self):
        self.stage_no += 1
        self.sst = ExitStack()
        self.cache = {}
        self.wr = []
        self.wi = 0

    def stage_end(self):
        self.P.barrier()
        self.P.emit()
        self.sst.close()
        self.sst = self.st
        self.cache = {}
        self.wr = []

    def buf(self, name):
        k = 'buf:' + name
        if k not in self.cache:
            self.cache[k] = self.P.buf(name)
        return self.cache[k]

    def next_ps(self, pool=None):
        if pool is not None:
            banks, st = self.pools[pool]
            t, b = self.ps[banks[st[0] % len(banks)]]
            st[0] += 1
            return t, b
        t, b = self.ps[self.psi % self.nring]
        self.psi += 1
        return t, b

    def set_pool(self, name, banks):
        self.pools[name] = (list(banks), [0])

    def dbg(self, name, ap, bufs):
        d = getattr(self, 'dbgs', None)
        if d and name in d and name not in self.cache:
            self.cache[name] = True
            self.P.dma('sp', d[name], ap, list(bufs), self.buf('dbg_out'))

    def acc(self, i):
        return self.ps[self.nring + i]

    def load_w(self, w_ap, r0, nrows, c0, ncols):
        kc = nrows // 128
        if not self.wr:
            for i in range(4):
                self.wr.append((self.sb("wr%d" % i, [128, 4096], BF16), self.buf("wr%d" % i)))
        t, b = self.wr[self.wi % 4]
        self.wi += 1
        view = t[:, 0:kc * ncols].rearrange("p (k c) -> p k c", k=kc)
        src = w_ap[r0:r0 + nrows, c0:c0 + ncols].rearrange("(k p) c -> p k c", p=128)
        self.P.dma('pool', view, src, [], b)
        return view, b

    def ring(self, name, n, shape, dt):
        slots = [(self.sb("%s%d" % (name, i), shape, dt), self.buf("%s%d" % (name, i))) for i in range(n)]
        k = 'ring:' + name
        if k not in self.cache:
            self.cache[k] = {'i': 0}
        st = self.cache[k]

        def nxt():
            s = slots[st['i'] % n]
            st['i'] += 1
            return s
        return nxt


def layer_norm_fm(C, x32, bx, xb, bxb, g_t, b_t, bgb, ntok_tiles, t0, sq_ring, st_ring):
    P = C.P
    for tt in range(ntok_tiles):
        sl = slice(t0 + tt * TT, t0 + (tt + 1) * TT)
        s1, bs1 = C.next_ps()
        s2, bs2 = C.next_ps()
        for k in range(8):
            P.mm(lambda e, k=k, s1=s1, sl=sl: e.matmul(s1[:], C.ones32[:], x32[:, k, sl], start=(k == 0), stop=(k == 7)),
                 [bx[k][tt], C.bconst], bs1)
        for k in range(8):
            sq, bsq = sq_ring()
            P.op('act', lambda e, k=k, sq=sq, sl=sl: e.activation(out=sq[:], in_=x32[:, k, sl], func=AF.Square), [bx[k][tt]], [bsq])
            P.mm(lambda e, k=k, s2=s2, sq=sq, sl=sl: e.matmul(s2[:], C.ones32[:], sq[:], start=(k == 0), stop=(k == 7)),
                 [bsq, C.bconst], bs2)
        m, bm = st_ring()
        msq, bmsq = st_ring()
        A, bA = st_ring()
        Bc, bBc = st_ring()
        P.op('dve', lambda e, m=m, s1=s1, sl=sl: e.tensor_scalar(out=m[:], in0=s1[:], scalar1=1.0 / D, scalar2=None, op0=ALU.mult), [bs1], [bm])
        P.op('dve', lambda e, m=m, msq=msq, sl=sl: e.tensor_tensor(out=msq[:], in0=m[:], in1=m[:], op=ALU.mult), [bm], [bmsq])
        P.op('dve', lambda e, msq=msq, s2=s2, sl=sl: e.scalar_tensor_tensor(out=msq[:], in0=s2[:], scalar=1.0 / D, in1=msq[:], op0=ALU.mult, op1=ALU.subtract),
             [bs2, bmsq], [bmsq])
        P.op('act', lambda e, msq=msq, A=A, sl=sl: e.activation(out=A[:], in_=msq[:], func=AF.Ln, bias=C.eps_ln[:, 0:1], scale=1.0), [bmsq, C.bconst], [bA])
        P.op('act', lambda e, A=A, sl=sl: e.activation(out=A[:], in_=A[:], func=AF.Exp, scale=-0.5), [bA], [bA])
        P.op('dve', lambda e, m=m, A=A, Bc=Bc, sl=sl: e.scalar_tensor_tensor(out=Bc[:], in0=m[:], scalar=-1.0, in1=A[:], op0=ALU.mult, op1=ALU.mult),
             [bm, bA], [bBc])
        for k in range(8):
            u, bu = sq_ring()
            P.op('dve', lambda e, k=k, u=u, A=A, sl=sl: e.scalar_tensor_tensor(out=u[:], in0=x32[:, k, sl], scalar=g_t[:, k:k + 1], in1=A[:], op0=ALU.mult, op1=ALU.mult),
                 [bx[k][tt], bA, bgb], [bu])
            P.op('dve', lambda e, k=k, u=u, Bc=Bc, sl=sl: e.scalar_tensor_tensor(out=u[:], in0=Bc[:], scalar=g_t[:, k:k + 1], in1=u[:], op0=ALU.mult, op1=ALU.add),
                 [bBc, bu, bgb], [bu])
            P.op('act', lambda e, k=k, u=u, sl=sl: e.activation(out=x32[:, k, sl], in_=u[:], func=AF.Identity, bias=b_t[:, k:k + 1], scale=1.0),
                 [bu, bgb], [bx[k][tt]])
            P.op('act', lambda e, k=k, u=u, sl=sl: e.activation(out=xb[:, k, sl], in_=u[:], func=AF.Identity, bias=b_t[:, k:k + 1], scale=1.0),
                 [bu, bgb], [bxb[k][tt]])


def stage_F(C, T, mixT, memoT, xT_in, xT_out, w_out, ln1g, ln1b, w_up, w_down, ln2g, ln2b, bdram_in, bdram_out, dbg=None):
    P = C.P
    nc = C.nc
    TH = min(T, 1024)
    ntt = TH // TT
    x32 = C.sb("F_x32", [128, 8, TH], F32)
    xb = C.sb("F_xb", [128, 8, TH], BF16)
    cat = C.sb("F_cat", [128, 12, TH], BF16)
    h = C.sb("F_h", [128, 32, TH], BF16)
    lnp = C.sb("F_lnp", [128, 4, 8], F32)
    blnp = C.buf("lnp")
    for i, v in enumerate([ln1g, ln1b, ln2g, ln2b]):
        P.dma("sp", lnp[:, i, :], v.rearrange("(k p) -> p k", p=128), [], blnp, allow_slow_non_contiguous=True)
    sq_ring = C.ring("F_sq", 3, [128, TT], F32)
    st_ring = C.ring("F_st", 4, [128, TT], F32)
    relu_ring = C.ring("F_relu", 2, [128, TT], F32)
    bx = [[C.buf("x%d_%d" % (k, t)) for t in range(ntt)] for k in range(8)]
    bxb = [[C.buf("xb%d_%d" % (k, t)) for t in range(ntt)] for k in range(8)]
    bcat = [[C.buf("cat%d_%d" % (k, t)) for t in range(ntt)] for k in range(12)]
    bh = [[C.buf("h%d_%d" % (k, t)) for t in range(ntt)] for k in range(32)]
    for half in range(T // TH):
        t0 = half * TH
        for k in range(12):
            src = mixT[k * 128:(k + 1) * 128, t0:t0 + TH] if k < 8 else memoT[(k - 8) * 128:(k - 7) * 128, t0:t0 + TH]
            for tt in range(ntt):
                P.dma('sp', cat[:, k, tt * TT:(tt + 1) * TT], src[:, tt * TT:(tt + 1) * TT], [bdram_in], bcat[k][tt])
        for k in range(8):
            for tt in range(ntt):
                P.dma('sp', x32[:, k, tt * TT:(tt + 1) * TT], xT_in[k * 128:(k + 1) * 128, t0 + tt * TT:t0 + (tt + 1) * TT], [bdram_in], bx[k][tt])
        for og in range(4):
            wt, bw = C.load_w(w_out, 0, 1536, og * 256, 256)
            for oc in range(2):
                o = og * 2 + oc
                for tt in range(ntt):
                    sl = slice(tt * TT, (tt + 1) * TT)
                    ps, bps = C.next_ps()
                    for k in range(12):
                        MM(P, ps[:], wt[:, k, oc * 128:(oc + 1) * 128], cat[:, k, sl], k == 0, k == 11, [bw, bcat[k][tt]], bps)
                    I(P, 'dve', 'scalar_tensor_tensor', [bps, bx[o][tt]], [bx[o][tt]], out=x32[:, o, sl], in0=x32[:, o, sl], scalar=ALPHA, in1=ps[:], op0=ALU.mult, op1=ALU.add)
        def dump(name):
            if dbg and name in dbg:
                for k in range(8):
                    for tt in range(ntt):
                        P.dma('sp', dbg[name][k * 128:(k + 1) * 128, t0 + tt * TT:t0 + (tt + 1) * TT], x32[:, k, tt * TT:(tt + 1) * TT], [bx[k][tt]], bdram_out)
        dump('z1')
        layer_norm_fm(C, x32, bx, xb, bxb, lnp[:, 0, :], lnp[:, 1, :], blnp, ntt, 0, sq_ring, st_ring)
        dump('x1')
        for hg in range(8):
            wt, bw = C.load_w(w_up, 0, 1024, hg * 512, 512)
            for oc in range(4):
                hc = hg * 4 + oc
                for tt in range(ntt):
                    sl = slice(tt * TT, (tt + 1) * TT)
                    ps, bps = C.next_ps()
                    for k in range(8):
                        P.mm(lambda e, ps=ps, wt=wt, k=k, oc=oc, sl=sl: e.matmul(ps[:], wt[:, k, oc * 128:(oc + 1) * 128], xb[:, k, sl], start=(k == 0), stop=(k == 7)),
                             [bw, bxb[k][tt]], bps)
                    r, br = relu_ring()
                    P.op('act', lambda e, ps=ps, r=r: e.activation(out=r[:], in_=ps[:], func=AF.Relu), [bps], [br])
                    P.op('pool', lambda e, r=r, hc=hc, sl=sl: e.tensor_tensor(out=h[:, hc, sl], in0=r[:], in1=r[:], op=ALU.mult), [br], [bh[hc][tt]])
        for og in range(4):
            wts = [C.load_w(w_down, kh * 2048, 2048, og * 256, 256) for kh in range(2)]
            for oc in range(2):
                o = og * 2 + oc
                for tt in range(ntt):
                    sl = slice(tt * TT, (tt + 1) * TT)
                    ps, bps = C.next_ps()
                    for k in range(32):
                        wt, bw = wts[k // 16]
                        MM(P, ps[:], wt[:, k % 16, oc * 128:(oc + 1) * 128], h[:, k, sl], k == 0, k == 31, [bw, bh[k][tt]], bps)
                    I(P, 'dve', 'scalar_tensor_tensor', [bps, bx[o][tt]], [bx[o][tt]], out=x32[:, o, sl], in0=x32[:, o, sl], scalar=ALPHA, in1=ps[:], op0=ALU.mult, op1=ALU.add)
        layer_norm_fm(C, x32, bx, xb, bxb, lnp[:, 2, :], lnp[:, 3, :], blnp, ntt, 0, sq_ring, st_ring)
        for k in range(8):
            for tt in range(ntt):
                P.dma('sp', xT_out[k * 128:(k + 1) * 128, t0 + tt * TT:t0 + (tt + 1) * TT], x32[:, k, tt * TT:(tt + 1) * TT], [bx[k][tt]], bdram_out)


def I(P, eng, name, reads, writes, *args, **kw):
    return P.op(eng, lambda e, name=name, args=args, kw=kw: getattr(e, name)(*args, **kw), reads, writes)


def MM(P, out, lhsT, rhs, start, stop, reads, obuf):
    return P.mm(lambda e, out=out, lhsT=lhsT, rhs=rhs, start=start, stop=stop: e.matmul(out, lhsT, rhs, start=start, stop=stop), reads, obuf)


MEM_SCALE = 128 ** -0.5


def stage_P(C, T, kind, xT_in, memT, w_in, w_kv, prm, outs, bdram_in, bdram_out):
    P = C.P
    ntt = T // TT
    xb = C.sb("P_xb", [128, 8, T], BF16)
    memb = C.sb("P_memb", [128, 8, 256], BF16)
    kmT = C.sb("P_kmT", [128, 4, 256], BF16)
    vm = C.sb("P_vm", [128, 2, 512], BF16)
    bxb = [C.buf("Pxb%d" % k) for k in range(8)]
    bmemb, bkmT, bvm = C.buf("memb"), C.buf("kmT"), C.buf("vm")
    for k in range(8):
        P.dma('pool', xb[:, k, :], xT_in[k * 128:(k + 1) * 128, :], [bdram_in], bxb[k])
    P.dma('pool', memb[:], memT.rearrange("(k p) m -> p k m", p=128), [], bmemb)
    wt, bw = C.load_w(w_kv, 0, 1024, 0, 512)
    for hd in range(4):
        ps, bps = C.next_ps()
        for k in range(8):
            MM(P, ps[:, 0:256], wt[:, k, hd * 128:(hd + 1) * 128], memb[:, k, :], k == 0, k == 7, [bw, bmemb], bps)
        I(P, 'act', 'copy', [bps], [bkmT], out=kmT[:, hd, :], in_=ps[:, 0:256])
    wt, bw = C.load_w(w_kv, 0, 1024, 512, 512)
    for mt in range(2):
        ps, bps = C.next_ps()
        for k in range(8):
            MM(P, ps[:], memb[:, k, mt * 128:(mt + 1) * 128], wt[:, k, :], k == 0, k == 7, [bw, bmemb], bps)
        I(P, 'act', 'copy', [bps], [bvm], out=vm[:, mt, :], in_=ps[:])

    stg32 = C.ring("P_s32", 3, [128, TT], F32)
    stgb = C.ring("P_sb", 3, [128, TT], BF16)
    scr32 = C.ring("P_c32", 3, [128, TT], F32)

    def proj_fm(c0, ncols, post):
        done = 0
        while done < ncols:
            g = min(512, ncols - done)
            wt, bw = C.load_w(w_in, 0, 1024, c0 + done, g)
            for oc in range((g + 127) // 128):
                nrow = min(128, g - oc * 128)
                for tt in range(ntt):
                    ps, bps = C.next_ps()
                    for k in range(8):
                        MM(P, ps[0:nrow, :], wt[:, k, oc * 128:oc * 128 + nrow], xb[:, k, tt * TT:(tt + 1) * TT], k == 0, k == 7, [bw, bxb[k]], bps)
                    post(ps, bps, (done + oc * 128) // 128, nrow, tt)
            done += g

    def proj_tm(c0, ncols, dst):
        for g0 in range(0, ncols, 512):
            wt, bw = C.load_w(w_in, 0, 1024, c0 + g0, 512)
            for t in range(T // 128):
                ps, bps = C.next_ps()
                for k in range(8):
                    MM(P, ps[:], xb[:, k, t * 128:(t + 1) * 128], wt[:, k, :], k == 0, k == 7, [bw, bxb[k]], bps)
                s, bs = stgb()
                I(P, 'act', 'copy', [bps], [bs], out=s[:], in_=ps[:])
                P.dma('sp', dst[t * 128:(t + 1) * 128, g0:g0 + 512], s[:], [bs], bdram_out)

    def store_fm(dst, dt_ring):
        def post(ps, bps, ci, nrow, tt, func=None):
            s, bs = dt_ring()
            I(P, 'act', 'activation', [bps], [bs], out=s[0:nrow, :], in_=ps[0:nrow, :], func=(func or AF.Copy))
            P.dma('sp', dst[ci * 128:ci * 128 + nrow, tt * TT:(tt + 1) * TT], s[0:nrow, :], [bs], bdram_out)
        return post

    def post_act(dst, func, ring):
        base = store_fm(dst, ring)
        return lambda ps, bps, ci, nrow, tt: base(ps, bps, ci, nrow, tt, func=func)

    gp = C.sb("P_gp", [16, 4], F32)
    bgp = C.buf("gp")

    def gates_post(spec):
        def post(ps, bps, ci, nrow, tt):
            spec(ps, bps, tt)
        return post

    if kind == 'gdn':
        proj_fm(0, 3072, store_fm(outs['qkvT'], stg32))
        proj_fm(3072, 1024, post_act(outs['szT'], AF.Silu, stg32))
        P.dma('sp', gp[0:8, 0:1], prm['a_log'].rearrange("(h o) -> h o", o=1), [], bgp)
        P.dma('sp', gp[0:8, 1:2], prm['dt_bias'].rearrange("(h o) -> h o", o=1), [], bgp)
        I(P, 'act', 'activation', [bgp], [bgp], out=gp[0:8, 2:3], in_=gp[0:8, 0:1], func=AF.Exp)
        I(P, 'dve', 'tensor_scalar', [bgp], [bgp], out=gp[0:8, 2:3], in0=gp[0:8, 2:3], scalar1=-1.0, scalar2=None, op0=ALU.mult)

        def gspec(ps, bps, tt):
            s, bs = scr32()
            I(P, 'act', 'activation', [bps, bgp], [bs], out=s[0:8, :], in_=ps[0:8, :], func=AF.Exp, bias=gp[0:8, 1:2], scale=1.0)
            I(P, 'act', 'activation', [bs], [bs], out=s[0:8, :], in_=s[0:8, :], func=AF.Ln, bias=1.0, scale=1.0)
            I(P, 'dve', 'tensor_scalar', [bs, bgp], [bs], out=s[0:8, :], in0=s[0:8, :], scalar1=gp[0:8, 2:3], scalar2=None, op0=ALU.mult)
            P.dma('sp', outs['gT'][:, tt * TT:(tt + 1) * TT], s[0:8, :], [bs], bdram_out)
            s2, bs2 = scr32()
            I(P, 'act', 'activation', [bps], [bs2], out=s2[0:16, :], in_=ps[0:16, :], func=AF.Sigmoid)
            P.dma('sp', outs['betaT'][:, tt * TT:(tt + 1) * TT], s2[8:16, :], [bs2], bdram_out)
        proj_fm(4096, 16, gates_post(gspec))
        memq0 = 4112
    elif kind == 'mlstm':
        proj_fm(0, 512, store_fm(outs['qT'], stg32))
        proj_fm(512, 512, store_fm(outs['kT'], stg32))
        proj_tm(1024, 1024, outs['v'])
        proj_fm(2048, 1024, post_act(outs['sogT'], AF.Sigmoid, stg32))
        P.dma('sp', gp[0:8, 0:1], prm['b_gate'][0, :].rearrange("(h o) -> h o", o=1), [], bgp)
        P.dma('sp', gp[8:16, 0:1], prm['b_gate'][1, :].rearrange("(h o) -> h o", o=1), [], bgp)
        I(P, 'dve', 'tensor_scalar', [bgp], [bgp], out=gp[0:16, 1:2], in0=gp[0:16, 0:1], scalar1=-1.0, scalar2=None, op0=ALU.mult)

        def gspec(ps, bps, tt):
            s, bs = scr32()
            I(P, 'act', 'activation', [bps, bgp], [bs], out=s[0:16, :], in_=ps[0:16, :], func=AF.Identity, bias=gp[0:16, 0:1], scale=1.0)
            P.dma('sp', outs['ilogT'][:, tt * TT:(tt + 1) * TT], s[0:8, :], [bs], bdram_out)
            s2, bs2 = scr32()
            I(P, 'act', 'activation', [bps, bgp], [bs2], out=s2[0:16, :], in_=ps[0:16, :], func=AF.Exp, bias=gp[0:16, 1:2], scale=-1.0)
            I(P, 'act', 'activation', [bs2], [bs2], out=s2[0:16, :], in_=s2[0:16, :], func=AF.Ln, bias=1.0, scale=1.0)
            I(P, 'dve', 'tensor_scalar', [bs2], [bs2], out=s2[0:16, :], in0=s2[0:16, :], scalar1=-1.0, scalar2=None, op0=ALU.mult)
            P.dma('sp', outs['flogT'][:, tt * TT:(tt + 1) * TT], s2[8:16, :], [bs2], bdram_out)
        proj_fm(3072, 16, gates_post(gspec))
        memq0 = 3088
    else:
        blk = C.sb("P_blk", [128, 128], F32)
        qkg = C.sb("P_qkg", [128, 2], F32)
        bblk = C.buf("blk")
        I(P, 'pool', 'memset', [], [bblk], blk[:], 0.0)
        I(P, 'pool', 'memset', [bblk], [bblk], blk[0:64, 0:64], 1.0 / 64)
        I(P, 'pool', 'memset', [bblk], [bblk], blk[64:128, 64:128], 1.0 / 64)
        for j in range(2):
            for hh in range(2):
                P.dma('sp', qkg[hh * 64:(hh + 1) * 64, j:j + 1], prm['qk_g'][j, :].rearrange("(d o) -> d o", o=1), [], bblk)
        I(P, 'dve', 'tensor_scalar', [bblk], [bblk], out=qkg[:, 0:1], in0=qkg[:, 0:1], scalar1=64 ** -0.5, scalar2=None, op0=ALU.mult)

        def rms_post(dst, j):
            def post(ps, bps, ci, nrow, tt):
                raw, braw = scr32()
                sq, bsq = scr32()
                I(P, 'act', 'copy', [bps], [braw], out=raw[:], in_=ps[:])
                I(P, 'act', 'activation', [bps], [bsq], out=sq[:], in_=ps[:], func=AF.Square)
                ps2, bps2 = C.next_ps()
                MM(P, ps2[:], blk[:], sq[:], True, True, [bblk, bsq], bps2)
                I(P, 'act', 'activation', [bps2, C.bconst], [bsq], out=sq[:], in_=ps2[:], func=AF.Ln, bias=C.eps_n[:, 0:1], scale=1.0)
                I(P, 'act', 'activation', [bsq], [bsq], out=sq[:], in_=sq[:], func=AF.Exp, scale=-0.5)
                s, bs = stgb()
                I(P, 'dve', 'scalar_tensor_tensor', [braw, bsq, bblk], [bs], out=s[:], in0=raw[:], scalar=qkg[:, j:j + 1], in1=sq[:], op0=ALU.mult, op1=ALU.mult)
                P.dma('sp', dst[ci * 128:(ci + 1) * 128, tt * TT:(tt + 1) * TT], s[:], [bs], bdram_out)
            return post
        proj_fm(0, 1024, rms_post(outs['qT'], 0))
        proj_fm(1024, 1024, rms_post(outs['kT'], 1))
        proj_tm(2048, 1024, outs['v'])
        proj_fm(3072, 1024, post_act(outs['sogT'], AF.Sigmoid, stg32))
        P.dma('sp', gp[0:16, 0:1], prm['b_f'].rearrange("(h o) -> h o", o=1), [], bgp)
        I(P, 'dve', 'tensor_scalar', [bgp], [bgp], out=gp[0:16, 1:2], in0=gp[0:16, 0:1], scalar1=-1.0, scalar2=None, op0=ALU.mult)

        def gspec(ps, bps, tt):
            s2, bs2 = scr32()
            I(P, 'act', 'activation', [bps, bgp], [bs2], out=s2[0:16, :], in_=ps[0:16, :], func=AF.Exp, bias=gp[0:16, 1:2], scale=-1.0)
            I(P, 'act', 'activation', [bs2], [bs2], out=s2[0:16, :], in_=s2[0:16, :], func=AF.Ln, bias=1.0, scale=1.0)
            I(P, 'dve', 'tensor_scalar', [bs2], [bs2], out=s2[0:16, :], in0=s2[0:16, :], scalar1=-1.0, scalar2=None, op0=ALU.mult)
            P.dma('sp', outs['flogT'][:, tt * TT:(tt + 1) * TT], s2[0:16, :], [bs2], bdram_out)
        proj_fm(4096, 16, gates_post(gspec))
        memq0 = 4112

    pT = C.ring("P_pT", 2, [128, 2, TT], BF16)

    for hd in range(4):
        wt, bw = C.load_w(w_in, 0, 1024, memq0 + hd * 128, 128)
        for tt in range(ntt):
            ps, bps = C.next_ps()
            for k in range(8):
                MM(P, ps[:], wt[:, k, :], xb[:, k, tt * TT:(tt + 1) * TT], k == 0, k == 7, [bw, bxb[k]], bps)
            q, bq = stgb()
            I(P, 'act', 'copy', [bps], [bq], out=q[:], in_=ps[:])
            p, bp = pT()
            for mt in range(2):
                ps2, bps2 = C.next_ps()
                MM(P, ps2[:], kmT[:, hd, mt * 128:(mt + 1) * 128], q[:], True, True, [bkmT, bq], bps2)
                I(P, 'act', 'activation', [bps2], [bp], out=p[:, mt, :], in_=ps2[:], func=AF.Exp, scale=MEM_SCALE)
            po, bpo = C.next_ps()
            pd, bpd = C.next_ps()
            for mt in range(2):
                MM(P, po[:], vm[:, mt, hd * 128:(hd + 1) * 128], p[:, mt, :], mt == 0, mt == 1, [bvm, bp], bpo)
            for mt in range(2):
                MM(P, pd[:], C.onesb[:], p[:, mt, :], mt == 0, mt == 1, [C.bconst, bp], bpd)
            r, br = scr32()
            I(P, 'dve', 'reciprocal', [bpd], [br], out=r[:], in_=pd[:])
            s, bs = stgb()
            I(P, 'dve', 'tensor_tensor', [bpo, br], [bs], out=s[:], in0=po[:], in1=r[:], op=ALU.mult)
            P.dma('sp', outs['memoT'][hd * 128:(hd + 1) * 128, tt * TT:(tt + 1) * TT], s[:], [bs], bdram_out)


def consts_attn(C):
    P = C.P
    if 'tri01' in C.pcache:
        return
    tri = C.sbp("tri01", [128, 128], BF16)
    onesrow = C.sbp("onesrow", [1, 1024], F32)
    negone = C.sbp("negone", [1, 1], F32)
    b = C.bconst
    I(P, 'pool', 'memset', [], [b], tri[:], 1.0)
    I(P, 'pool', 'affine_select', [b], [b], out=tri[:], in_=tri[:], pattern=[[1, 128]], compare_op=ALU.is_ge, fill=0.0, base=0, channel_multiplier=-1)
    I(P, 'pool', 'memset', [], [b], onesrow[:], 1.0)
    I(P, 'pool', 'memset', [], [b], negone[:], -1.0)
    C.tri01, C.onesrow, C.negone = tri, onesrow, negone


def cumsum_row(C, dst_row, src_row, n, rb, wb):
    I(C.P, 'dve', 'tensor_tensor_scan', rb + [C.bconst], wb, out=dst_row, data0=C.onesrow[0:1, 0:n], data1=src_row, initial=0.0, op0=ALU.mult, op1=ALU.add)


def cumsum_dram_row(C, src, S, crow, bcrow, stg_r, bdram_in):
    P = C.P
    PC = min(1024, S)
    for pc in range(S // PC):
        sg, bsg = stg_r()
        P.dma('sp', sg[0:1, 0:PC], src[:, pc * PC:(pc + 1) * PC], [bdram_in], bsg)
        init = 0.0 if pc == 0 else crow[0:1, pc * PC - 1:pc * PC]
        I(P, 'dve', 'tensor_tensor_scan', [bsg, bcrow, C.bconst], [bcrow], out=crow[0:1, pc * PC:(pc + 1) * PC], data0=C.onesrow[0:1, 0:PC], data1=sg[0:1, 0:PC],
          initial=init, op0=ALU.mult, op1=ALU.add)


def attn_loop(C, S, KA, bKA, QA, bQA, kdim, VA, bVA, vcols, negc, E, bside, nacc, den_ones, epilogue, LA=2, ED=3):
    P = C.P
    nq = S // 512
    pT = C.ring("A_pT", 4, [128, 512], BF16)
    C.set_pool('score', [0, 1, 2])
    blocks = []
    for qi in range(nq):
        nk = 4 * qi + 4
        for kt in range(nk):
            blocks.append((qi, kt, nk))
    sc = {}

    def rec_score(n):
        qi, kt, nk = blocks[n]
        j = kt - 4 * qi
        c0 = 128 * j if j > 0 else 0
        ps, bps = C.next_ps('score')
        MM(P, ps[:, c0:512], KA[0:kdim, kt * 128:(kt + 1) * 128], QA[0:kdim, qi * 512 + c0:(qi + 1) * 512], True, True, [bKA, bQA], bps)
        sc[n] = (ps, bps, c0, j)

    pend = []
    for n in range(min(LA, len(blocks))):
        rec_score(n)
    for n in range(len(blocks)):
        if n + LA < len(blocks):
            rec_score(n + LA)
        qi, kt, nk = blocks[n]
        ps, bps, c0, j = sc.pop(n)
        accs = [C.acc((qi % 2) * nacc + i) for i in range(nacc)]
        p, bp = pT()
        if negc is not None:
            I(P, 'act', 'activation', [bps, bside], [bp], out=p[:, c0:512], in_=ps[:, c0:512], func=AF.Exp, bias=negc[:, kt:kt + 1], scale=1.0)
        else:
            I(P, 'act', 'activation', [bps, bside], [bp], out=p[:, c0:512], in_=ps[:, c0:512], func=AF.Copy, scale=E[:, qi * (S // 128) + kt:qi * (S // 128) + kt + 1])
        if j >= 0:
            I(P, 'pool', 'tensor_tensor', [bp, C.bconst], [bp], out=p[:, c0:c0 + 128], in0=p[:, c0:c0 + 128], in1=C.tri01[:], op=ALU.mult)
        MM(P, accs[0][0][0:vcols, c0:512], VA[:, kt, :], p[:, c0:512], kt == 0, kt == nk - 1, [bVA, bp], accs[0][1])
        if den_ones:
            MM(P, accs[1][0][:, c0:512], C.onesb[:], p[:, c0:512], kt == 0, kt == nk - 1, [C.bconst, bp], accs[1][1])
        for e in pend:
            e[0] -= 1
        while pend and pend[0][0] <= 0:
            g = pend.pop(0)[1]
            for _ in g:
                pass
        if kt == nk - 1:
            g = epilogue(qi, accs)
            next(g)
            pend.append([ED, g])
    for e in pend:
        for _ in e[1]:
            pass


def mixer_fox(C, S, nh, qT, kT, v, sogT, flogT, mixT, bdram_in, bdram_out):
    P = C.P
    consts_attn(C)
    C.nring = 4
    C.set_pool('misc', [3, 6, 7])
    KA = [C.sb("X_KA%d" % i, [65, S], BF16) for i in range(2)]
    QA = [C.sb("X_QA%d" % i, [65, S], BF16) for i in range(2)]
    VA = [C.sb("X_VA%d" % i, [128, S // 128, 65], BF16) for i in range(2)]
    crow = C.sb("X_crow", [1, S], F32)
    cb = C.sb("X_cb", [1, S], BF16)
    negc = [C.sb("X_negc%d" % i, [128, S // 128], F32) for i in range(2)]
    bKA = [C.buf("X_KA%d" % i) for i in range(2)]
    bQA = [C.buf("X_QA%d" % i) for i in range(2)]
    bVA = [C.buf("X_VA%d" % i) for i in range(2)]
    bcrow = C.buf("X_crow")
    bneg = [C.buf("X_negc%d" % i) for i in range(2)]
    stg_r = C.ring("X_stg", 2, [1, 1024], F32)
    sog_r = C.ring("X_sog", 3, [64, 512], F32)
    o_r = C.ring("X_o", 3, [65, 512], F32)
    ob_r = C.ring("X_ob", 2, [64, 512], BF16)
    for h in range(nh):
        i = h % 2
        P.dma('sp', KA[i][0:64, :], kT[h * 64:(h + 1) * 64, :], [bdram_in], bKA[i])
        I(P, 'pool', 'memset', [bKA[i]], [bKA[i]], KA[i][64:65, :], 1.0)
        P.dma('sp', QA[i][0:64, :], qT[h * 64:(h + 1) * 64, :], [bdram_in], bQA[i])
        P.dma('sp', VA[i][:, :, 0:64], v[:, h * 64:(h + 1) * 64].rearrange("(t p) d -> p t d", p=128), [bdram_in], bVA[i])
        I(P, 'pool', 'memset', [bVA[i]], [bVA[i]], VA[i][:, :, 64:65], 1.0)
        cumsum_dram_row(C, flogT[h:h + 1, :], S, crow, bcrow, stg_r, bdram_in)
        I(P, 'dve', 'tensor_copy', [bcrow], [bcrow], out=cb[:], in_=crow[:])
        P.dma('sp', QA[i][64:65, :], cb[:], [bcrow], bQA[i])
        ps, bps = C.next_ps()
        for t in range(S // 128):
            MM(P, ps[:, t:t + 1], crow[0:1, t * 128:(t + 1) * 128], C.negone[0:1, 0:1], True, True, [bcrow, C.bconst], bps)
        I(P, 'dve', 'tensor_copy', [bps], [bneg[i]], out=negc[i][:], in_=ps[:, 0:S // 128])

        def epi(qi, accs, h=h, i=i):
            po, bpo = accs[0]
            o, bo = o_r()
            I(P, 'dve', 'reciprocal', [bpo], [bo], out=o[64:65, :], in_=po[64:65, :])
            I(P, 'act', 'copy', [bpo], [bo], out=o[0:64, :], in_=po[0:64, :])
            sg, bsg = sog_r()
            P.dma('sp', sg[:], sogT[h * 64:(h + 1) * 64, qi * 512:(qi + 1) * 512], [bdram_in], bsg)
            yield
            pb, bpb = C.next_ps('misc')
            MM(P, pb[0:64, :], C.ones32[64:65, 0:64], o[64:65, :], True, True, [C.bconst, bo], bpb)
            I(P, 'dve', 'tensor_tensor', [bo, bpb], [bo], out=o[0:64, :], in0=o[0:64, :], in1=pb[0:64, :], op=ALU.mult)
            ob, bob = ob_r()
            I(P, 'dve', 'tensor_tensor', [bo, bsg], [bob], out=ob[:], in0=o[0:64, :], in1=sg[:], op=ALU.mult)
            P.dma('sp', mixT[h * 64:(h + 1) * 64, qi * 512:(qi + 1) * 512], ob[:], [bob], bdram_out)
            yield
        attn_loop(C, S, KA[i], bKA[i], QA[i], bQA[i], 65, VA[i], bVA[i], 65, negc[i], None, bneg[i], 1, False, epi)
    C.nring = 8


def mixer_mlstm(C, S, nh, qT, kT, v, sogT, ilogT, flogT, ng, mixT, bdram_in, bdram_out):
    P = C.P
    consts_attn(C)
    C.nring = 4
    C.set_pool('misc', [3])
    nq, nkb = S // 512, S // 128
    KA = [C.sb("X_KA%d" % i, [65, S], BF16) for i in range(2)]
    QA = [C.sb("X_QA%d" % i, [65, S], BF16) for i in range(2)]
    VA = [C.sb("M_VA%d" % i, [128, nkb, 128], BF16) for i in range(2)]
    Brow = C.sb("M_Brow", [1, S], F32)
    xrow = C.sb("M_xrow", [1, 1024], F32)
    bBrow = C.buf("M_Brow")
    stg_r = C.ring("M_stg", 2, [1, 1024], F32)
    f_r = C.ring("M_f", 4, [1, 512], F32)
    refs = [C.sb("M_refs%d" % i, [1, 128], F32) for i in range(2)]
    E = [C.sb("M_E%d" % i, [128, 16 * 64], F32) for i in range(2)]
    gcol = C.sb("M_g", [128, 8], F32)
    avg = C.sb("M_avg", [128, 128], F32)
    bKA = [C.buf("X_KA%d" % i) for i in range(2)]
    bQA = [C.buf("X_QA%d" % i) for i in range(2)]
    bVA = [C.buf("M_VA%d" % i) for i in range(2)]
    bE = [C.buf("M_E%d" % i) for i in range(2)]
    bg = C.buf("M_g")
    I(P, 'pool', 'memset', [], [bg], avg[:], 1.0 / 128)
    P.dma('sp', gcol[:, 0:nh], ng.rearrange("(h d) -> d h", d=128), [], bg, allow_slow_non_contiguous=True)
    ld_r = C.ring("M_ld", 2, [64, 512], F32)
    sog_r = C.ring("M_sog", 3, [128, 512], F32)
    w_r = C.ring("M_w", 5, [128, 512], F32)
    ob_r = C.ring("M_ob", 2, [128, 512], BF16)
    for h in range(nh):
        i = h % 2
        cumsum_dram_row(C, flogT[h:h + 1, :], S, Brow, bBrow, stg_r, bdram_in)
        B3q = Brow[0:1, :].rearrange("p (t c) -> p t c", c=512)
        B3k = Brow[0:1, :].rearrange("p (t c) -> p t c", c=128)
        brf = bE[i]
        I(P, 'dve', 'tensor_copy', [bBrow], [brf], out=refs[i][0:1, 0:nq], in_=B3q[:, :, 0])
        I(P, 'dve', 'tensor_copy', [bBrow], [brf], out=refs[i][0:1, 64:64 + nkb], in_=B3k[:, :, 0])
        X3 = xrow[0:1, 0:nq * nkb].rearrange("p (a b) -> p a b", b=nkb)
        I(P, 'dve', 'tensor_tensor', [brf], [bBrow], out=X3, in0=refs[i][0:1, 0:nq].unsqueeze(2).to_broadcast([1, nq, nkb]),
          in1=refs[i][0:1, 64:64 + nkb].unsqueeze(1).to_broadcast([1, nq, nkb]), op=ALU.subtract)
        I(P, 'dve', 'tensor_scalar', [bBrow], [bBrow], out=xrow[0:1, 0:nq * nkb], in0=xrow[0:1, 0:nq * nkb], scalar1=60.0, scalar2=None, op0=ALU.min)
        for c in range((nq * nkb + 511) // 512):
            n = min(512, nq * nkb - c * 512)
            ps, bps = C.next_ps()
            MM(P, ps[:, 0:n], C.ones32[0:1, :], xrow[0:1, c * 512:c * 512 + n], True, True, [C.bconst, bBrow], bps)
            I(P, 'act', 'activation', [bps], [bE[i]], out=E[i][:, c * 512:c * 512 + n], in_=ps[:, 0:n], func=AF.Exp)
        for c in range(S // 512):
            fq, bfq = f_r()
            I(P, 'dve', 'tensor_scalar', [bBrow, brf], [bfq], out=fq[:], in0=Brow[0:1, c * 512:(c + 1) * 512], scalar1=refs[i][0:1, c:c + 1], scalar2=None, op0=ALU.subtract)
            I(P, 'act', 'activation', [bfq], [bfq], out=fq[:], in_=fq[:], func=AF.Exp)
            fk, bfk = f_r()
            P.dma('sp', fk[:], ilogT[h:h + 1, c * 512:(c + 1) * 512], [bdram_in], bfk)
            I(P, 'dve', 'tensor_tensor', [bfk, bBrow], [bfk], out=fk[:], in0=fk[:], in1=Brow[0:1, c * 512:(c + 1) * 512], op=ALU.subtract)
            I(P, 'dve', 'tensor_tensor', [bfk, brf], [bfk], out=fk[:].rearrange("p (t c) -> p t c", c=128), in0=fk[:].rearrange("p (t c) -> p t c", c=128),
              in1=refs[i][0:1, 64 + c * 4:64 + c * 4 + 4].unsqueeze(2).to_broadcast([1, 4, 128]), op=ALU.add)
            I(P, 'act', 'activation', [bfk], [bfk], out=fk[:], in_=fk[:], func=AF.Exp)
            for (src, dstA, bdst, frow, bfrow, sc) in ((qT, QA[i], bQA[i], fq, bfq, 0.125), (kT, KA[i], bKA[i], fk, bfk, 1.0)):
                ld, bld = ld_r()
                P.dma('sp', ld[:], src[h * 64:(h + 1) * 64, c * 512:(c + 1) * 512], [bdram_in], bld)
                ps, bps = C.next_ps()
                MM(P, ps[0:64, :], C.ones32[0:1, 0:64], frow[:], True, True, [C.bconst, bfrow], bps)
                I(P, 'dve', 'scalar_tensor_tensor', [bld, bps], [bdst], out=dstA[0:64, c * 512:(c + 1) * 512], in0=ld[:], scalar=sc, in1=ps[0:64, :], op0=ALU.mult, op1=ALU.mult)
        P.dma('sp', VA[i][:], v[:, h * 128:(h + 1) * 128].rearrange("(t p) d -> p t d", p=128), [bdram_in], bVA[i])

        def epi(qi, accs, h=h, i=i):
            (pn, bpn), (pd, bpd) = accs
            r, br = w_r()
            I(P, 'act', 'activation', [bpd], [br], out=r[:], in_=pd[:], func=AF.Abs)
            I(P, 'dve', 'tensor_scalar', [br], [br], out=r[:], in0=r[:], scalar1=1.0, scalar2=None, op0=ALU.max)
            I(P, 'dve', 'reciprocal', [br], [br], out=r[:], in_=r[:])
            hh, bh = w_r()
            I(P, 'dve', 'tensor_tensor', [bpn, br], [bh], out=hh[:], in0=pn[:], in1=r[:], op=ALU.mult)
            sq, bsq = w_r()
            I(P, 'act', 'activation', [bh], [bsq], out=sq[:], in_=hh[:], func=AF.Square)
            sg, bsg = sog_r()
            P.dma('sp', sg[:], sogT[h * 128:(h + 1) * 128, qi * 512:(qi + 1) * 512], [bdram_in], bsg)
            yield
            pm, bpm = C.next_ps('misc')
            MM(P, pm[:, 0:256], avg[:], hh[:, 0:256], True, True, [bg, bh], bpm)
            MM(P, pm[:, 256:512], avg[:], hh[:, 256:512], True, True, [bg, bh], bpm)
            mean, bmean = w_r()
            I(P, 'act', 'copy', [bpm], [bmean], out=mean[:], in_=pm[:])
            pv, bpv = C.next_ps('misc')
            MM(P, pv[:], avg[:], sq[:], True, True, [bg, bsq], bpv)
            I(P, 'act', 'activation', [bmean], [br], out=r[:], in_=mean[:], func=AF.Square)
            I(P, 'dve', 'tensor_tensor', [bpv, br], [br], out=r[:], in0=pv[:], in1=r[:], op=ALU.subtract)
            I(P, 'act', 'activation', [br, C.bconst], [br], out=r[:], in_=r[:], func=AF.Ln, bias=C.eps_n[:, 0:1], scale=1.0)
            I(P, 'act', 'activation', [br], [br], out=r[:], in_=r[:], func=AF.Exp, scale=-0.5)
            I(P, 'dve', 'tensor_tensor', [bh, bmean], [bh], out=hh[:], in0=hh[:], in1=mean[:], op=ALU.subtract)
            I(P, 'dve', 'scalar_tensor_tensor', [bh, br, bg], [bh], out=hh[:], in0=hh[:], scalar=gcol[:, h:h + 1], in1=r[:], op0=ALU.mult, op1=ALU.mult)
            ob, bob = ob_r()
            I(P, 'dve', 'tensor_tensor', [bh, bsg], [bob], out=ob[:], in0=hh[:], in1=sg[:], op=ALU.mult)
            P.dma('sp', mixT[h * 128:(h + 1) * 128, qi * 512:(qi + 1) * 512], ob[:], [bob], bdram_out)
            yield
        attn_loop(C, S, KA[i], bKA[i], QA[i], bQA[i], 64, VA[i], bVA[i], 128, None, E[i], bE[i], 2, True, epi)
    C.nring = 8


GC = 64
GB = 8


def consts_gdn(C):
    P = C.P
    if 'g_tri' in C.pcache:
        return
    b = C.bconst
    tri = C.sbp("g_tri", [64, 64], F32)
    idn = C.sbp("g_idn", [64, 64], F32)
    off = C.sbp("g_off", [64, 64], F32)
    mS = C.sbp("g_mS", [64, GB, 64], F32)
    mI = C.sbp("g_mI", [64, GB, 64], F32)
    idb = C.sbp("g_idb", [128, 128], BF16)
    avg = C.sbp("g_avg", [128, 128], F32)
    for t, op in ((tri, ALU.is_ge), (idn, ALU.is_equal), (off, ALU.not_equal)):
        I(P, 'pool', 'memset', [], [b], t[:], 1.0)
        I(P, 'pool', 'affine_select', [b], [b], out=t[:], in_=t[:], pattern=[[1, 64]], compare_op=op, fill=0.0, base=0, channel_multiplier=-1)
    I(P, 'pool', 'memset', [], [b], mS[:], 0.0)
    I(P, 'pool', 'affine_select', [b], [b], out=mS[:], in_=mS[:], pattern=[[0, GB], [-1, 64]], compare_op=ALU.is_gt, fill=-1.0e4, base=0, channel_multiplier=1)
    I(P, 'pool', 'memset', [], [b], mI[:], 0.0)
    I(P, 'pool', 'affine_select', [b], [b], out=mI[:], in_=mI[:], pattern=[[0, GB], [1, 64]], compare_op=ALU.is_ge, fill=-1.0e4, base=0, channel_multiplier=-1)
    I(P, 'pool', 'memset', [], [b], idb[:], 1.0)
    I(P, 'pool', 'affine_select', [b], [b], out=idb[:], in_=idb[:], pattern=[[1, 128]], compare_op=ALU.is_equal, fill=0.0, base=0, channel_multiplier=-1)
    I(P, 'pool', 'memset', [], [b], avg[:], 1.0 / 128)
    C.g_tri, C.g_idn, C.g_off, C.g_mS, C.g_mI, C.g_idb, C.g_avg = tri, idn, off, mS, mI, idb, avg


def mixer_gdn(C, S, heads, qkvT, conv_w, szT, gT, betaT, norm_g, mixT, bdram_in, bdram_out):
    P = C.P
    consts_gdn(C)
    C.nring = 4
    NCH = S // GC
    NB = S // (GC * GB)
    nh = len(heads)
    CW = min(1024, S)
    ngl = C.sb("G_ng", [128, 1], F32)
    bng = C.buf("G_ng")
    P.dma('sp', ngl[:], norm_g.rearrange("(d o) -> d o", o=1), [], bng)
    cst = C.ring("G_cst", 2, [128, CW + 3], F32)
    cac = C.ring("G_cac", 2, [128, CW], F32)
    csq = C.ring("G_csq", 2, [128, 512], F32)
    grow = C.ring("G_grow", 2, [1, 512], F32)
    HT = []
    for hi in range(2):
        d = {}
        for nm in ('qb', 'kb', 'vb'):
            d[nm] = C.sb("G_%s%d" % (nm, hi), [128, S], BF16)
            d['b' + nm] = C.buf("G_%s%d" % (nm, hi))
        d['cw'] = C.sb("G_cw%d" % hi, [128, 3, 4], F32)
        d['tm'] = C.sb("G_tm%d" % hi, [64, 8, NCH], F32)
        d['dl'] = C.sb("G_dl%d" % hi, [128, NCH], F32)
        d['S32'] = C.sb("G_S32_%d" % hi, [128, 128], F32)
        d['Sb'] = C.sb("G_Sb_%d" % hi, [128, 128], BF16)
        for nm in ('cw', 'tm', 'dl', 'S32', 'Sb'):
            d['b' + nm] = C.buf("G_%s%d" % (nm, hi))
        HT.append(d)

    def phaseA(hidx):
        hi = hidx % 2
        (q0, k0, v0, gr, o0) = heads[hidx]
        d = HT[hi]
        for xi, r0 in enumerate((q0, k0, v0)):
            P.dma('sp', d['cw'][:, xi, :], conv_w[:, r0:r0 + 128].rearrange("j c -> c j"), [], d['bcw'], allow_slow_non_contiguous=True)
        for xi, (r0, nm) in enumerate(((q0, 'qb'), (k0, 'kb'), (v0, 'vb'))):
            dst, bdst = d[nm], d['b' + nm]
            for cc in range(S // CW):
                st, bst = cst()
                if cc == 0:
                    I(P, 'pool', 'memset', [bst], [bst], st[:, 0:3], 0.0)
                    P.dma('sp', st[:, 3:3 + CW], qkvT[r0:r0 + 128, 0:CW], [bdram_in], bst)
                else:
                    P.dma('sp', st[:, 0:3 + CW], qkvT[r0:r0 + 128, cc * CW - 3:(cc + 1) * CW], [bdram_in], bst)
                ac, bac = cac()
                I(P, 'dve', 'tensor_scalar', [bst, d['bcw']], [bac], out=ac[:], in0=st[:, 3:3 + CW], scalar1=d['cw'][:, xi, 3:4], scalar2=None, op0=ALU.mult)
                for j in range(3):
                    I(P, 'dve', 'scalar_tensor_tensor', [bst, bac, d['bcw']], [bac], out=ac[:], in0=st[:, j:j + CW], scalar=d['cw'][:, xi, j:j + 1], in1=ac[:], op0=ALU.mult, op1=ALU.add)
                I(P, 'act', 'activation', [bac], [bac], out=ac[:], in_=ac[:], func=AF.Silu)
                if nm == 'vb':
                    I(P, 'act', 'copy', [bac], [bdst], out=dst[:, cc * CW:(cc + 1) * CW], in_=ac[:])
                else:
                    for t in range(CW // 512):
                        sq, bsq = csq()
                        I(P, 'act', 'activation', [bac], [bsq], out=sq[:], in_=ac[:, t * 512:(t + 1) * 512], func=AF.Square)
                        ps, bps = C.next_ps()
                        MM(P, ps[:], C.ones32[:], sq[:], True, True, [C.bconst, bsq], bps)
                        I(P, 'act', 'activation', [bps, C.bconst], [bsq], out=sq[:], in_=ps[:], func=AF.Ln, bias=C.eps_n[:, 0:1], scale=1.0)
                        I(P, 'act', 'activation', [bsq], [bsq], out=sq[:], in_=sq[:], func=AF.Exp, scale=-0.5)
                        sc = (128 ** -0.5) if nm == 'qb' else 1.0
                        I(P, 'dve', 'scalar_tensor_tensor', [bac, bsq], [bdst], out=dst[:, cc * CW + t * 512:cc * CW + (t + 1) * 512], in0=ac[:, t * 512:(t + 1) * 512], scalar=sc, in1=sq[:], op0=ALU.mult, op1=ALU.mult)
                yield
        tm = d['tm']
        PC = 512
        for w, src in enumerate((gT, betaT)):
            ps, bps = C.next_ps()
            for pc in range(S // PC):
                sg, bsg = grow()
                P.dma('sp', sg[0:1, 0:PC], src[gr:gr + 1, pc * PC:(pc + 1) * PC], [bdram_in], bsg)
                for nn in range(PC // 64):
                    col = pc * (PC // 64) + nn
                    MM(P, ps[0:64, col:col + 1], sg[0:1, nn * 64:(nn + 1) * 64], C.ones32[0:1, 0:1], True, True, [bsg, C.bconst], bps)
            I(P, 'dve', 'tensor_copy', [bps], [d['btm']], out=tm[:, w, :], in_=ps[0:64, 0:NCH])
        ps, bps = C.next_ps()
        MM(P, ps[0:64, 0:NCH], C.g_tri[:], tm[:, 0, :], True, True, [C.bconst, d['btm']], bps)
        I(P, 'dve', 'tensor_copy', [bps], [d['btm']], out=tm[:, 2, :], in_=ps[0:64, 0:NCH])
        ps2, bps2 = C.next_ps()
        MM(P, ps2[:, 0:NCH], C.ones32[0:64, :], tm[:, 0, :], True, True, [C.bconst, d['btm']], bps2)
        I(P, 'act', 'activation', [bps2], [d['bdl']], out=d['dl'][:], in_=ps2[:, 0:NCH], func=AF.Exp)
        I(P, 'act', 'activation', [d['btm']], [d['btm']], out=tm[:, 3, :], in_=tm[:, 2, :], func=AF.Exp)
        I(P, 'dve', 'tensor_scalar', [d['btm']], [d['btm']], out=tm[:, 5, :], in0=tm[:, 1, :], scalar1=-1.0, scalar2=None, op0=ALU.mult)
        I(P, 'dve', 'tensor_tensor', [d['btm']], [d['btm']], out=tm[:, 7, :], in0=tm[:, 3, :], in1=tm[:, 5, :], op=ALU.mult)
        I(P, 'dve', 'tensor_tensor', [d['btm'], bps2], [d['btm']], out=tm[:, 4, :], in0=ps2[0:64, 0:NCH], in1=tm[:, 2, :], op=ALU.subtract)
        I(P, 'act', 'activation', [d['btm']], [d['btm']], out=tm[:, 4, :], in_=tm[:, 4, :], func=AF.Exp)
        I(P, 'pool', 'memset', [], [d['bS32']], d['S32'][:], 0.0)
        yield
        if hidx == 0:
            C.dbg('d_qb', d['qb'][:, 0:512], [d['bqb']]); C.dbg('d_kb', d['kb'][:, 0:512], [d['bkb']]); C.dbg('d_vb', d['vb'][:, 0:512], [d['bvb']])
            C.dbg('d_tm', d['tm'][:, :, 0:8], [d['btm']]); C.dbg('d_dl', d['dl'][:, 0:8], [d['bdl']])

    f32r = C.ring("G_f32", 8, [64, 512], F32)
    Xr = C.ring("G_X", 2, [64, 512], F32)
    gtr = C.ring("G_gtr", 2, [64, 512], F32)
    Rr = C.ring("G_R", 2, [64, GB, 256], BF16)
    kgr = C.ring("G_kg", 2, [64, GB, 128], BF16)
    UWr = C.ring("G_UW", 2, [64, GB, 256], BF16)
    atr = C.ring("G_at", 2, [64, 512], BF16)
    NTr = C.ring("G_NT", 2, [128, GB, 128], BF16)
    Qpr = C.ring("G_Qp", 2, [128, 512], BF16)
    q32r = C.ring("G_q32", 1, [128, 512], F32)
    o32r = C.ring("G_o32", 2, [128, 512], F32)
    obr = C.ring("G_ob", 1, [128, 512], BF16)
    szr = C.ring("G_sz", 1, [128, 512], F32)
    Xbr = C.ring("G_Xb", 2, [64, 512], BF16)

    def bc(t, w, n0):
        return t[:, w, n0:n0 + GB].unsqueeze(2).to_broadcast([64, GB, 64])

    def v3(t):
        return t[:, :].rearrange("p (n x) -> p n x", x=64)

    def pre(hidx, b):
        hi = hidx % 2
        d = HT[hi]
        tm, btm = d['tm'], d['btm']
        n0 = b * GB
        tok = slice(b * 512, (b + 1) * 512)
        gtri, bgtri = gtr()
        I(P, 'dve', 'tensor_tensor', [btm, C.bconst], [bgtri], out=v3(gtri), in0=C.g_tri[:, :].unsqueeze(1).to_broadcast([64, GB, 64]), in1=bc(tm, 0, n0), op=ALU.mult)
        p1, bp1 = C.next_ps()
        MM(P, p1[0:64, :], C.ones32[0:64, 0:64], gtri[:], True, False, [C.bconst, bgtri], bp1)
        MM(P, p1[0:64, :], C.g_idn[:], C.g_mI[:].rearrange("p n x -> p (n x)"), False, True, [C.bconst], bp1)
        E2, bE2 = f32r()
        I(P, 'dve', 'tensor_tensor', [bp1, btm], [bE2], out=v3(E2), in0=p1[0:64, :].rearrange("p (n x) -> p n x", x=64), in1=bc(tm, 2, n0), op=ALU.subtract)
        I(P, 'act', 'activation', [bE2], [bE2], out=E2[:], in_=E2[:], func=AF.Exp)
        yield
        ngt, bngt = gtr()
        I(P, 'dve', 'tensor_scalar', [bgtri], [bngt], out=ngt[:], in0=gtri[:], scalar1=-1.0, scalar2=None, op0=ALU.mult)
        p2, bp2 = C.next_ps()
        MM(P, p2[0:64, :], C.ones32[0:64, 0:64], ngt[:], True, False, [C.bconst, bngt], bp2)
        MM(P, p2[0:64, :], C.g_idn[:], C.g_mS[:].rearrange("p n x -> p (n x)"), False, True, [C.bconst], bp2)
        E1, bE1 = f32r()
        I(P, 'dve', 'tensor_tensor', [bp2, btm], [bE1], out=v3(E1), in0=p2[0:64, :].rearrange("p (n x) -> p n x", x=64), in1=bc(tm, 2, n0), op=ALU.add)
        I(P, 'act', 'activation', [bE1], [bE1], out=E1[:], in_=E1[:], func=AF.Exp)
        yield
        bdg, bbdg = gtr()
        I(P, 'dve', 'tensor_tensor', [btm, C.bconst], [bbdg], out=v3(bdg), in0=C.g_idn[:, :].unsqueeze(1).to_broadcast([64, GB, 64]), in1=bc(tm, 5, n0), op=ALU.mult)
        p3, bp3 = C.next_ps()
        MM(P, p3[0:64, :], C.g_off[:], bdg[:], True, True, [C.bconst, bbdg], bp3)
        pk, bpk = C.next_ps()
        pq, bpq = C.next_ps()
        for n in range(GB):
            cs = slice(b * 512 + n * 64, b * 512 + (n + 1) * 64)
            MM(P, pk[0:64, n * 64:(n + 1) * 64], d['kb'][:, cs], d['kb'][:, cs], True, True, [d['bkb']], bpk)
        for n in range(GB):
            cs = slice(b * 512 + n * 64, b * 512 + (n + 1) * 64)
            MM(P, pq[0:64, n * 64:(n + 1) * 64], d['kb'][:, cs], d['qb'][:, cs], True, True, [d['bkb'], d['bqb']], bpq)
        N0, bN0 = f32r()
        I(P, 'dve', 'tensor_tensor', [bpk, bE1], [bN0], out=N0[:], in0=pk[0:64, :], in1=E1[:], op=ALU.mult)
        I(P, 'dve', 'tensor_tensor', [bN0, btm], [bN0], out=v3(N0), in0=v3(N0), in1=bc(tm, 5, n0), op=ALU.mult)
        NT0, bNT0 = f32r()
        I(P, 'dve', 'tensor_tensor', [bpk, bE2], [bNT0], out=NT0[:], in0=pk[0:64, :], in1=E2[:], op=ALU.mult)
        I(P, 'dve', 'tensor_tensor', [bNT0, bp3], [bNT0], out=NT0[:], in0=NT0[:], in1=p3[0:64, :], op=ALU.mult)
        at, bat = atr()
        I(P, 'dve', 'tensor_tensor', [bpq, bE2], [bat], out=at[:], in0=pq[0:64, :], in1=E2[:], op=ALU.mult)
        if hi == 0 and b == 0:
            C.dbg('d_E1', E1[:], [bE1]); C.dbg('d_E2', E2[:], [bE2]); C.dbg('d_N0', N0[:], [bN0]); C.dbg('d_NT0', NT0[:], [bNT0]); C.dbg('d_at', at[:], [bat])
        yield
        X, bX = Xr()
        I(P, 'dve', 'tensor_tensor', [bNT0, C.bconst], [bX], out=v3(X), in0=v3(NT0), in1=C.g_idn[:, :].unsqueeze(1).to_broadcast([64, GB, 64]), op=ALU.add)
        Pj, bPj, PTj, bPTj = N0, bN0, NT0, bNT0
        for lvl in range(1, 6):
            pa, bpa = C.next_ps()
            for n in range(GB):
                sl = slice(n * 64, (n + 1) * 64)
                MM(P, pa[0:64, sl], PTj[:, sl], Pj[:, sl], True, True, [bPTj, bPj], bpa)
            Pn, bPn = f32r()
            I(P, 'act', 'copy', [bpa], [bPn], out=Pn[:], in_=pa[0:64, :])
            if lvl < 5:
                pb_, bpb_ = C.next_ps()
                for n in range(GB):
                    sl = slice(n * 64, (n + 1) * 64)
                    MM(P, pb_[0:64, sl], Pj[:, sl], PTj[:, sl], True, True, [bPTj, bPj], bpb_)
                PTn, bPTn = f32r()
                I(P, 'act', 'copy', [bpb_], [bPTn], out=PTn[:], in_=pb_[0:64, :])
            px, bpx = C.next_ps()
            for n in range(GB):
                sl = slice(n * 64, (n + 1) * 64)
                MM(P, px[0:64, sl], Pn[:, sl], X[:, sl], True, True, [bPn, bX], bpx)
            I(P, 'dve', 'tensor_tensor', [bpx, bX], [bX], out=X[:], in0=X[:], in1=px[0:64, :], op=ALU.add)
            Pj, bPj = Pn, bPn
            if lvl < 5:
                PTj, bPTj = PTn, bPTn
            yield
        if hi == 0 and b == 0:
            C.dbg('d_X', X[:], [bX])
        Xb, bXb = Xbr()
        I(P, 'act', 'copy', [bX], [bXb], out=Xb[:], in_=X[:])
        Rt, bRt = Rr()
        kg, bkg = kgr()
        for (src, bsrc, which) in ((d['kb'], d['bkb'], 'k'), (d['vb'], d['bvb'], 'v')):
            for half in range(2):
                pt, bpt = C.next_ps()
                ptb = pt[:].bitcast(BF16)
                for n in range(4):
                    nn = half * 4 + n
                    cs = slice(b * 512 + nn * 64, b * 512 + (nn + 1) * 64)
                    P.mm(lambda e, o=ptb[0:64, n * 128:(n + 1) * 128], i_=src[:, cs]: e.transpose(o, i_, C.g_idb[:]), [bsrc, C.bconst], bpt)
                pv3 = ptb[0:64, 0:512].rearrange("p (n x) -> p n x", x=128)
                hs = slice(half * 4, half * 4 + 4)

                def bc4(w):
                    return tm[:, w, n0 + half * 4:n0 + half * 4 + 4].unsqueeze(2).to_broadcast([64, 4, 128])
                if which == 'k':
                    I(P, 'dve', 'tensor_tensor', [bpt, btm], [bRt], out=Rt[:, hs, 128:256], in0=pv3, in1=bc4(7), op=ALU.mult)
                    I(P, 'dve', 'tensor_tensor', [bpt, btm], [bkg], out=kg[:, hs, :], in0=pv3, in1=bc4(4), op=ALU.mult)
                else:
                    I(P, 'dve', 'tensor_tensor', [bpt, btm], [bRt], out=Rt[:, hs, 0:128], in0=pv3, in1=bc4(1), op=ALU.mult)
            yield
        UW, bUW = UWr()
        for pr in range(4):
            pu, bpu = C.next_ps()
            for n2 in range(2):
                n = pr * 2 + n2
                MM(P, pu[0:64, n2 * 256:(n2 + 1) * 256], Xb[:, n * 64:(n + 1) * 64], Rt[:, n, :], True, True, [bXb, bRt], bpu)
            I(P, 'act', 'copy', [bpu], [bUW], out=UW[:, pr * 2:pr * 2 + 2, :].rearrange("p n x -> p (n x)"), in_=pu[0:64, :])
        yield
        NT, bNT = NTr()
        for pr in range(2):
            pn, bpn = C.next_ps()
            for n4 in range(4):
                n = pr * 4 + n4
                MM(P, pn[:, n4 * 128:(n4 + 1) * 128], UW[:, n, 128:256], kg[:, n, :], True, True, [bUW, bkg], bpn)
            I(P, 'act', 'copy', [bpn], [bNT], out=NT[:, pr * 4:pr * 4 + 4, :].rearrange("p n x -> p (n x)"), in_=pn[:])
        yield
        edd, bedd = gtr()
        I(P, 'dve', 'tensor_tensor', [btm, C.bconst], [bedd], out=v3(edd), in0=C.g_idn[:, :].unsqueeze(1).to_broadcast([64, GB, 64]), in1=bc(tm, 3, n0), op=ALU.mult)
        pe_, bpe_ = C.next_ps()
        MM(P, pe_[:], C.ones32[0:64, :], edd[:], True, True, [C.bconst, bedd], bpe_)
        q32, bq32 = q32r()
        I(P, 'dve', 'tensor_tensor', [d['bqb'], bpe_], [bq32], out=q32[:], in0=d['qb'][:, tok], in1=pe_[:], op=ALU.mult)
        pw, bpw = C.next_ps()
        for n in range(GB):
            MM(P, pw[:, n * 64:(n + 1) * 64], UW[:, n, 128:256], at[:, n * 64:(n + 1) * 64], True, True, [bUW, bat], bpw)
        Qp, bQp = Qpr()
        I(P, 'dve', 'tensor_tensor', [bq32, bpw], [bQp], out=Qp[:], in0=q32[:], in1=pw[:], op=ALU.add)
        if hi == 0 and b == 0:
            C.dbg('d_R', Rt[:].rearrange("p n x -> p (n x)"), [bRt]); C.dbg('d_kg', kg[:].rearrange("p n x -> p (n x)"), [bkg])
            C.dbg('d_UW', UW[:].rearrange("p n x -> p (n x)"), [bUW]); C.dbg('d_NT', NT[:].rearrange("p n x -> p (n x)"), [bNT]); C.dbg('d_Qp', Qp[:], [bQp])
        d['cur'] = dict(UW=UW, bUW=bUW, kg=kg, bkg=bkg, at=at, bat=bat, NT=NT, bNT=bNT, Qp=Qp, bQp=bQp)
        yield

    def chain(hidx, b, cur):
        hi = hidx % 2
        d = HT[hi]
        po, bpo = C.acc_o[hi]
        for n in range(GB):
            gn = b * GB + n
            first = (gn == 0)
            osl = po[:, n * 64:(n + 1) * 64]
            if not first:
                MM(P, osl, d['Sb'][:], cur['Qp'][:, n * 64:(n + 1) * 64], True, False, [d['bSb'], cur['bQp']], bpo)
            MM(P, osl, cur['UW'][:, n, 0:128], cur['at'][:, n * 64:(n + 1) * 64], first, True, [cur['bUW'], cur['bat']], bpo)
            ps, bps = C.acc_s[hi]
            if not first:
                MM(P, ps[:, 0:128], cur['NT'][:, n, :], d['Sb'][:], True, False, [cur['bNT'], d['bSb']], bps)
            MM(P, ps[:, 0:128], cur['kg'][:, n, :], cur['UW'][:, n, 0:128], first, True, [cur['bkg'], cur['bUW']], bps)
            I(P, 'dve', 'scalar_tensor_tensor', [bps, d['bS32'], d['bdl']], [d['bS32']], out=d['S32'][:], in0=d['S32'][:], scalar=d['dl'][:, gn:gn + 1], in1=ps[:, 0:128], op0=ALU.mult, op1=ALU.add)
            I(P, 'act', 'copy', [d['bS32']], [d['bSb']], out=d['Sb'][:], in_=d['S32'][:])
            yield
        (q0, k0, v0, gr, o0) = heads[hidx]
        o32, bo32 = o32r()
        I(P, 'act', 'copy', [bpo], [bo32], out=o32[:], in_=po[:])
        sq, bsq = o32r()
        I(P, 'act', 'activation', [bpo], [bsq], out=sq[:], in_=po[:], func=AF.Square)
        pm, bpm = C.next_ps()
        MM(P, pm[:], C.g_avg[:], sq[:], True, True, [C.bconst, bsq], bpm)
        I(P, 'act', 'activation', [bpm, C.bconst], [bsq], out=sq[:], in_=pm[:], func=AF.Ln, bias=C.eps_n[:, 0:1], scale=1.0)
        I(P, 'act', 'activation', [bsq], [bsq], out=sq[:], in_=sq[:], func=AF.Exp, scale=-0.5)
        I(P, 'dve', 'scalar_tensor_tensor', [bo32, bsq, bng], [bo32], out=o32[:], in0=o32[:], scalar=ngl[:, 0:1], in1=sq[:], op0=ALU.mult, op1=ALU.mult)
        sz, bsz = szr()
        P.dma('sp', sz[:], szT[o0:o0 + 128, b * 512:(b + 1) * 512], [bdram_in], bsz)
        ob, bob = obr()
        I(P, 'dve', 'tensor_tensor', [bo32, bsz], [bob], out=ob[:], in0=o32[:], in1=sz[:], op=ALU.mult)
        P.dma('sp', mixT[o0:o0 + 128, b * 512:(b + 1) * 512], ob[:], [bob], bdram_out)
        yield

    C.acc_o = [C.acc(0), C.acc(1)]
    C.acc_s = [C.acc(2), C.acc(3)]
    def headB(hidx):
        hi = hidx % 2
        for _ in pre(hidx, 0):
            yield
        for b in range(NB):
            cg = chain(hidx, b, HT[hi]['cur'])
            pg = pre(hidx, b + 1) if b + 1 < NB else iter(())
            c_alive = p_alive = True
            while c_alive or p_alive:
                if c_alive:
                    try:
                        next(cg)
                    except StopIteration:
                        c_alive = False
                if p_alive:
                    for _ in range(2):
                        try:
                            next(pg)
                        except StopIteration:
                            p_alive = False
                            break
                yield

    for _ in phaseA(0):
        pass
    for hidx in range(nh):
        bg = phaseA(hidx + 1) if hidx + 1 < nh else None
        tick = 0
        for _ in headB(hidx):
            tick += 1
            if bg is not None and tick % 3 == 0:
                try:
                    next(bg)
                except StopIteration:
                    bg = None
        if bg is not None:
            for _ in bg:
                pass
    C.nring = 8


import ml_dtypes
from concourse.bass_utils import run_bass_kernel_spmd

KINDS = ['gdn', 'mlstm', 'fox', 'gdn']
NCOLS = {'gdn': 4624, 'mlstm': 3600, 'fox': 4624}
TC = 2048
SEQ = 8192
_BF = ml_dtypes.bfloat16
_progs = {}


def _p_out_specs(kind, T):
    if kind == 'gdn':
        return dict(qkvT=([3072, T], F32), szT=([1024, T], F32), gT=([8, T], F32), betaT=([8, T], F32), memoT=([512, T], BF16))
    if kind == 'mlstm':
        return dict(qT=([512, T], F32), kT=([512, T], F32), v=([T, 1024], BF16), sogT=([1024, T], F32), ilogT=([8, T], F32), flogT=([8, T], F32), memoT=([512, T], BF16))
    return dict(qT=([1024, T], BF16), kT=([1024, T], BF16), v=([T, 1024], BF16), sogT=([1024, T], F32), flogT=([16, T], F32), memoT=([512, T], BF16))


def _p_prm_specs(kind):
    if kind == 'gdn':
        return dict(a_log=[8], dt_bias=[8])
    if kind == 'mlstm':
        return dict(b_gate=[2, 8])
    return dict(b_f=[16], qk_g=[2, 64])


def _new_nc():
    return bass.Bass("TRN2", target_bir_lowering=False)


def build_P(kind):
    key = ('P', kind)
    if key in _progs:
        return _progs[key]
    nc = _new_nc()
    di = lambda n, s, dt=F32: nc.dram_tensor(n, s, dt, kind="ExternalInput").ap()
    xT = di("xT", [1024, TC]); memT = di("memT", [1024, 256]); w_in = di("w_in", [1024, NCOLS[kind]]); w_kv = di("w_kv", [1024, 1024])
    prm = {k: di(k, s) for k, s in _p_prm_specs(kind).items()}
    outs = {k: nc.dram_tensor(k, s, dt, kind="ExternalOutput").ap() for k, (s, dt) in _p_out_specs(kind, TC).items()}
    with ExitStack() as st:
        P = Prog(nc, st)
        C = Ctx(nc, st, P)
        stage_P(C, TC, kind, xT, memT, w_in, w_kv, prm, outs, P.buf("din"), P.buf("dout"))
        P.barrier()
        P.emit()
    _progs[key] = nc
    return nc


def build_F():
    key = ('F',)
    if key in _progs:
        return _progs[key]
    nc = _new_nc()
    di = lambda n, s, dt=F32: nc.dram_tensor(n, s, dt, kind="ExternalInput").ap()
    mixT = di("mixT", [1024, TC], BF16); memoT = di("memoT", [512, TC], BF16); xT = di("xT", [1024, TC])
    w_out = di("w_out", [1536, 1024]); w_up = di("w_up", [1024, 4096]); w_down = di("w_down", [4096, 1024])
    l1g = di("l1g", [1024]); l1b = di("l1b", [1024]); l2g = di("l2g", [1024]); l2b = di("l2b", [1024])
    yT = nc.dram_tensor("yT", [1024, TC], F32, kind="ExternalOutput").ap()
    with ExitStack() as st:
        P = Prog(nc, st)
        C = Ctx(nc, st, P)
        stage_F(C, TC, mixT, memoT, xT, yT, w_out, l1g, l1b, w_up, w_down, l2g, l2b, P.buf("din"), P.buf("dout"))
        P.barrier()
        P.emit()
    _progs[key] = nc
    return nc


def build_M(kind):
    key = ('M', kind)
    if key in _progs:
        return _progs[key]
    nc = _new_nc()
    S = SEQ
    di = lambda n, s, dt=F32: nc.dram_tensor(n, s, dt, kind="ExternalInput").ap()
    with ExitStack() as st:
        P = Prog(nc, st)
        C = Ctx(nc, st, P)
        bin_, bout = P.buf("din"), P.buf("dout")
        mixT = nc.dram_tensor("mixT", [256, S], BF16, kind="ExternalOutput").ap()
        if kind == 'gdn':
            qkvT = di("qkvT", [768, S]); conv_w = di("conv_w", [4, 768]); szT = di("szT", [256, S]); gT = di("gT", [2, S]); betaT = di("betaT", [2, S]); ng = di("ng", [128])
            heads = [(h * 128, 256 + h * 128, 512 + h * 128, h, h * 128) for h in range(2)]
            mixer_gdn(C, S, heads, qkvT, conv_w, szT, gT, betaT, ng, mixT, bin_, bout)
        elif kind == 'mlstm':
            qT = di("qT", [128, S]); kT = di("kT", [128, S]); v = di("v", [S, 256], BF16)
            sogT = di("sogT", [256, S]); flogT = di("flogT", [2, S]); ilogT = di("ilogT", [2, S]); ng = di("ng", [256])
            mixer_mlstm(C, S, 2, qT, kT, v, sogT, ilogT, flogT, ng, mixT, bin_, bout)
        else:
            qT = di("qT", [256, S], BF16); kT = di("kT", [256, S], BF16); v = di("v", [S, 256], BF16)
            sogT = di("sogT", [256, S]); flogT = di("flogT", [4, S])
            mixer_fox(C, S, 4, qT, kT, v, sogT, flogT, mixT, bin_, bout)
        P.barrier()
        P.emit()
    _progs[key] = nc
    return nc


FUSED = True
_W_SPECS = dict(gdn_w_in=[2, 1024, 4624], gdn_conv_w=[2, 4, 3072], gdn_a_log=[2, 8], gdn_dt_bias=[2, 8], gdn_norm_g=[2, 128],
                mlstm_w_in=[1, 1024, 3600], mlstm_b_gate=[1, 2, 8], mlstm_norm_g=[1, 1024], fox_w_in=[1, 1024, 4624], fox_b_f=[1, 16],
                fox_qk_g=[1, 2, 64], mem_w_kv=[4, 1024, 1024], w_out=[4, 1536, 1024], ln1_g=[4, 1024], ln1_b=[4, 1024],
                w_up=[4, 1024, 4096], w_down=[4, 4096, 1024], ln2_g=[4, 1024], ln2_b=[4, 1024])


def build_fused(nlayers=4):
    key = ('fused', nlayers)
    if key in _progs:
        return _progs[key]
    nc = _new_nc()
    S = SEQ
    di = lambda n, s, dt=F32: nc.dram_tensor(n, s, dt, kind="ExternalInput").ap()
    sc = lambda n, s, dt=F32: nc.dram_tensor(n, s, dt).ap()
    xT0 = di("xT", [1024, S]); memT = di("memT", [1024, 256])
    W = {k: di(k, shp) for k, shp in _W_SPECS.items()}
    yT = nc.dram_tensor("yT", [1024, S], F32, kind="ExternalOutput").ap()
    xa, xb_ = sc("xa", [1024, S]), sc("xb", [1024, S])
    scr = {}
    for kind in ('gdn', 'mlstm', 'fox'):
        scr[kind] = {k: sc("%s_%s" % (kind, k), shp, dt) for k, (shp, dt) in _p_out_specs(kind, S).items()}
    mixT = sc("mixT", [1024, S], BF16)
    with ExitStack() as st:
        P = Prog(nc, st)
        C = Ctx(nc, st, P)
        cur = xT0
        for li in range(nlayers):
            kind, j = KINDS[li], li // 3
            nxt = yT if li == nlayers - 1 else (xa if li % 2 == 0 else xb_)
            o = scr[kind]
            if kind == 'gdn':
                w_in = W['gdn_w_in'][j]
                prm = dict(a_log=W['gdn_a_log'][j], dt_bias=W['gdn_dt_bias'][j])
            elif kind == 'mlstm':
                w_in = W['mlstm_w_in'][j]
                prm = dict(b_gate=W['mlstm_b_gate'][j])
            else:
                w_in = W['fox_w_in'][j]
                prm = dict(b_f=W['fox_b_f'][j], qk_g=W['fox_qk_g'][j])
            C.stage_begin()
            for tc in range(S // TC):
                tok = slice(tc * TC, (tc + 1) * TC)
                outs = {k: (a[tok, :] if k == 'v' else a[:, tok]) for k, a in o.items()}
                stage_P(C, TC, kind, cur[:, tok], memT, w_in, W['mem_w_kv'][li], prm, outs, C.buf("din"), C.buf("dout"))
            C.stage_end()
            C.stage_begin()
            bi, bo = C.buf("din"), C.buf("dout")
            if kind == 'gdn':
                heads = [(h * 128, 1024 + h * 128, 2048 + h * 128, h, h * 128) for h in range(8)]
                mixer_gdn(C, S, heads, o['qkvT'], W['gdn_conv_w'][j], o['szT'], o['gT'], o['betaT'], W['gdn_norm_g'][j], mixT, bi, bo)
            elif kind == 'mlstm':
                mixer_mlstm(C, S, 8, o['qT'], o['kT'], o['v'], o['sogT'], o['ilogT'], o['flogT'], W['mlstm_norm_g'][j], mixT, bi, bo)
            else:
                mixer_fox(C, S, 16, o['qT'], o['kT'], o['v'], o['sogT'], o['flogT'], mixT, bi, bo)
            C.stage_end()
            C.stage_begin()
            for tc in range(S // TC):
                tok = slice(tc * TC, (tc + 1) * TC)
                stage_F(C, TC, mixT[:, tok], o['memoT'][:, tok], cur[:, tok], nxt[:, tok], W['w_out'][li], W['ln1_g'][li], W['ln1_b'][li],
                        W['w_up'][li], W['w_down'][li], W['ln2_g'][li], W['ln2_b'][li], C.buf("din"), C.buf("dout"))
            C.stage_end()
            cur = nxt
    _progs[key] = nc
    return nc


def kernel_fused(inputs):
    f32 = np.float32
    x = np.asarray(inputs['x'], f32)
    mem = np.asarray(inputs['mem'], f32)
    wts = {}
    for k, shp in _W_SPECS.items():
        wts[k] = _c(np.asarray(inputs[k], f32).reshape(shp))
    ims = []
    for c in range(8):
        b = c % 2
        ims.append(dict(xT=_c(x[b].T), memT=_c(mem[b].T), **wts))
    res = _run(build_fused(), ims)
    out = np.empty((2, SEQ, 1024), f32)
    for b in range(2):
        out[b] = np.asarray(res[b]['yT']).T
    return out


def _run(nc, in_maps):
    res = run_bass_kernel_spmd(nc, in_maps, core_ids=list(range(8)))
    return res.results


def _cat_tok(outs, name, b, axis):
    return np.concatenate([np.asarray(outs[b * 4 + c][name]) for c in range(4)], axis=axis)


def _c(a):
    return np.ascontiguousarray(a)


def kernel(x, mem, gdn_w_in, gdn_conv_w, gdn_a_log, gdn_dt_bias, gdn_norm_g,
           mlstm_w_in, mlstm_b_gate, mlstm_norm_g, fox_w_in, fox_b_f, fox_qk_g,
           mem_w_kv, w_out, ln1_g, ln1_b, w_up, w_down, ln2_g, ln2_b):
    if FUSED:
        return kernel_fused(dict(x=x, mem=mem, gdn_w_in=gdn_w_in, gdn_conv_w=gdn_conv_w, gdn_a_log=gdn_a_log, gdn_dt_bias=gdn_dt_bias,
                                 gdn_norm_g=gdn_norm_g, mlstm_w_in=mlstm_w_in, mlstm_b_gate=mlstm_b_gate, mlstm_norm_g=mlstm_norm_g,
                                 fox_w_in=fox_w_in, fox_b_f=fox_b_f, fox_qk_g=fox_qk_g, mem_w_kv=mem_w_kv, w_out=w_out, ln1_g=ln1_g,
                                 ln1_b=ln1_b, w_up=w_up, w_down=w_down, ln2_g=ln2_g, ln2_b=ln2_b))
    f32 = np.float32
    x = np.asarray(x, f32)
    mem = np.asarray(mem, f32)
    xT = [_c(x[c // 4, (c % 4) * TC:(c % 4 + 1) * TC, :].T) for c in range(8)]
    memT = [_c(mem[b].T) for b in range(2)]
    for li in range(4):
        kind, j = KINDS[li], li // 3
        if kind == 'gdn':
            w_in = np.asarray(gdn_w_in[j], f32)
            prm = dict(a_log=np.asarray(gdn_a_log[j], f32), dt_bias=np.asarray(gdn_dt_bias[j], f32))
        elif kind == 'mlstm':
            w_in = np.asarray(mlstm_w_in[j], f32)
            prm = dict(b_gate=np.asarray(mlstm_b_gate[j], f32))
        else:
            w_in = np.asarray(fox_w_in[j], f32)
            prm = dict(b_f=np.asarray(fox_b_f[j], f32), qk_g=np.asarray(fox_qk_g[j], f32))
        wkv = np.asarray(mem_w_kv[li], f32)
        ims = [dict(xT=xT[c], memT=memT[c // 4], w_in=w_in, w_kv=wkv, **prm) for c in range(8)]
        po = _run(build_P(kind), ims)
        ims = []
        for jc in range(8):
            b, hg = jc // 4, jc % 4
            if kind == 'gdn':
                qkv = _cat_tok(po, 'qkvT', b, 1) if hg == 0 else qkv_cache
                qkv_cache = qkv
                hs = [2 * hg, 2 * hg + 1]
                rows = np.concatenate([np.arange(o + h * 128, o + (h + 1) * 128) for o in (0, 1024, 2048) for h in hs])
                if hg == 0:
                    sz_c = _cat_tok(po, 'szT', b, 1); g_c = _cat_tok(po, 'gT', b, 1); be_c = _cat_tok(po, 'betaT', b, 1)
                ims.append(dict(qkvT=_c(qkv[rows]), conv_w=_c(np.asarray(gdn_conv_w[j], f32)[:, rows]), szT=_c(sz_c[hs[0] * 128:(hs[1] + 1) * 128]),
                                gT=_c(g_c[hs[0]:hs[1] + 1]), betaT=_c(be_c[hs[0]:hs[1] + 1]), ng=np.asarray(gdn_norm_g[j], f32)))
            elif kind == 'mlstm':
                if hg == 0:
                    q_c = _cat_tok(po, 'qT', b, 1); k_c = _cat_tok(po, 'kT', b, 1); v_c = _cat_tok(po, 'v', b, 0)
                    so_c = _cat_tok(po, 'sogT', b, 1); il_c = _cat_tok(po, 'ilogT', b, 1); fl_c = _cat_tok(po, 'flogT', b, 1)
                h0 = 2 * hg
                ims.append(dict(qT=_c(q_c[h0 * 64:(h0 + 2) * 64]), kT=_c(k_c[h0 * 64:(h0 + 2) * 64]), v=_c(v_c[:, h0 * 128:(h0 + 2) * 128]),
                                sogT=_c(so_c[h0 * 128:(h0 + 2) * 128]), ilogT=_c(il_c[h0:h0 + 2]), flogT=_c(fl_c[h0:h0 + 2]),
                                ng=_c(np.asarray(mlstm_norm_g[j], f32)[h0:h0 + 2].reshape(-1))))
            else:
                if hg == 0:
                    q_c = _cat_tok(po, 'qT', b, 1); k_c = _cat_tok(po, 'kT', b, 1); v_c = _cat_tok(po, 'v', b, 0)
                    so_c = _cat_tok(po, 'sogT', b, 1); fl_c = _cat_tok(po, 'flogT', b, 1)
                h0 = 4 * hg
                ims.append(dict(qT=_c(q_c[h0 * 64:(h0 + 4) * 64]), kT=_c(k_c[h0 * 64:(h0 + 4) * 64]), v=_c(v_c[:, h0 * 64:(h0 + 4) * 64]),
                                sogT=_c(so_c[h0 * 64:(h0 + 4) * 64]), flogT=_c(fl_c[h0:h0 + 4])))
        mo = _run(build_M(kind), ims)
        ims = []
        for c in range(8):
            b, sc = c // 4, c % 4
            mixT = np.concatenate([np.asarray(mo[b * 4 + hg]['mixT'])[:, sc * TC:(sc + 1) * TC] for hg in range(4)], axis=0)
            ims.append(dict(mixT=_c(mixT), memoT=_c(np.asarray(po[c]['memoT'])), xT=xT[c],
                            w_out=np.asarray(w_out[li], f32), w_up=np.asarray(w_up[li], f32), w_down=np.asarray(w_down[li], f32),
                            l1g=np.asarray(ln1_g[li], f32), l1b=np.asarray(ln1_b[li], f32), l2g=np.asarray(ln2_g[li], f32), l2b=np.asarray(ln2_b[li], f32)))
        fo = _run(build_F(), ims)
        xT = [_c(np.asarray(fo[c]['yT'])) for c in range(8)]
    out = np.empty((2, SEQ, 1024), f32)
    for c in range(8):
        out[c // 4, (c % 4) * TC:(c % 4 + 1) * TC, :] = xT[c].T
    return out
```

```python
from contextlib import ExitStack
import numpy as np
import concourse.bass as bass
import concourse.mybir as mybir

F32 = mybir.dt.float32
BF16 = mybir.dt.bfloat16
AF = mybir.ActivationFunctionType
ALU = mybir.AluOpType
AX = mybir.AxisListType

ENGS = ['pe', 'act', 'dve', 'pool', 'sp']
SYNC_SAME_ENGINE = True
NDSEM = 32


class Buf:
    __slots__ = ('name', 'last_w', 'readers', 'sem', 'cnt')

    def __init__(self, name):
        self.name = name
        self.last_w = None
        self.readers = []
        self.sem = None
        self.cnt = 0


class Op:
    __slots__ = ('eng', 'fn', 'deps', 'is_dma', 'tok', 'signal', 'seq', 'pe_buf', 'inc')

    def __init__(self, eng, fn):
        self.eng = eng
        self.fn = fn
        self.deps = []
        self.is_dma = False
        self.tok = None
        self.signal = False
        self.seq = None
        self.pe_buf = None
        self.inc = 1


class Prog:
    def __init__(self, nc, stack):
        self.nc = nc
        self.stack = stack
        self.ops = {e: [] for e in ENGS}
        self.esem = {e: stack.enter_context(nc.semaphore('es_' + e)) for e in ENGS}
        self.ecnt = {e: 0 for e in ENGS}
        self.waited = {e: {} for e in ENGS}
        self.bufs = []
        self.dsem = [stack.enter_context(nc.semaphore('ds_%d' % i)) for i in range(NDSEM)]
        self.dlast = [None] * NDSEM
        self.dcount = 0
        self.nops = 0
        self.serial = False
        self.prev = None

    def buf(self, name):
        b = Buf(name)
        self.bufs.append(b)
        return b

    def _deps(self, op, reads, writes):
        if self.serial:
            if self.prev is not None:
                op.deps.append(self.prev)
            self.prev = op
        for b in reads:
            if b.last_w is not None:
                op.deps.append(b.last_w)
        for b in writes:
            if b.last_w is not None:
                op.deps.append(b.last_w)
            op.deps.extend(b.readers)
        for b in reads:
            b.readers.append(op)
        for b in writes:
            b.last_w = op
            b.readers = []

    def op(self, eng, fn, reads=(), writes=(), inc=1):
        o = Op(eng, fn)
        o.inc = inc
        self._deps(o, reads, writes)
        self.ops[eng].append(o)
        self.nops += 1
        return o

    def mm(self, fn, reads, out):
        o = Op('pe', fn)
        o.pe_buf = out
        self._deps(o, reads, [out])
        o.deps = [d for d in o.deps if not (d.eng == 'pe' and d.pe_buf is out)]
        self.ops['pe'].append(o)
        self.nops += 1
        return o

    def dma(self, queue, out, in_, reads, wbuf, **kw):
        n = self.dcount
        self.dcount += 1
        i = n % NDSEM
        sem = self.dsem[i]
        val = 16 * (n // NDSEM + 1)

        def fn(eng, out=out, in_=in_, kw=kw):
            return eng.dma_start(out=out, in_=in_, **kw)
        o = Op(queue, fn)
        o.is_dma = True
        o.tok = (sem, val)
        if self.dlast[i] is not None:
            o.deps.append(self.dlast[i])
        self.dlast[i] = o
        if not isinstance(wbuf, (list, tuple)):
            wbuf = [wbuf]
        self._deps(o, reads, wbuf)
        self.ops[queue].append(o)
        self.nops += 1
        return o

    def barrier(self):
        lasts = []
        for e in ENGS:
            if self.ops[e]:
                for o in reversed(self.ops[e]):
                    if not o.is_dma:
                        lasts.append(o)
                        break
        dl = [o for o in self.dlast if o is not None]
        for e in ENGS:
            o = Op(e, lambda eng: eng.nop())
            o.deps = list(lasts) + dl
            self.ops[e].append(o)
        for b in self.bufs:
            b.last_w = None
            b.readers = []

    def emit(self):
        nc = self.nc
        for e in ENGS:
            for o in self.ops[e]:
                for d in o.deps:
                    if not d.is_dma:
                        if d.eng == e and not SYNC_SAME_ENGINE:
                            continue
                        d.signal = True
        for e in ENGS:
            for o in self.ops[e]:
                if o.signal and not o.is_dma and o.seq is None:
                    self.ecnt[e] += o.inc
                    o.seq = self.ecnt[e]
        engobj = {'pe': nc.tensor, 'act': nc.scalar, 'dve': nc.vector, 'pool': nc.gpsimd, 'sp': nc.sync}
        stats = {'waits': 0}

        def run(e, eng):
            waited = self.waited[e]
            for o in self.ops[e]:
                need = {}
                for d in o.deps:
                    if d.is_dma:
                        sem, val = d.tok
                    else:
                        if d.eng == e and not SYNC_SAME_ENGINE:
                            continue
                        sem, val = self.esem[d.eng], d.seq
                    k = id(sem)
                    if waited.get(k, (None, 0))[1] >= val:
                        continue
                    if k not in need or need[k][1] < val:
                        need[k] = (sem, val)
                for k, (sem, val) in need.items():
                    eng.wait_ge(sem, val)
                    waited[k] = (sem, val)
                    stats['waits'] += 1
                ins = o.fn(eng)
                if o.is_dma:
                    ins.then_inc(o.tok[0], 16)
                elif o.signal:
                    ins.then_inc(self.esem[e], o.inc)
            self.ops[e] = []

        with nc.Block() as block:
            @block.tensor
            def _(eng):
                run('pe', eng)

            @block.scalar
            def _(eng):
                run('act', eng)

            @block.vector
            def _(eng):
                run('dve', eng)

            @block.gpsimd
            def _(eng):
                run('pool', eng)

            @block.sync
            def _(eng):
                run('sp', eng)
        return stats


D = 1024
DFF = 4096
ALPHA = 8 ** 0.25
LN_EPS = 1e-5
NORM_EPS = 1e-6
TT = 512


class Ctx:
    def __init__(self, nc, st, P):
        self.nc, self.st, self.P = nc, st, P
        self.ps = []
        for i in range(8):
            t = st.enter_context(nc.psum_tensor("ps%d" % i, [128, 512], F32))
            self.ps.append((t, P.buf("ps%d" % i)))
        self.psi = 0
        self.nring = 8
        self.cache = {}
        self.pcache = {}
        self.pools = {}
        self.sst = st
        self.stage_no = 0
        self.wr = []
        self.wi = 0
        self.ones32 = self.sbp("ones32", [128, 128], F32)
        self.onesb = self.sbp("onesb", [128, 128], BF16)
        b = P.buf("consts")
        self.bconst = b
        P.op('pool', lambda e: e.memset(self.ones32[:], 1.0), [], [b])
        P.op('pool', lambda e: e.memset(self.onesb[:], 1.0), [], [b])
        self.eps_ln = self.sbp('eps_ln', [128, 1], F32)
        self.eps_n = self.sbp('eps_n', [128, 1], F32)
        P.op('pool', lambda e: e.memset(self.eps_ln[:], LN_EPS), [], [b])
        P.op('pool', lambda e: e.memset(self.eps_n[:], NORM_EPS), [], [b])

    def sb(self, name, shape, dt):
        if name in self.pcache:
            return self.pcache[name]
        if name not in self.cache:
            self.cache[name] = self.sst.enter_context(self.nc.sbuf_tensor("%s_s%d" % (name, self.stage_no), shape, dt))
        return self.cache[name]

    def sbp(self, name, shape, dt):
        if name not in self.pcache:
            self.pcache[name] = self.st.enter_context(self.nc.sbuf_tensor(name, shape, dt))
        return self.pcache[name]

    def stage_begin(self):
        self.stage_no += 1
        self.sst = ExitStack()
        self.cache = {}
        self.wr = []
        self.wi = 0

    def stage_end(self):
        self.P.barrier()
        self.P.emit()
        self.sst.close()
        self.sst = self.st
        self.cache = {}
        self.wr = []

    def buf(self, name):
        k = 'buf:' + name
        if k not in self.cache:
            self.cache[k] = self.P.buf(name)
        return self.cache[k]

    def next_ps(self, pool=None):
        if pool is not None:
            banks, st = self.pools[pool]
            t, b = self.ps[banks[st[0] % len(banks)]]
            st[0] += 1
            return t, b
        t, b = self.ps[self.psi % self.nring]
        self.psi += 1
        return t, b

    def set_pool(self, name, banks):
        self.pools[name] = (list(banks), [0])

    def dbg(self, name, ap, bufs):
        d = getattr(self, 'dbgs', None)
        if d and name in d and name not in self.cache:
            self.cache[name] = True
            self.P.dma('sp', d[name], ap, list(bufs), self.buf('dbg_out'))

    def acc(self, i):
        return self.ps[self.nring + i]

    def load_w(self, w_ap, r0, nrows, c0, ncols):
        kc = nrows // 128
        if not self.wr:
            for i in range(4):
                self.wr.append((self.sb("wr%d" % i, [128, 4096], BF16), self.buf("wr%d" % i)))
        t, b = self.wr[self.wi % 4]
        self.wi += 1
        view = t[:, 0:kc * ncols].rearrange("p (k c) -> p k c", k=kc)
        src = w_ap[r0:r0 + nrows, c0:c0 + ncols].rearrange("(k p) c -> p k c", p=128)
        self.P.dma('pool', view, src, [], b)
        return view, b

    def ring(self, name, n, shape, dt):
        slots = [(self.sb("%s%d" % (name, i), shape, dt), self.buf("%s%d" % (name, i))) for i in range(n)]
        k = 'ring:' + name
        if k not in self.cache:
            self.cache[k] = {'i': 0}
        st = self.cache[k]

        def nxt():
            s = slots[st['i'] % n]
            st['i'] += 1
            return s
        return nxt


def layer_norm_fm(C, x32, bx, xb, bxb, g_t, b_t, bgb, ntok_tiles, t0, sq_ring, st_ring):
    P = C.P
    for tt in range(ntok_tiles):
        sl = slice(t0 + tt * TT, t0 + (tt + 1) * TT)
        s1, bs1 = C.next_ps()
        s2, bs2 = C.next_ps()
        for k in range(8):
            P.mm(lambda e, k=k, s1=s1, sl=sl: e.matmul(s1[:], C.ones32[:], x32[:, k, sl], start=(k == 0), stop=(k == 7)),
                 [bx[k][tt], C.bconst], bs1)
        for k in range(8):
            sq, bsq = sq_ring()
            P.op('act', lambda e, k=k, sq=sq, sl=sl: e.activation(out=sq[:], in_=x32[:, k, sl], func=AF.Square), [bx[k][tt]], [bsq])
            P.mm(lambda e, k=k, s2=s2, sq=sq, sl=sl: e.matmul(s2[:], C.ones32[:], sq[:], start=(k == 0), stop=(k == 7)),
                 [bsq, C.bconst], bs2)
        m, bm = st_ring()
        msq, bmsq = st_ring()
        A, bA = st_ring()
        Bc, bBc = st_ring()
        P.op('dve', lambda e, m=m, s1=s1, sl=sl: e.tensor_scalar(out=m[:], in0=s1[:], scalar1=1.0 / D, scalar2=None, op0=ALU.mult), [bs1], [bm])
        P.op('dve', lambda e, m=m, msq=msq, sl=sl: e.tensor_tensor(out=msq[:], in0=m[:], in1=m[:], op=ALU.mult), [bm], [bmsq])
        P.op('dve', lambda e, msq=msq, s2=s2, sl=sl: e.scalar_tensor_tensor(out=msq[:], in0=s2[:], scalar=1.0 / D, in1=msq[:], op0=ALU.mult, op1=ALU.subtract),
             [bs2, bmsq], [bmsq])
        P.op('act', lambda e, msq=msq, A=A, sl=sl: e.activation(out=A[:], in_=msq[:], func=AF.Ln, bias=C.eps_ln[:, 0:1], scale=1.0), [bmsq, C.bconst], [bA])
        P.op('act', lambda e, A=A, sl=sl: e.activation(out=A[:], in_=A[:], func=AF.Exp, scale=-0.5), [bA], [bA])
        P.op('dve', lambda e, m=m, A=A, Bc=Bc, sl=sl: e.scalar_tensor_tensor(out=Bc[:], in0=m[:], scalar=-1.0, in1=A[:], op0=ALU.mult, op1=ALU.mult),
             [bm, bA], [bBc])
        for k in range(8):
            u, bu = sq_ring()
            P.op('dve', lambda e, k=k, u=u, A=A, sl=sl: e.scalar_tensor_tensor(out=u[:], in0=x32[:, k, sl], scalar=g_t[:, k:k + 1], in1=A[:], op0=ALU.mult, op1=ALU.mult),
                 [bx[k][tt], bA, bgb], [bu])
            P.op('dve', lambda e, k=k, u=u, Bc=Bc, sl=sl: e.scalar_tensor_tensor(out=u[:], in0=Bc[:], scalar=g_t[:, k:k + 1], in1=u[:], op0=ALU.mult, op1=ALU.add),
                 [bBc, bu, bgb], [bu])
            P.op('act', lambda e, k=k, u=u, sl=sl: e.activation(out=x32[:, k, sl], in_=u[:], func=AF.Identity, bias=b_t[:, k:k + 1], scale=1.0),
                 [bu, bgb], [bx[k][tt]])
            P.op('act', lambda e, k=k, u=u, sl=sl: e.activation(out=xb[:, k, sl], in_=u[:], func=AF.Identity, bias=b_t[:, k:k + 1], scale=1.0),
                 [bu, bgb], [bxb[k][tt]])


def stage_F(C, T, mixT, memoT, xT_in, xT_out, w_out, ln1g, ln1b, w_up, w_down, ln2g, ln2b, bdram_in, bdram_out, dbg=None):
    P = C.P
    nc = C.nc
    TH = min(T, 1024)
    ntt = TH // TT
    x32 = C.sb("F_x32", [128, 8, TH], F32)
    xb = C.sb("F_xb", [128, 8, TH], BF16)
    cat = C.sb("F_cat", [128, 12, TH], BF16)
    h = C.sb("F_h", [128, 32, TH], BF16)
    lnp = C.sb("F_lnp", [128, 4, 8], F32)
    blnp = C.buf("lnp")
    for i, v in enumerate([ln1g, ln1b, ln2g, ln2b]):
        P.dma("sp", lnp[:, i, :], v.rearrange("(k p) -> p k", p=128), [], blnp, allow_slow_non_contiguous=True)
    sq_ring = C.ring("F_sq", 3, [128, TT], F32)
    st_ring = C.ring("F_st", 4, [128, TT], F32)
    relu_ring = C.ring("F_relu", 3, [128, TT], F32)
    bx = [[C.buf("x%d_%d" % (k, t)) for t in range(ntt)] for k in range(8)]
    bxb = [[C.buf("xb%d_%d" % (k, t)) for t in range(ntt)] for k in range(8)]
    bcat = [[C.buf("cat%d_%d" % (k, t)) for t in range(ntt)] for k in range(12)]
    bh = [[C.buf("h%d_%d" % (k, t)) for t in range(ntt)] for k in range(32)]
    for half in range(T // TH):
        t0 = half * TH
        for k in range(12):
            src = mixT[k * 128:(k + 1) * 128, t0:t0 + TH] if k < 8 else memoT[(k - 8) * 128:(k - 7) * 128, t0:t0 + TH]
            for tt in range(ntt):
                P.dma('sp', cat[:, k, tt * TT:(tt + 1) * TT], src[:, tt * TT:(tt + 1) * TT], [bdram_in], bcat[k][tt])
        for k in range(8):
            for tt in range(ntt):
                P.dma('sp', x32[:, k, tt * TT:(tt + 1) * TT], xT_in[k * 128:(k + 1) * 128, t0 + tt * TT:t0 + (tt + 1) * TT], [bdram_in], bx[k][tt])
        for og in range(4):
            wt, bw = C.load_w(w_out, 0, 1536, og * 256, 256)
            for oc in range(2):
                o = og * 2 + oc
                for tt in range(ntt):
                    sl = slice(tt * TT, (tt + 1) * TT)
                    ps, bps = C.next_ps()
                    for k in range(12):
                        MM(P, ps[:], wt[:, k, oc * 128:(oc + 1) * 128], cat[:, k, sl], k == 0, k == 11, [bw, bcat[k][tt]], bps)
                    I(P, 'dve', 'scalar_tensor_tensor', [bps, bx[o][tt]], [bx[o][tt]], out=x32[:, o, sl], in0=x32[:, o, sl], scalar=ALPHA, in1=ps[:], op0=ALU.mult, op1=ALU.add)
        def dump(name):
            if dbg and name in dbg:
                for k in range(8):
                    for tt in range(ntt):
                        P.dma('sp', dbg[name][k * 128:(k + 1) * 128, t0 + tt * TT:t0 + (tt + 1) * TT], x32[:, k, tt * TT:(tt + 1) * TT], [bx[k][tt]], bdram_out)
        dump('z1')
        layer_norm_fm(C, x32, bx, xb, bxb, lnp[:, 0, :], lnp[:, 1, :], blnp, ntt, 0, sq_ring, st_ring)
        dump('x1')
        for hg in range(8):
            wt, bw = C.load_w(w_up, 0, 1024, hg * 512, 512)
            for oc in range(4):
                hc = hg * 4 + oc
                for tt in range(ntt):
                    sl = slice(tt * TT, (tt + 1) * TT)
                    ps, bps = C.next_ps()
                    for k in range(8):
                        P.mm(lambda e, ps=ps, wt=wt, k=k, oc=oc, sl=sl: e.matmul(ps[:], wt[:, k, oc * 128:(oc + 1) * 128], xb[:, k, sl], start=(k == 0), stop=(k == 7)),
                             [bw, bxb[k][tt]], bps)
                    r, br = relu_ring()
                    I(P, 'act', 'activation', [bps], [br], out=r[:], in_=ps[:], func=AF.Relu)
                    I(P, 'dve', 'tensor_tensor', [br, bps], [bh[hc][tt]], out=h[:, hc, sl], in0=r[:], in1=ps[:], op=ALU.mult)
        for og in range(4):
            wts = [C.load_w(w_down, kh * 2048, 2048, og * 256, 256) for kh in range(2)]
            for oc in range(2):
                o = og * 2 + oc
                for tt in range(ntt):
                    sl = slice(tt * TT, (tt + 1) * TT)
                    ps, bps = C.next_ps()
                    for k in range(32):
                        wt, bw = wts[k // 16]
                        MM(P, ps[:], wt[:, k % 16, oc * 128:(oc + 1) * 128], h[:, k, sl], k == 0, k == 31, [bw, bh[k][tt]], bps)
                    I(P, 'dve', 'scalar_tensor_tensor', [bps, bx[o][tt]], [bx[o][tt]], out=x32[:, o, sl], in0=x32[:, o, sl], scalar=ALPHA, in1=ps[:], op0=ALU.mult, op1=ALU.add)
        layer_norm_fm(C, x32, bx, xb, bxb, lnp[:, 2, :], lnp[:, 3, :], blnp, ntt, 0, sq_ring, st_ring)
        for k in range(8):
            for tt in range(ntt):
                P.dma('sp', xT_out[k * 128:(k + 1) * 128, t0 + tt * TT:t0 + (tt + 1) * TT], x32[:, k, tt * TT:(tt + 1) * TT], [bx[k][tt]], bdram_out)


def I(P, eng, name, reads, writes, *args, **kw):
    return P.op(eng, lambda e, name=name, args=args, kw=kw: getattr(e, name)(*args, **kw), reads, writes)


def MM(P, out, lhsT, rhs, start, stop, reads, obuf):
    return P.mm(lambda e, out=out, lhsT=lhsT, rhs=rhs, start=start, stop=stop: e.matmul(out, lhsT, rhs, start=start, stop=stop), reads, obuf)


MEM_SCALE = 128 ** -0.5


def stage_P(C, T, kind, xT_in, memT, w_in, w_kv, prm, outs, bdram_in, bdram_out):
    P = C.P
    ntt = T // TT
    xb = C.sb("P_xb", [128, 8, T], BF16)
    memb = C.sb("P_memb", [128, 8, 256], BF16)
    kmT = C.sb("P_kmT", [128, 4, 256], BF16)
    vm = C.sb("P_vm", [128, 2, 512], BF16)
    bxb = [C.buf("Pxb%d" % k) for k in range(8)]
    bmemb, bkmT, bvm = C.buf("memb"), C.buf("kmT"), C.buf("vm")
    for k in range(8):
        P.dma('pool', xb[:, k, :], xT_in[k * 128:(k + 1) * 128, :], [bdram_in], bxb[k])
    P.dma('pool', memb[:], memT.rearrange("(k p) m -> p k m", p=128), [], bmemb)
    wt, bw = C.load_w(w_kv, 0, 1024, 0, 512)
    for hd in range(4):
        ps, bps = C.next_ps()
        for k in range(8):
            MM(P, ps[:, 0:256], wt[:, k, hd * 128:(hd + 1) * 128], memb[:, k, :], k == 0, k == 7, [bw, bmemb], bps)
        I(P, 'act', 'copy', [bps], [bkmT], out=kmT[:, hd, :], in_=ps[:, 0:256])
    wt, bw = C.load_w(w_kv, 0, 1024, 512, 512)
    for mt in range(2):
        ps, bps = C.next_ps()
        for k in range(8):
            MM(P, ps[:], memb[:, k, mt * 128:(mt + 1) * 128], wt[:, k, :], k == 0, k == 7, [bw, bmemb], bps)
        I(P, 'act', 'copy', [bps], [bvm], out=vm[:, mt, :], in_=ps[:])

    stg32 = C.ring("P_s32", 3, [128, TT], F32)
    stgb = C.ring("P_sb", 3, [128, TT], BF16)
    scr32 = C.ring("P_c32", 3, [128, TT], F32)

    def proj_fm(c0, ncols, post):
        done = 0
        while done < ncols:
            g = min(512, ncols - done)
            wt, bw = C.load_w(w_in, 0, 1024, c0 + done, g)
            for oc in range((g + 127) // 128):
                nrow = min(128, g - oc * 128)
                for tt in range(ntt):
                    ps, bps = C.next_ps()
                    for k in range(8):
                        MM(P, ps[0:nrow, :], wt[:, k, oc * 128:oc * 128 + nrow], xb[:, k, tt * TT:(tt + 1) * TT], k == 0, k == 7, [bw, bxb[k]], bps)
                    post(ps, bps, (done + oc * 128) // 128, nrow, tt)
            done += g

    def proj_tm(c0, ncols, dst):
        for g0 in range(0, ncols, 512):
            wt, bw = C.load_w(w_in, 0, 1024, c0 + g0, 512)
            for t in range(T // 128):
                ps, bps = C.next_ps()
                for k in range(8):
                    MM(P, ps[:], xb[:, k, t * 128:(t + 1) * 128], wt[:, k, :], k == 0, k == 7, [bw, bxb[k]], bps)
                s, bs = stgb()
                I(P, 'act', 'copy', [bps], [bs], out=s[:], in_=ps[:])
                P.dma('sp', dst[t * 128:(t + 1) * 128, g0:g0 + 512], s[:], [bs], bdram_out)

    def store_fm(dst, dt_ring):
        def post(ps, bps, ci, nrow, tt, func=None):
            s, bs = dt_ring()
            I(P, 'act', 'activation', [bps], [bs], out=s[0:nrow, :], in_=ps[0:nrow, :], func=(func or AF.Copy))
            P.dma('sp', dst[ci * 128:ci * 128 + nrow, tt * TT:(tt + 1) * TT], s[0:nrow, :], [bs], bdram_out)
        return post

    def post_act(dst, func, ring):
        base = store_fm(dst, ring)
        return lambda ps, bps, ci, nrow, tt: base(ps, bps, ci, nrow, tt, func=func)

    gp = C.sb("P_gp", [16, 4], F32)
    bgp = C.buf("gp")

    def gates_post(spec):
        def post(ps, bps, ci, nrow, tt):
            spec(ps, bps, tt)
        return post

    if kind == 'gdn':
        proj_fm(0, 3072, store_fm(outs['qkvT'], stg32))
        proj_fm(3072, 1024, post_act(outs['szT'], AF.Silu, stg32))
        P.dma('sp', gp[0:8, 0:1], prm['a_log'].rearrange("(h o) -> h o", o=1), [], bgp)
        P.dma('sp', gp[0:8, 1:2], prm['dt_bias'].rearrange("(h o) -> h o", o=1), [], bgp)
        I(P, 'act', 'activation', [bgp], [bgp], out=gp[0:8, 2:3], in_=gp[0:8, 0:1], func=AF.Exp)
        I(P, 'dve', 'tensor_scalar', [bgp], [bgp], out=gp[0:8, 2:3], in0=gp[0:8, 2:3], scalar1=-1.0, scalar2=None, op0=ALU.mult)

        def gspec(ps, bps, tt):
            s, bs = scr32()
            I(P, 'act', 'activation', [bps, bgp], [bs], out=s[0:8, :], in_=ps[0:8, :], func=AF.Exp, bias=gp[0:8, 1:2], scale=1.0)
            I(P, 'act', 'activation', [bs], [bs], out=s[0:8, :], in_=s[0:8, :], func=AF.Ln, bias=1.0, scale=1.0)
            I(P, 'dve', 'tensor_scalar', [bs, bgp], [bs], out=s[0:8, :], in0=s[0:8, :], scalar1=gp[0:8, 2:3], scalar2=None, op0=ALU.mult)
            P.dma('sp', outs['gT'][:, tt * TT:(tt + 1) * TT], s[0:8, :], [bs], bdram_out)
            s2, bs2 = scr32()
            I(P, 'act', 'activation', [bps], [bs2], out=s2[0:16, :], in_=ps[0:16, :], func=AF.Sigmoid)
            P.dma('sp', outs['betaT'][:, tt * TT:(tt + 1) * TT], s2[8:16, :], [bs2], bdram_out)
        proj_fm(4096, 16, gates_post(gspec))
        memq0 = 4112
    elif kind == 'mlstm':
        proj_fm(0, 512, store_fm(outs['qT'], stg32))
        proj_fm(512, 512, store_fm(outs['kT'], stg32))
        proj_tm(1024, 1024, outs['v'])
        proj_fm(2048, 1024, post_act(outs['sogT'], AF.Sigmoid, stg32))
        P.dma('sp', gp[0:8, 0:1], prm['b_gate'][0, :].rearrange("(h o) -> h o", o=1), [], bgp)
        P.dma('sp', gp[8:16, 0:1], prm['b_gate'][1, :].rearrange("(h o) -> h o", o=1), [], bgp)
        I(P, 'dve', 'tensor_scalar', [bgp], [bgp], out=gp[0:16, 1:2], in0=gp[0:16, 0:1], scalar1=-1.0, scalar2=None, op0=ALU.mult)

        def gspec(ps, bps, tt):
            s, bs = scr32()
            I(P, 'act', 'activation', [bps, bgp], [bs], out=s[0:16, :], in_=ps[0:16, :], func=AF.Identity, bias=gp[0:16, 0:1], scale=1.0)
            P.dma('sp', outs['ilogT'][:, tt * TT:(tt + 1) * TT], s[0:8, :], [bs], bdram_out)
            s2, bs2 = scr32()
            I(P, 'act', 'activation', [bps, bgp], [bs2], out=s2[0:16, :], in_=ps[0:16, :], func=AF.Exp, bias=gp[0:16, 1:2], scale=-1.0)
            I(P, 'act', 'activation', [bs2], [bs2], out=s2[0:16, :], in_=s2[0:16, :], func=AF.Ln, bias=1.0, scale=1.0)
            I(P, 'dve', 'tensor_scalar', [bs2], [bs2], out=s2[0:16, :], in0=s2[0:16, :], scalar1=-1.0, scalar2=None, op0=ALU.mult)
            P.dma('sp', outs['flogT'][:, tt * TT:(tt + 1) * TT], s2[8:16, :], [bs2], bdram_out)
        proj_fm(3072, 16, gates_post(gspec))
        memq0 = 3088
    else:
        blk = C.sb("P_blk", [128, 128], F32)
        qkg = C.sb("P_qkg", [128, 2], F32)
        bblk = C.buf("blk")
        I(P, 'pool', 'memset', [], [bblk], blk[:], 0.0)
        I(P, 'pool', 'memset', [bblk], [bblk], blk[0:64, 0:64], 1.0 / 64)
        I(P, 'pool', 'memset', [bblk], [bblk], blk[64:128, 64:128], 1.0 / 64)
        for j in range(2):
            for hh in range(2):
                P.dma('sp', qkg[hh * 64:(hh + 1) * 64, j:j + 1], prm['qk_g'][j, :].rearrange("(d o) -> d o", o=1), [], bblk)
        I(P, 'dve', 'tensor_scalar', [bblk], [bblk], out=qkg[:, 0:1], in0=qkg[:, 0:1], scalar1=64 ** -0.5, scalar2=None, op0=ALU.mult)

        def rms_post(dst, j):
            def post(ps, bps, ci, nrow, tt):
                raw, braw = scr32()
                sq, bsq = scr32()
                I(P, 'act', 'copy', [bps], [braw], out=raw[:], in_=ps[:])
                I(P, 'act', 'activation', [bps], [bsq], out=sq[:], in_=ps[:], func=AF.Square)
                ps2, bps2 = C.next_ps()
                MM(P, ps2[:], blk[:], sq[:], True, True, [bblk, bsq], bps2)
                I(P, 'act', 'activation', [bps2, C.bconst], [bsq], out=sq[:], in_=ps2[:], func=AF.Ln, bias=C.eps_n[:, 0:1], scale=1.0)
                I(P, 'act', 'activation', [bsq], [bsq], out=sq[:], in_=sq[:], func=AF.Exp, scale=-0.5)
                s, bs = stgb()
                I(P, 'dve', 'scalar_tensor_tensor', [braw, bsq, bblk], [bs], out=s[:], in0=raw[:], scalar=qkg[:, j:j + 1], in1=sq[:], op0=ALU.mult, op1=ALU.mult)
                P.dma('sp', dst[ci * 128:(ci + 1) * 128, tt * TT:(tt + 1) * TT], s[:], [bs], bdram_out)
            return post
        proj_fm(0, 1024, rms_post(outs['qT'], 0))
        proj_fm(1024, 1024, rms_post(outs['kT'], 1))
        proj_tm(2048, 1024, outs['v'])
        proj_fm(3072, 1024, post_act(outs['sogT'], AF.Sigmoid, stg32))
        P.dma('sp', gp[0:16, 0:1], prm['b_f'].rearrange("(h o) -> h o", o=1), [], bgp)
        I(P, 'dve', 'tensor_scalar', [bgp], [bgp], out=gp[0:16, 1:2], in0=gp[0:16, 0:1], scalar1=-1.0, scalar2=None, op0=ALU.mult)

        def gspec(ps, bps, tt):
            s2, bs2 = scr32()
            I(P, 'act', 'activation', [bps, bgp], [bs2], out=s2[0:16, :], in_=ps[0:16, :], func=AF.Exp, bias=gp[0:16, 1:2], scale=-1.0)
            I(P, 'act', 'activation', [bs2], [bs2], out=s2[0:16, :], in_=s2[0:16, :], func=AF.Ln, bias=1.0, scale=1.0)
            I(P, 'dve', 'tensor_scalar', [bs2], [bs2], out=s2[0:16, :], in0=s2[0:16, :], scalar1=-1.0, scalar2=None, op0=ALU.mult)
            P.dma('sp', outs['flogT'][:, tt * TT:(tt + 1) * TT], s2[0:16, :], [bs2], bdram_out)
        proj_fm(4096, 16, gates_post(gspec))
        memq0 = 4112

    pT = C.ring("P_pT", 2, [128, 2, TT], BF16)

    for hd in range(4):
        wt, bw = C.load_w(w_in, 0, 1024, memq0 + hd * 128, 128)
        for tt in range(ntt):
            ps, bps = C.next_ps()
            for k in range(8):
                MM(P, ps[:], wt[:, k, :], xb[:, k, tt * TT:(tt + 1) * TT], k == 0, k == 7, [bw, bxb[k]], bps)
            q, bq = stgb()
            I(P, 'act', 'copy', [bps], [bq], out=q[:], in_=ps[:])
            p, bp = pT()
            for mt in range(2):
                ps2, bps2 = C.next_ps()
                MM(P, ps2[:], kmT[:, hd, mt * 128:(mt + 1) * 128], q[:], True, True, [bkmT, bq], bps2)
                I(P, 'act', 'activation', [bps2], [bp], out=p[:, mt, :], in_=ps2[:], func=AF.Exp, scale=MEM_SCALE)
            po, bpo = C.next_ps()
            pd, bpd = C.next_ps()
            for mt in range(2):
                MM(P, po[:], vm[:, mt, hd * 128:(hd + 1) * 128], p[:, mt, :], mt == 0, mt == 1, [bvm, bp], bpo)
            for mt in range(2):
                MM(P, pd[:], C.onesb[:], p[:, mt, :], mt == 0, mt == 1, [C.bconst, bp], bpd)
            r, br = scr32()
            I(P, 'dve', 'reciprocal', [bpd], [br], out=r[:], in_=pd[:])
            s, bs = stgb()
            I(P, 'dve', 'tensor_tensor', [bpo, br], [bs], out=s[:], in0=po[:], in1=r[:], op=ALU.mult)
            P.dma('sp', outs['memoT'][hd * 128:(hd + 1) * 128, tt * TT:(tt + 1) * TT], s[:], [bs], bdram_out)


def consts_attn(C):
    P = C.P
    if 'tri01' in C.pcache:
        return
    tri = C.sbp("tri01", [128, 128], BF16)
    onesrow = C.sbp("onesrow", [1, 1024], F32)
    negone = C.sbp("negone", [1, 1], F32)
    b = C.bconst
    I(P, 'pool', 'memset', [], [b], tri[:], 1.0)
    I(P, 'pool', 'affine_select', [b], [b], out=tri[:], in_=tri[:], pattern=[[1, 128]], compare_op=ALU.is_ge, fill=0.0, base=0, channel_multiplier=-1)
    I(P, 'pool', 'memset', [], [b], onesrow[:], 1.0)
    I(P, 'pool', 'memset', [], [b], negone[:], -1.0)
    C.tri01, C.onesrow, C.negone = tri, onesrow, negone


def cumsum_row(C, dst_row, src_row, n, rb, wb):
    I(C.P, 'dve', 'tensor_tensor_scan', rb + [C.bconst], wb, out=dst_row, data0=C.onesrow[0:1, 0:n], data1=src_row, initial=0.0, op0=ALU.mult, op1=ALU.add)


def cumsum_dram_row(C, src, S, crow, bcrow, stg_r, bdram_in):
    P = C.P
    PC = min(1024, S)
    for pc in range(S // PC):
        sg, bsg = stg_r()
        P.dma('sp', sg[0:1, 0:PC], src[:, pc * PC:(pc + 1) * PC], [bdram_in], bsg)
        init = 0.0 if pc == 0 else crow[0:1, pc * PC - 1:pc * PC]
        I(P, 'dve', 'tensor_tensor_scan', [bsg, bcrow, C.bconst], [bcrow], out=crow[0:1, pc * PC:(pc + 1) * PC], data0=C.onesrow[0:1, 0:PC], data1=sg[0:1, 0:PC],
          initial=init, op0=ALU.mult, op1=ALU.add)


def attn_loop(C, S, KA, bKA, QA, bQA, kdim, VA, bVA, vcols, negc, E, bside, nacc, den_ones, epilogue, LA=2, ED=3):
    P = C.P
    nq = S // 512
    pT = C.ring("A_pT", 4, [128, 512], BF16)
    C.set_pool('score', [0, 1, 2])
    blocks = []
    for qi in range(nq):
        nk = 4 * qi + 4
        for kt in range(nk):
            blocks.append((qi, kt, nk))
    sc = {}

    def rec_score(n):
        qi, kt, nk = blocks[n]
        j = kt - 4 * qi
        c0 = 128 * j if j > 0 else 0
        ps, bps = C.next_ps('score')
        MM(P, ps[:, c0:512], KA[0:kdim, kt * 128:(kt + 1) * 128], QA[0:kdim, qi * 512 + c0:(qi + 1) * 512], True, True, [bKA, bQA], bps)
        sc[n] = (ps, bps, c0, j)

    pend = []
    for n in range(min(LA, len(blocks))):
        rec_score(n)
    for n in range(len(blocks)):
        if n + LA < len(blocks):
            rec_score(n + LA)
        qi, kt, nk = blocks[n]
        ps, bps, c0, j = sc.pop(n)
        accs = [C.acc((qi % 2) * nacc + i) for i in range(nacc)]
        p, bp = pT()
        if negc is not None:
            I(P, 'act', 'activation', [bps, bside], [bp], out=p[:, c0:512], in_=ps[:, c0:512], func=AF.Exp, bias=negc[:, kt:kt + 1], scale=1.0)
        else:
            I(P, 'act', 'activation', [bps, bside], [bp], out=p[:, c0:512], in_=ps[:, c0:512], func=AF.Copy, scale=E[:, qi * (S // 128) + kt:qi * (S // 128) + kt + 1])
        if j >= 0:
            I(P, 'pool', 'tensor_tensor', [bp, C.bconst], [bp], out=p[:, c0:c0 + 128], in0=p[:, c0:c0 + 128], in1=C.tri01[:], op=ALU.mult)
        MM(P, accs[0][0][0:vcols, c0:512], VA[:, kt, :], p[:, c0:512], kt == 0, kt == nk - 1, [bVA, bp], accs[0][1])
        if den_ones:
            MM(P, accs[1][0][:, c0:512], C.onesb[:], p[:, c0:512], kt == 0, kt == nk - 1, [C.bconst, bp], accs[1][1])
        for e in pend:
            e[0] -= 1
        while pend and pend[0][0] <= 0:
            g = pend.pop(0)[1]
            for _ in g:
                pass
        if kt == nk - 1:
            g = epilogue(qi, accs)
            next(g)
            pend.append([ED, g])
    for e in pend:
        for _ in e[1]:
            pass


def mixer_fox(C, S, nh, qT, kT, v, sogT, flogT, mixT, bdram_in, bdram_out):
    P = C.P
    consts_attn(C)
    C.nring = 4
    C.set_pool('misc', [3, 6, 7])
    KA = [C.sb("X_KA%d" % i, [65, S], BF16) for i in range(2)]
    QA = [C.sb("X_QA%d" % i, [65, S], BF16) for i in range(2)]
    VA = [C.sb("X_VA%d" % i, [128, S // 128, 65], BF16) for i in range(2)]
    crow = C.sb("X_crow", [1, S], F32)
    cb = C.sb("X_cb", [1, S], BF16)
    negc = [C.sb("X_negc%d" % i, [128, S // 128], F32) for i in range(2)]
    bKA = [C.buf("X_KA%d" % i) for i in range(2)]
    bQA = [C.buf("X_QA%d" % i) for i in range(2)]
    bVA = [C.buf("X_VA%d" % i) for i in range(2)]
    bcrow = C.buf("X_crow")
    bneg = [C.buf("X_negc%d" % i) for i in range(2)]
    stg_r = C.ring("X_stg", 2, [1, 1024], F32)
    sog_r = C.ring("X_sog", 3, [64, 512], F32)
    o_r = C.ring("X_o", 3, [65, 512], F32)
    ob_r = C.ring("X_ob", 2, [64, 512], BF16)
    for h in range(nh):
        i = h % 2
        P.dma('sp', KA[i][0:64, :], kT[h * 64:(h + 1) * 64, :], [bdram_in], bKA[i])
        I(P, 'pool', 'memset', [bKA[i]], [bKA[i]], KA[i][64:65, :], 1.0)
        P.dma('sp', QA[i][0:64, :], qT[h * 64:(h + 1) * 64, :], [bdram_in], bQA[i])
        P.dma('sp', VA[i][:, :, 0:64], v[:, h * 64:(h + 1) * 64].rearrange("(t p) d -> p t d", p=128), [bdram_in], bVA[i])
        I(P, 'pool', 'memset', [bVA[i]], [bVA[i]], VA[i][:, :, 64:65], 1.0)
        cumsum_dram_row(C, flogT[h:h + 1, :], S, crow, bcrow, stg_r, bdram_in)
        I(P, 'dve', 'tensor_copy', [bcrow], [bcrow], out=cb[:], in_=crow[:])
        P.dma('sp', QA[i][64:65, :], cb[:], [bcrow], bQA[i])
        ps, bps = C.next_ps()
        for t in range(S // 128):
            MM(P, ps[:, t:t + 1], crow[0:1, t * 128:(t + 1) * 128], C.negone[0:1, 0:1], True, True, [bcrow, C.bconst], bps)
        I(P, 'dve', 'tensor_copy', [bps], [bneg[i]], out=negc[i][:], in_=ps[:, 0:S // 128])

        def epi(qi, accs, h=h, i=i):
            po, bpo = accs[0]
            o, bo = o_r()
            I(P, 'dve', 'reciprocal', [bpo], [bo], out=o[64:65, :], in_=po[64:65, :])
            I(P, 'act', 'copy', [bpo], [bo], out=o[0:64, :], in_=po[0:64, :])
            sg, bsg = sog_r()
            P.dma('sp', sg[:], sogT[h * 64:(h + 1) * 64, qi * 512:(qi + 1) * 512], [bdram_in], bsg)
            yield
            pb, bpb = C.next_ps('misc')
            MM(P, pb[0:64, :], C.ones32[64:65, 0:64], o[64:65, :], True, True, [C.bconst, bo], bpb)
            I(P, 'dve', 'tensor_tensor', [bo, bpb], [bo], out=o[0:64, :], in0=o[0:64, :], in1=pb[0:64, :], op=ALU.mult)
            ob, bob = ob_r()
            I(P, 'dve', 'tensor_tensor', [bo, bsg], [bob], out=ob[:], in0=o[0:64, :], in1=sg[:], op=ALU.mult)
            P.dma('sp', mixT[h * 64:(h + 1) * 64, qi * 512:(qi + 1) * 512], ob[:], [bob], bdram_out)
            yield
        attn_loop(C, S, KA[i], bKA[i], QA[i], bQA[i], 65, VA[i], bVA[i], 65, negc[i], None, bneg[i], 1, False, epi)
    C.nring = 8


def mixer_mlstm(C, S, nh, qT, kT, v, sogT, ilogT, flogT, ng, mixT, bdram_in, bdram_out):
    P = C.P
    consts_attn(C)
    C.nring = 4
    C.set_pool('misc', [3])
    nq, nkb = S // 512, S // 128
    KA = [C.sb("X_KA%d" % i, [65, S], BF16) for i in range(2)]
    QA = [C.sb("X_QA%d" % i, [65, S], BF16) for i in range(2)]
    VA = [C.sb("M_VA%d" % i, [128, nkb, 128], BF16) for i in range(2)]
    Brow = C.sb("M_Brow", [1, S], F32)
    xrow = C.sb("M_xrow", [1, 1024], F32)
    bBrow = C.buf("M_Brow")
    stg_r = C.ring("M_stg", 2, [1, 1024], F32)
    f_r = C.ring("M_f", 4, [1, 512], F32)
    refs = [C.sb("M_refs%d" % i, [1, 128], F32) for i in range(2)]
    E = [C.sb("M_E%d" % i, [128, 16 * 64], F32) for i in range(2)]
    gcol = C.sb("M_g", [128, 8], F32)
    avg = C.sb("M_avg", [128, 128], F32)
    bKA = [C.buf("X_KA%d" % i) for i in range(2)]
    bQA = [C.buf("X_QA%d" % i) for i in range(2)]
    bVA = [C.buf("M_VA%d" % i) for i in range(2)]
    bE = [C.buf("M_E%d" % i) for i in range(2)]
    bg = C.buf("M_g")
    I(P, 'pool', 'memset', [], [bg], avg[:], 1.0 / 128)
    P.dma('sp', gcol[:, 0:nh], ng.rearrange("(h d) -> d h", d=128), [], bg, allow_slow_non_contiguous=True)
    ld_r = C.ring("M_ld", 2, [64, 512], F32)
    sog_r = C.ring("M_sog", 3, [128, 512], F32)
    w_r = C.ring("M_w", 5, [128, 512], F32)
    ob_r = C.ring("M_ob", 2, [128, 512], BF16)
    for h in range(nh):
        i = h % 2
        cumsum_dram_row(C, flogT[h:h + 1, :], S, Brow, bBrow, stg_r, bdram_in)
        B3q = Brow[0:1, :].rearrange("p (t c) -> p t c", c=512)
        B3k = Brow[0:1, :].rearrange("p (t c) -> p t c", c=128)
        brf = bE[i]
        I(P, 'dve', 'tensor_copy', [bBrow], [brf], out=refs[i][0:1, 0:nq], in_=B3q[:, :, 0])
        I(P, 'dve', 'tensor_copy', [bBrow], [brf], out=refs[i][0:1, 64:64 + nkb], in_=B3k[:, :, 0])
        X3 = xrow[0:1, 0:nq * nkb].rearrange("p (a b) -> p a b", b=nkb)
        I(P, 'dve', 'tensor_tensor', [brf], [bBrow], out=X3, in0=refs[i][0:1, 0:nq].unsqueeze(2).to_broadcast([1, nq, nkb]),
          in1=refs[i][0:1, 64:64 + nkb].unsqueeze(1).to_broadcast([1, nq, nkb]), op=ALU.subtract)
        I(P, 'dve', 'tensor_scalar', [bBrow], [bBrow], out=xrow[0:1, 0:nq * nkb], in0=xrow[0:1, 0:nq * nkb], scalar1=60.0, scalar2=None, op0=ALU.min)
        for c in range((nq * nkb + 511) // 512):
            n = min(512, nq * nkb - c * 512)
            ps, bps = C.next_ps()
            MM(P, ps[:, 0:n], C.ones32[0:1, :], xrow[0:1, c * 512:c * 512 + n], True, True, [C.bconst, bBrow], bps)
            I(P, 'act', 'activation', [bps], [bE[i]], out=E[i][:, c * 512:c * 512 + n], in_=ps[:, 0:n], func=AF.Exp)
        for c in range(S // 512):
            fq, bfq = f_r()
            I(P, 'dve', 'tensor_scalar', [bBrow, brf], [bfq], out=fq[:], in0=Brow[0:1, c * 512:(c + 1) * 512], scalar1=refs[i][0:1, c:c + 1], scalar2=None, op0=ALU.subtract)
            I(P, 'act', 'activation', [bfq], [bfq], out=fq[:], in_=fq[:], func=AF.Exp)
            fk, bfk = f_r()
            P.dma('sp', fk[:], ilogT[h:h + 1, c * 512:(c + 1) * 512], [bdram_in], bfk)
            I(P, 'dve', 'tensor_tensor', [bfk, bBrow], [bfk], out=fk[:], in0=fk[:], in1=Brow[0:1, c * 512:(c + 1) * 512], op=ALU.subtract)
            I(P, 'dve', 'tensor_tensor', [bfk, brf], [bfk], out=fk[:].rearrange("p (t c) -> p t c", c=128), in0=fk[:].rearrange("p (t c) -> p t c", c=128),
              in1=refs[i][0:1, 64 + c * 4:64 + c * 4 + 4].unsqueeze(2).to_broadcast([1, 4, 128]), op=ALU.add)
            I(P, 'act', 'activation', [bfk], [bfk], out=fk[:], in_=fk[:], func=AF.Exp)
            for (src, dstA, bdst, frow, bfrow, sc) in ((qT, QA[i], bQA[i], fq, bfq, 0.125), (kT, KA[i], bKA[i], fk, bfk, 1.0)):
                ld, bld = ld_r()
                P.dma('sp', ld[:], src[h * 64:(h + 1) * 64, c * 512:(c + 1) * 512], [bdram_in], bld)
                ps, bps = C.next_ps()
                MM(P, ps[0:64, :], C.ones32[0:1, 0:64], frow[:], True, True, [C.bconst, bfrow], bps)
                I(P, 'dve', 'scalar_tensor_tensor', [bld, bps], [bdst], out=dstA[0:64, c * 512:(c + 1) * 512], in0=ld[:], scalar=sc, in1=ps[0:64, :], op0=ALU.mult, op1=ALU.mult)
        P.dma('sp', VA[i][:], v[:, h * 128:(h + 1) * 128].rearrange("(t p) d -> p t d", p=128), [bdram_in], bVA[i])

        def epi(qi, accs, h=h, i=i):
            (pn, bpn), (pd, bpd) = accs
            r, br = w_r()
            I(P, 'act', 'activation', [bpd], [br], out=r[:], in_=pd[:], func=AF.Abs)
            I(P, 'dve', 'tensor_scalar', [br], [br], out=r[:], in0=r[:], scalar1=1.0, scalar2=None, op0=ALU.max)
            I(P, 'dve', 'reciprocal', [br], [br], out=r[:], in_=r[:])
            hh, bh = w_r()
            I(P, 'dve', 'tensor_tensor', [bpn, br], [bh], out=hh[:], in0=pn[:], in1=r[:], op=ALU.mult)
            sq, bsq = w_r()
            I(P, 'act', 'activation', [bh], [bsq], out=sq[:], in_=hh[:], func=AF.Square)
            sg, bsg = sog_r()
            P.dma('sp', sg[:], sogT[h * 128:(h + 1) * 128, qi * 512:(qi + 1) * 512], [bdram_in], bsg)
            yield
            pm, bpm = C.next_ps('misc')
            MM(P, pm[:, 0:256], avg[:], hh[:, 0:256], True, True, [bg, bh], bpm)
            MM(P, pm[:, 256:512], avg[:], hh[:, 256:512], True, True, [bg, bh], bpm)
            mean, bmean = w_r()
            I(P, 'act', 'copy', [bpm], [bmean], out=mean[:], in_=pm[:])
            pv, bpv = C.next_ps('misc')
            MM(P, pv[:], avg[:], sq[:], True, True, [bg, bsq], bpv)
            I(P, 'act', 'activation', [bmean], [br], out=r[:], in_=mean[:], func=AF.Square)
            I(P, 'dve', 'tensor_tensor', [bpv, br], [br], out=r[:], in0=pv[:], in1=r[:], op=ALU.subtract)
            I(P, 'act', 'activation', [br, C.bconst], [br], out=r[:], in_=r[:], func=AF.Ln, bias=C.eps_n[:, 0:1], scale=1.0)
            I(P, 'act', 'activation', [br], [br], out=r[:], in_=r[:], func=AF.Exp, scale=-0.5)
            I(P, 'dve', 'tensor_tensor', [bh, bmean], [bh], out=hh[:], in0=hh[:], in1=mean[:], op=ALU.subtract)
            I(P, 'dve', 'scalar_tensor_tensor', [bh, br, bg], [bh], out=hh[:], in0=hh[:], scalar=gcol[:, h:h + 1], in1=r[:], op0=ALU.mult, op1=ALU.mult)
            ob, bob = ob_r()
            I(P, 'dve', 'tensor_tensor', [bh, bsg], [bob], out=ob[:], in0=hh[:], in1=sg[:], op=ALU.mult)
            P.dma('sp', mixT[h * 128:(h + 1) * 128, qi * 512:(qi + 1) * 512], ob[:], [bob], bdram_out)
            yield
        attn_loop(C, S, KA[i], bKA[i], QA[i], bQA[i], 64, VA[i], bVA[i], 128, None, E[i], bE[i], 2, True, epi)
    C.nring = 8


GC = 64
GB = 8


def consts_gdn(C):
    P = C.P
    if 'g_tri' in C.pcache:
        return
    b = C.bconst
    tri = C.sbp("g_tri", [64, 64], F32)
    idn = C.sbp("g_idn", [64, 64], F32)
    off = C.sbp("g_off", [64, 64], F32)
    mS = C.sbp("g_mS", [64, GB, 64], F32)
    mI = C.sbp("g_mI", [64, GB, 64], F32)
    idb = C.sbp("g_idb", [128, 128], BF16)
    offb = C.sbp("g_offb", [64, 64], BF16)
    mSb = C.sbp("g_mSb", [64, GB, 64], BF16)
    mIb = C.sbp("g_mIb", [64, GB, 64], BF16)
    avgb = C.sbp("g_avgb", [128, 128], BF16)
    avg = C.sbp("g_avg", [128, 128], F32)
    for t, op in ((tri, ALU.is_ge), (idn, ALU.is_equal), (off, ALU.not_equal)):
        I(P, 'pool', 'memset', [], [b], t[:], 1.0)
        I(P, 'pool', 'affine_select', [b], [b], out=t[:], in_=t[:], pattern=[[1, 64]], compare_op=op, fill=0.0, base=0, channel_multiplier=-1)
    I(P, 'pool', 'memset', [], [b], mS[:], 0.0)
    I(P, 'pool', 'affine_select', [b], [b], out=mS[:], in_=mS[:], pattern=[[0, GB], [-1, 64]], compare_op=ALU.is_gt, fill=-1.0e4, base=0, channel_multiplier=1)
    I(P, 'pool', 'memset', [], [b], mI[:], 0.0)
    I(P, 'pool', 'affine_select', [b], [b], out=mI[:], in_=mI[:], pattern=[[0, GB], [1, 64]], compare_op=ALU.is_ge, fill=-1.0e4, base=0, channel_multiplier=-1)
    I(P, 'pool', 'memset', [], [b], idb[:], 1.0)
    I(P, 'pool', 'affine_select', [b], [b], out=idb[:], in_=idb[:], pattern=[[1, 128]], compare_op=ALU.is_equal, fill=0.0, base=0, channel_multiplier=-1)
    I(P, 'pool', 'memset', [], [b], avg[:], 1.0 / 128)
    I(P, 'pool', 'tensor_copy', [b], [b], out=offb[:], in_=off[:])
    I(P, 'pool', 'tensor_copy', [b], [b], out=mSb[:], in_=mS[:])
    I(P, 'pool', 'tensor_copy', [b], [b], out=mIb[:], in_=mI[:])
    I(P, 'pool', 'memset', [], [b], avgb[:], 1.0 / 128)
    C.g_tri, C.g_idn, C.g_off, C.g_mS, C.g_mI, C.g_idb, C.g_avg = tri, idn, off, mS, mI, idb, avg
    C.g_offb, C.g_mSb, C.g_mIb, C.g_avgb = offb, mSb, mIb, avgb


def mixer_gdn(C, S, heads, qkvT, conv_w, szT, gT, betaT, norm_g, mixT, bdram_in, bdram_out):
    P = C.P
    consts_gdn(C)
    C.nring = 4
    NCH = S // GC
    NB = S // (GC * GB)
    nh = len(heads)
    CW = min(1024, S)
    ngl = C.sb("G_ng", [128, 1], F32)
    bng = C.buf("G_ng")
    P.dma('sp', ngl[:], norm_g.rearrange("(d o) -> d o", o=1), [], bng)
    cst = C.ring("G_cst", 2, [128, CW + 3], F32)
    cac = C.ring("G_cac", 2, [128, CW], F32)
    csq = C.ring("G_csq", 1, [128, 512], F32)
    grow = C.ring("G_grow", 1, [1, 512], F32)
    HT = []
    for hi in range(2):
        d = {}
        for nm in ('qb', 'kb', 'vb'):
            d[nm] = C.sb("G_%s%d" % (nm, hi), [128, S], BF16)
            d['b' + nm] = C.buf("G_%s%d" % (nm, hi))
        d['cw'] = C.sb("G_cw%d" % hi, [128, 3, 4], F32)
        d['tm'] = C.sb("G_tm%d" % hi, [64, 8, NCH], F32)
        d['dl'] = C.sb("G_dl%d" % hi, [128, NCH], F32)
        d['S32'] = C.sb("G_S32_%d" % hi, [128, 128], F32)
        d['Sb'] = C.sb("G_Sb_%d" % hi, [128, 128], BF16)
        for nm in ('cw', 'tm', 'dl', 'S32', 'Sb'):
            d['b' + nm] = C.buf("G_%s%d" % (nm, hi))
        HT.append(d)

    def phaseA(hidx):
        hi = hidx % 2
        (q0, k0, v0, gr, o0) = heads[hidx]
        d = HT[hi]
        for xi, r0 in enumerate((q0, k0, v0)):
            P.dma('sp', d['cw'][:, xi, :], conv_w[:, r0:r0 + 128].rearrange("j c -> c j"), [], d['bcw'], allow_slow_non_contiguous=True)
        for xi, (r0, nm) in enumerate(((q0, 'qb'), (k0, 'kb'), (v0, 'vb'))):
            dst, bdst = d[nm], d['b' + nm]
            for cc in range(S // CW):
                st, bst = cst()
                if cc == 0:
                    I(P, 'pool', 'memset', [bst], [bst], st[:, 0:3], 0.0)
                    P.dma('sp', st[:, 3:3 + CW], qkvT[r0:r0 + 128, 0:CW], [bdram_in], bst)
                else:
                    P.dma('sp', st[:, 0:3 + CW], qkvT[r0:r0 + 128, cc * CW - 3:(cc + 1) * CW], [bdram_in], bst)
                ac, bac = cac()
                I(P, 'dve', 'tensor_scalar', [bst, d['bcw']], [bac], out=ac[:], in0=st[:, 3:3 + CW], scalar1=d['cw'][:, xi, 3:4], scalar2=None, op0=ALU.mult)
                for j in range(3):
                    I(P, 'dve', 'scalar_tensor_tensor', [bst, bac, d['bcw']], [bac], out=ac[:], in0=st[:, j:j + CW], scalar=d['cw'][:, xi, j:j + 1], in1=ac[:], op0=ALU.mult, op1=ALU.add)
                I(P, 'act', 'activation', [bac], [bac], out=ac[:], in_=ac[:], func=AF.Silu)
                if nm == 'vb':
                    I(P, 'act', 'copy', [bac], [bdst], out=dst[:, cc * CW:(cc + 1) * CW], in_=ac[:])
                else:
                    for t in range(CW // 512):
                        sq, bsq = csq()
                        I(P, 'act', 'activation', [bac], [bsq], out=sq[:], in_=ac[:, t * 512:(t + 1) * 512], func=AF.Square)
                        ps, bps = C.next_ps()
                        MM(P, ps[:], C.ones32[:], sq[:], True, True, [C.bconst, bsq], bps)
                        I(P, 'act', 'activation', [bps, C.bconst], [bsq], out=sq[:], in_=ps[:], func=AF.Ln, bias=C.eps_n[:, 0:1], scale=1.0)
                        I(P, 'act', 'activation', [bsq], [bsq], out=sq[:], in_=sq[:], func=AF.Exp, scale=-0.5)
                        sc = (128 ** -0.5) if nm == 'qb' else 1.0
                        I(P, 'dve', 'scalar_tensor_tensor', [bac, bsq], [bdst], out=dst[:, cc * CW + t * 512:cc * CW + (t + 1) * 512], in0=ac[:, t * 512:(t + 1) * 512], scalar=sc, in1=sq[:], op0=ALU.mult, op1=ALU.mult)
                yield
        tm = d['tm']
        PC = 512
        for w, src in enumerate((gT, betaT)):
            ps, bps = C.next_ps()
            for pc in range(S // PC):
                sg, bsg = grow()
                P.dma('sp', sg[0:1, 0:PC], src[gr:gr + 1, pc * PC:(pc + 1) * PC], [bdram_in], bsg)
                for nn in range(PC // 64):
                    col = pc * (PC // 64) + nn
                    MM(P, ps[0:64, col:col + 1], sg[0:1, nn * 64:(nn + 1) * 64], C.ones32[0:1, 0:1], True, True, [bsg, C.bconst], bps)
            I(P, 'dve', 'tensor_copy', [bps], [d['btm']], out=tm[:, w, :], in_=ps[0:64, 0:NCH])
        ps, bps = C.next_ps()
        MM(P, ps[0:64, 0:NCH], C.g_tri[:], tm[:, 0, :], True, True, [C.bconst, d['btm']], bps)
        I(P, 'dve', 'tensor_copy', [bps], [d['btm']], out=tm[:, 2, :], in_=ps[0:64, 0:NCH])
        ps2, bps2 = C.next_ps()
        MM(P, ps2[:, 0:NCH], C.ones32[0:64, :], tm[:, 0, :], True, True, [C.bconst, d['btm']], bps2)
        I(P, 'act', 'activation', [bps2], [d['bdl']], out=d['dl'][:], in_=ps2[:, 0:NCH], func=AF.Exp)
        I(P, 'act', 'activation', [d['btm']], [d['btm']], out=tm[:, 3, :], in_=tm[:, 2, :], func=AF.Exp)
        I(P, 'dve', 'tensor_scalar', [d['btm']], [d['btm']], out=tm[:, 5, :], in0=tm[:, 1, :], scalar1=-1.0, scalar2=None, op0=ALU.mult)
        I(P, 'dve', 'tensor_tensor', [d['btm']], [d['btm']], out=tm[:, 7, :], in0=tm[:, 3, :], in1=tm[:, 5, :], op=ALU.mult)
        I(P, 'dve', 'tensor_tensor', [d['btm'], bps2], [d['btm']], out=tm[:, 4, :], in0=ps2[0:64, 0:NCH], in1=tm[:, 2, :], op=ALU.subtract)
        I(P, 'act', 'activation', [d['btm']], [d['btm']], out=tm[:, 4, :], in_=tm[:, 4, :], func=AF.Exp)
        I(P, 'pool', 'memset', [], [d['bS32']], d['S32'][:], 0.0)
        yield
        if hidx == 0:
            C.dbg('d_qb', d['qb'][:, 0:512], [d['bqb']]); C.dbg('d_kb', d['kb'][:, 0:512], [d['bkb']]); C.dbg('d_vb', d['vb'][:, 0:512], [d['bvb']])
            C.dbg('d_tm', d['tm'][:, :, 0:8], [d['btm']]); C.dbg('d_dl', d['dl'][:, 0:8], [d['bdl']])

    f32r = C.ring("G_f32", 4, [64, 512], F32)
    b16r = C.ring("G_b16", 8, [64, 512], BF16)
    Xr = C.ring("G_X", 2, [64, 512], F32)
    gtr = C.ring("G_gtr", 2, [64, 512], F32)
    gtrb = C.ring("G_gtrb", 2, [64, 512], BF16)
    Rr = C.ring("G_R", 2, [64, GB, 256], BF16)
    kgr = C.ring("G_kg", 2, [64, GB, 128], BF16)
    UWr = C.ring("G_UW", 2, [64, GB, 256], BF16)
    atr = C.ring("G_at", 2, [64, 512], BF16)
    NTr = C.ring("G_NT", 2, [128, GB, 128], BF16)
    Qpr = C.ring("G_Qp", 2, [128, 512], BF16)
    q32r = C.ring("G_q32", 1, [128, 512], F32)
    o32r = C.ring("G_o32", 2, [128, 512], F32)
    sqbr = C.ring("G_sqb", 1, [128, 512], BF16)
    obr = C.ring("G_ob", 1, [128, 512], BF16)
    szr = C.ring("G_sz", 1, [128, 512], F32)
    Xbr = C.ring("G_Xb", 2, [64, 512], BF16)

    def bc(t, w, n0):
        return t[:, w, n0:n0 + GB].unsqueeze(2).to_broadcast([64, GB, 64])

    def v3(t):
        return t[:, :].rearrange("p (n x) -> p n x", x=64)

    def pre(hidx, b):
        hi = hidx % 2
        d = HT[hi]
        tm, btm = d['tm'], d['btm']
        n0 = b * GB
        tok = slice(b * 512, (b + 1) * 512)
        gtri, bgtri = gtr()
        I(P, 'dve', 'tensor_tensor', [btm, C.bconst], [bgtri], out=v3(gtri), in0=C.g_tri[:, :].unsqueeze(1).to_broadcast([64, GB, 64]), in1=bc(tm, 0, n0), op=ALU.mult)
        p1, bp1 = C.next_ps()
        MM(P, p1[0:64, :], C.ones32[0:64, 0:64], gtri[:], True, False, [C.bconst, bgtri], bp1)
        MM(P, p1[0:64, :], C.g_idb[0:64, 0:64], C.g_mIb[:].rearrange("p n x -> p (n x)"), False, True, [C.bconst], bp1)
        E2, bE2 = f32r()
        I(P, 'dve', 'tensor_tensor', [bp1, btm], [bE2], out=v3(E2), in0=p1[0:64, :].rearrange("p (n x) -> p n x", x=64), in1=bc(tm, 2, n0), op=ALU.subtract)
        I(P, 'act', 'activation', [bE2], [bE2], out=E2[:], in_=E2[:], func=AF.Exp)
        yield
        ngt, bngt = gtr()
        I(P, 'dve', 'tensor_scalar', [bgtri], [bngt], out=ngt[:], in0=gtri[:], scalar1=-1.0, scalar2=None, op0=ALU.mult)
        p2, bp2 = C.next_ps()
        MM(P, p2[0:64, :], C.ones32[0:64, 0:64], ngt[:], True, False, [C.bconst, bngt], bp2)
        MM(P, p2[0:64, :], C.g_idb[0:64, 0:64], C.g_mSb[:].rearrange("p n x -> p (n x)"), False, True, [C.bconst], bp2)
        E1, bE1 = f32r()
        I(P, 'dve', 'tensor_tensor', [bp2, btm], [bE1], out=v3(E1), in0=p2[0:64, :].rearrange("p (n x) -> p n x", x=64), in1=bc(tm, 2, n0), op=ALU.add)
        I(P, 'act', 'activation', [bE1], [bE1], out=E1[:], in_=E1[:], func=AF.Exp)
        yield
        bdg, bbdg = gtrb()
        I(P, 'dve', 'tensor_tensor', [btm, C.bconst], [bbdg], out=v3(bdg), in0=C.g_idn[:, :].unsqueeze(1).to_broadcast([64, GB, 64]), in1=bc(tm, 5, n0), op=ALU.mult)
        p3, bp3 = C.next_ps()
        MM(P, p3[0:64, :], C.g_offb[:], bdg[:], True, True, [C.bconst, bbdg], bp3)
        pk, bpk = C.next_ps()
        pq, bpq = C.next_ps()
        for n in range(GB):
            cs = slice(b * 512 + n * 64, b * 512 + (n + 1) * 64)
            MM(P, pk[0:64, n * 64:(n + 1) * 64], d['kb'][:, cs], d['kb'][:, cs], True, True, [d['bkb']], bpk)
        for n in range(GB):
            cs = slice(b * 512 + n * 64, b * 512 + (n + 1) * 64)
            MM(P, pq[0:64, n * 64:(n + 1) * 64], d['kb'][:, cs], d['qb'][:, cs], True, True, [d['bkb'], d['bqb']], bpq)
        N0t, bN0t = f32r()
        I(P, 'dve', 'tensor_tensor', [bpk, bE1], [bN0t], out=N0t[:], in0=pk[0:64, :], in1=E1[:], op=ALU.mult)
        N0, bN0 = b16r()
        I(P, 'dve', 'tensor_tensor', [bN0t, btm], [bN0], out=v3(N0), in0=v3(N0t), in1=bc(tm, 5, n0), op=ALU.mult)
        NT0t, bNT0t = f32r()
        I(P, 'dve', 'tensor_tensor', [bpk, bE2], [bNT0t], out=NT0t[:], in0=pk[0:64, :], in1=E2[:], op=ALU.mult)
        X, bX = Xr()
        I(P, 'dve', 'tensor_tensor', [bNT0t, bp3], [bX], out=X[:], in0=NT0t[:], in1=p3[0:64, :], op=ALU.mult)
        NT0, bNT0 = b16r()
        I(P, 'act', 'copy', [bX], [bNT0], out=NT0[:], in_=X[:])
        at, bat = atr()
        I(P, 'dve', 'tensor_tensor', [bpq, bE2], [bat], out=at[:], in0=pq[0:64, :], in1=E2[:], op=ALU.mult)
        if hi == 0 and b == 0:
            C.dbg('d_E1', E1[:], [bE1]); C.dbg('d_E2', E2[:], [bE2]);  C.dbg('d_at', at[:], [bat])
        yield
        I(P, 'dve', 'tensor_tensor', [bX, C.bconst], [bX], out=v3(X), in0=v3(X), in1=C.g_idn[:, :].unsqueeze(1).to_broadcast([64, GB, 64]), op=ALU.add)
        Xs, bXs = Xbr()
        I(P, 'act', 'copy', [bX], [bXs], out=Xs[:], in_=X[:])
        Pj, bPj, PTj, bPTj = N0, bN0, NT0, bNT0
        for lvl in range(1, 6):
            pa, bpa = C.next_ps()
            for n in range(GB):
                sl = slice(n * 64, (n + 1) * 64)
                MM(P, pa[0:64, sl], PTj[:, sl], Pj[:, sl], True, True, [bPTj, bPj], bpa)
            Pn, bPn = b16r()
            I(P, 'act', 'copy', [bpa], [bPn], out=Pn[:], in_=pa[0:64, :])
            if lvl < 5:
                pb_, bpb_ = C.next_ps()
                for n in range(GB):
                    sl = slice(n * 64, (n + 1) * 64)
                    MM(P, pb_[0:64, sl], Pj[:, sl], PTj[:, sl], True, True, [bPTj, bPj], bpb_)
                PTn, bPTn = b16r()
                I(P, 'act', 'copy', [bpb_], [bPTn], out=PTn[:], in_=pb_[0:64, :])
            px, bpx = C.next_ps()
            for n in range(GB):
                sl = slice(n * 64, (n + 1) * 64)
                MM(P, px[0:64, sl], Pn[:, sl], Xs[:, sl], True, True, [bPn, bXs], bpx)
            I(P, 'dve', 'tensor_tensor', [bpx, bX], [bX], out=X[:], in0=X[:], in1=px[0:64, :], op=ALU.add)
            Xs, bXs = Xbr()
            I(P, 'act', 'copy', [bX], [bXs], out=Xs[:], in_=X[:])
            Pj, bPj = Pn, bPn
            if lvl < 5:
                PTj, bPTj = PTn, bPTn
            yield
        if hi == 0 and b == 0:
            C.dbg('d_X', X[:], [bX])
        Xb, bXb = Xs, bXs
        Rt, bRt = Rr()
        kg, bkg = kgr()
        for (src, bsrc, which) in ((d['kb'], d['bkb'], 'k'), (d['vb'], d['bvb'], 'v')):
            for half in range(2):
                pt, bpt = C.next_ps()
                ptb = pt[:].bitcast(BF16)
                for n in range(4):
                    nn = half * 4 + n
                    cs = slice(b * 512 + nn * 64, b * 512 + (nn + 1) * 64)
                    P.mm(lambda e, o=ptb[0:64, n * 128:(n + 1) * 128], i_=src[:, cs]: e.transpose(o, i_, C.g_idb[:]), [bsrc, C.bconst], bpt)
                pv3 = ptb[0:64, 0:512].rearrange("p (n x) -> p n x", x=128)
                hs = slice(half * 4, half * 4 + 4)

                def bc4(w):
                    return tm[:, w, n0 + half * 4:n0 + half * 4 + 4].unsqueeze(2).to_broadcast([64, 4, 128])
                if which == 'k':
                    I(P, 'dve', 'tensor_tensor', [bpt, btm], [bRt], out=Rt[:, hs, 128:256], in0=pv3, in1=bc4(7), op=ALU.mult)
                    I(P, 'dve', 'tensor_tensor', [bpt, btm], [bkg], out=kg[:, hs, :], in0=pv3, in1=bc4(4), op=ALU.mult)
                else:
                    I(P, 'dve', 'tensor_tensor', [bpt, btm], [bRt], out=Rt[:, hs, 0:128], in0=pv3, in1=bc4(1), op=ALU.mult)
            yield
        UW, bUW = UWr()
        for pr in range(4):
            pu, bpu = C.next_ps()
            for n2 in range(2):
                n = pr * 2 + n2
                MM(P, pu[0:64, n2 * 256:(n2 + 1) * 256], Xb[:, n * 64:(n + 1) * 64], Rt[:, n, :], True, True, [bXb, bRt], bpu)
            I(P, 'act', 'copy', [bpu], [bUW], out=UW[:, pr * 2:pr * 2 + 2, :].rearrange("p n x -> p (n x)"), in_=pu[0:64, :])
        yield
        NT, bNT = NTr()
        for pr in range(2):
            pn, bpn = C.next_ps()
            for n4 in range(4):
                n = pr * 4 + n4
                MM(P, pn[:, n4 * 128:(n4 + 1) * 128], UW[:, n, 128:256], kg[:, n, :], True, True, [bUW, bkg], bpn)
            I(P, 'act', 'copy', [bpn], [bNT], out=NT[:, pr * 4:pr * 4 + 4, :].rearrange("p n x -> p (n x)"), in_=pn[:])
        yield
        edd, bedd = gtrb()
        I(P, 'dve', 'tensor_tensor', [btm, C.bconst], [bedd], out=v3(edd), in0=C.g_idn[:, :].unsqueeze(1).to_broadcast([64, GB, 64]), in1=bc(tm, 3, n0), op=ALU.mult)
        pe_, bpe_ = C.next_ps()
        MM(P, pe_[:], C.onesb[0:64, :], edd[:], True, True, [C.bconst, bedd], bpe_)
        q32, bq32 = q32r()
        I(P, 'dve', 'tensor_tensor', [d['bqb'], bpe_], [bq32], out=q32[:], in0=d['qb'][:, tok], in1=pe_[:], op=ALU.mult)
        pw, bpw = C.next_ps()
        for n in range(GB):
            MM(P, pw[:, n * 64:(n + 1) * 64], UW[:, n, 128:256], at[:, n * 64:(n + 1) * 64], True, True, [bUW, bat], bpw)
        Qp, bQp = Qpr()
        I(P, 'dve', 'tensor_tensor', [bq32, bpw], [bQp], out=Qp[:], in0=q32[:], in1=pw[:], op=ALU.add)
        if hi == 0 and b == 0:
            C.dbg('d_R', Rt[:].rearrange("p n x -> p (n x)"), [bRt]); C.dbg('d_kg', kg[:].rearrange("p n x -> p (n x)"), [bkg])
            C.dbg('d_UW', UW[:].rearrange("p n x -> p (n x)"), [bUW]); C.dbg('d_NT', NT[:].rearrange("p n x -> p (n x)"), [bNT]); C.dbg('d_Qp', Qp[:], [bQp])
        d['cur'] = dict(UW=UW, bUW=bUW, kg=kg, bkg=bkg, at=at, bat=bat, NT=NT, bNT=bNT, Qp=Qp, bQp=bQp)
        yield

    def chain(hidx, b, cur):
        hi = hidx % 2
        d = HT[hi]
        po, bpo = C.acc_o[hi]
        for n in range(GB):
            gn = b * GB + n
            first = (gn == 0)
            osl = po[:, n * 64:(n + 1) * 64]
            if not first:
                MM(P, osl, d['Sb'][:], cur['Qp'][:, n * 64:(n + 1) * 64], True, False, [d['bSb'], cur['bQp']], bpo)
            MM(P, osl, cur['UW'][:, n, 0:128], cur['at'][:, n * 64:(n + 1) * 64], first, True, [cur['bUW'], cur['bat']], bpo)
            ps, bps = C.acc_s[hi]
            if not first:
                MM(P, ps[:, 0:128], cur['NT'][:, n, :], d['Sb'][:], True, False, [cur['bNT'], d['bSb']], bps)
            MM(P, ps[:, 0:128], cur['kg'][:, n, :], cur['UW'][:, n, 0:128], first, True, [cur['bkg'], cur['bUW']], bps)
            I(P, 'dve', 'scalar_tensor_tensor', [bps, d['bS32'], d['bdl']], [d['bS32']], out=d['S32'][:], in0=d['S32'][:], scalar=d['dl'][:, gn:gn + 1], in1=ps[:, 0:128], op0=ALU.mult, op1=ALU.add)
            I(P, 'act', 'copy', [d['bS32']], [d['bSb']], out=d['Sb'][:], in_=d['S32'][:])
            yield
        (q0, k0, v0, gr, o0) = heads[hidx]
        o32, bo32 = o32r()
        I(P, 'act', 'copy', [bpo], [bo32], out=o32[:], in_=po[:])
        sqb, bsqb = sqbr()
        I(P, 'act', 'activation', [bpo], [bsqb], out=sqb[:], in_=po[:], func=AF.Square)
        pm, bpm = C.next_ps()
        MM(P, pm[:], C.g_avgb[:], sqb[:], True, True, [C.bconst, bsqb], bpm)
        sq, bsq = o32r()
        I(P, 'act', 'activation', [bpm, C.bconst], [bsq], out=sq[:], in_=pm[:], func=AF.Ln, bias=C.eps_n[:, 0:1], scale=1.0)
        I(P, 'act', 'activation', [bsq], [bsq], out=sq[:], in_=sq[:], func=AF.Exp, scale=-0.5)
        I(P, 'dve', 'scalar_tensor_tensor', [bo32, bsq, bng], [bo32], out=o32[:], in0=o32[:], scalar=ngl[:, 0:1], in1=sq[:], op0=ALU.mult, op1=ALU.mult)
        sz, bsz = szr()
        P.dma('sp', sz[:], szT[o0:o0 + 128, b * 512:(b + 1) * 512], [bdram_in], bsz)
        ob, bob = obr()
        I(P, 'dve', 'tensor_tensor', [bo32, bsz], [bob], out=ob[:], in0=o32[:], in1=sz[:], op=ALU.mult)
        P.dma('sp', mixT[o0:o0 + 128, b * 512:(b + 1) * 512], ob[:], [bob], bdram_out)
        yield

    C.acc_o = [C.acc(0), C.acc(1)]
    C.acc_s = [C.acc(2), C.acc(3)]
    def headB(hidx):
        hi = hidx % 2
        for _ in pre(hidx, 0):
            yield
        for b in range(NB):
            cg = chain(hidx, b, HT[hi]['cur'])
            pg = pre(hidx, b + 1) if b + 1 < NB else iter(())
            c_alive = p_alive = True
            while c_alive or p_alive:
                if c_alive:
                    try:
                        next(cg)
                    except StopIteration:
                        c_alive = False
                if p_alive:
                    for _ in range(2):
                        try:
                            next(pg)
                        except StopIteration:
                            p_alive = False
                            break
                yield

    for _ in phaseA(0):
        pass
    for hidx in range(nh):
        bg = phaseA(hidx + 1) if hidx + 1 < nh else None
        tick = 0
        for _ in headB(hidx):
            tick += 1
            if bg is not None and tick % 3 == 0:
                try:
                    next(bg)
                except StopIteration:
                    bg = None
        if bg is not None:
            for _ in bg:
                pass
    C.nring = 8


import ml_dtypes
from concourse.bass_utils import run_bass_kernel_spmd

KINDS = ['gdn', 'mlstm', 'fox', 'gdn']
NCOLS = {'gdn': 4624, 'mlstm': 3600, 'fox': 4624}
TC = 2048
SEQ = 8192
_BF = ml_dtypes.bfloat16
_progs = {}


def _p_out_specs(kind, T):
    if kind == 'gdn':
        return dict(qkvT=([3072, T], F32), szT=([1024, T], F32), gT=([8, T], F32), betaT=([8, T], F32), memoT=([512, T], BF16))
    if kind == 'mlstm':
        return dict(qT=([512, T], F32), kT=([512, T], F32), v=([T, 1024], BF16), sogT=([1024, T], F32), ilogT=([8, T], F32), flogT=([8, T], F32), memoT=([512, T], BF16))
    return dict(qT=([1024, T], BF16), kT=([1024, T], BF16), v=([T, 1024], BF16), sogT=([1024, T], F32), flogT=([16, T], F32), memoT=([512, T], BF16))


def _p_prm_specs(kind):
    if kind == 'gdn':
        return dict(a_log=[8], dt_bias=[8])
    if kind == 'mlstm':
        return dict(b_gate=[2, 8])
    return dict(b_f=[16], qk_g=[2, 64])


def _new_nc():
    return bass.Bass("TRN2", target_bir_lowering=False)


def build_P(kind):
    key = ('P', kind)
    if key in _progs:
        return _progs[key]
    nc = _new_nc()
    di = lambda n, s, dt=F32: nc.dram_tensor(n, s, dt, kind="ExternalInput").ap()
    xT = di("xT", [1024, TC]); memT = di("memT", [1024, 256]); w_in = di("w_in", [1024, NCOLS[kind]]); w_kv = di("w_kv", [1024, 1024])
    prm = {k: di(k, s) for k, s in _p_prm_specs(kind).items()}
    outs = {k: nc.dram_tensor(k, s, dt, kind="ExternalOutput").ap() for k, (s, dt) in _p_out_specs(kind, TC).items()}
    with ExitStack() as st:
        P = Prog(nc, st)
        C = Ctx(nc, st, P)
        stage_P(C, TC, kind, xT, memT, w_in, w_kv, prm, outs, P.buf("din"), P.buf("dout"))
        P.barrier()
        P.emit()
    _progs[key] = nc
    return nc


def build_F():
    key = ('F',)
    if key in _progs:
        return _progs[key]
    nc = _new_nc()
    di = lambda n, s, dt=F32: nc.dram_tensor(n, s, dt, kind="ExternalInput").ap()
    mixT = di("mixT", [1024, TC], BF16); memoT = di("memoT", [512, TC], BF16); xT = di("xT", [1024, TC])
    w_out = di("w_out", [1536, 1024]); w_up = di("w_up", [1024, 4096]); w_down = di("w_down", [4096, 1024])
    l1g = di("l1g", [1024]); l1b = di("l1b", [1024]); l2g = di("l2g", [1024]); l2b = di("l2b", [1024])
    yT = nc.dram_tensor("yT", [1024, TC], F32, kind="ExternalOutput").ap()
    with ExitStack() as st:
        P = Prog(nc, st)
        C = Ctx(nc, st, P)
        stage_F(C, TC, mixT, memoT, xT, yT, w_out, l1g, l1b, w_up, w_down, l2g, l2b, P.buf("din"), P.buf("dout"))
        P.barrier()
        P.emit()
    _progs[key] = nc
    return nc


def build_M(kind):
    key = ('M', kind)
    if key in _progs:
        return _progs[key]
    nc = _new_nc()
    S = SEQ
    di = lambda n, s, dt=F32: nc.dram_tensor(n, s, dt, kind="ExternalInput").ap()
    with ExitStack() as st:
        P = Prog(nc, st)
        C = Ctx(nc, st, P)
        bin_, bout = P.buf("din"), P.buf("dout")
        mixT = nc.dram_tensor("mixT", [256, S], BF16, kind="ExternalOutput").ap()
        if kind == 'gdn':
            qkvT = di("qkvT", [768, S]); conv_w = di("conv_w", [4, 768]); szT = di("szT", [256, S]); gT = di("gT", [2, S]); betaT = di("betaT", [2, S]); ng = di("ng", [128])
            heads = [(h * 128, 256 + h * 128, 512 + h * 128, h, h * 128) for h in range(2)]
            mixer_gdn(C, S, heads, qkvT, conv_w, szT, gT, betaT, ng, mixT, bin_, bout)
        elif kind == 'mlstm':
            qT = di("qT", [128, S]); kT = di("kT", [128, S]); v = di("v", [S, 256], BF16)
            sogT = di("sogT", [256, S]); flogT = di("flogT", [2, S]); ilogT = di("ilogT", [2, S]); ng = di("ng", [256])
            mixer_mlstm(C, S, 2, qT, kT, v, sogT, ilogT, flogT, ng, mixT, bin_, bout)
        else:
            qT = di("qT", [256, S], BF16); kT = di("kT", [256, S], BF16); v = di("v", [S, 256], BF16)
            sogT = di("sogT", [256, S]); flogT = di("flogT", [4, S])
            mixer_fox(C, S, 4, qT, kT, v, sogT, flogT, mixT, bin_, bout)
        P.barrier()
        P.emit()
    _progs[key] = nc
    return nc


FUSED = True
_W_SPECS = dict(gdn_w_in=[2, 1024, 4624], gdn_conv_w=[2, 4, 3072], gdn_a_log=[2, 8], gdn_dt_bias=[2, 8], gdn_norm_g=[2, 128],
                mlstm_w_in=[1, 1024, 3600], mlstm_b_gate=[1, 2, 8], mlstm_norm_g=[1, 1024], fox_w_in=[1, 1024, 4624], fox_b_f=[1, 16],
                fox_qk_g=[1, 2, 64], mem_w_kv=[4, 1024, 1024], w_out=[4, 1536, 1024], ln1_g=[4, 1024], ln1_b=[4, 1024],
                w_up=[4, 1024, 4096], w_down=[4, 4096, 1024], ln2_g=[4, 1024], ln2_b=[4, 1024])


def build_fused(nlayers=4):
    key = ('fused', nlayers)
    if key in _progs:
        return _progs[key]
    nc = _new_nc()
    S = SEQ
    di = lambda n, s, dt=F32: nc.dram_tensor(n, s, dt, kind="ExternalInput").ap()
    sc = lambda n, s, dt=F32: nc.dram_tensor(n, s, dt).ap()
    xT0 = di("xT", [1024, S]); memT = di("memT", [1024, 256])
    W = {k: di(k, shp) for k, shp in _W_SPECS.items()}
    yT = nc.dram_tensor("yT", [1024, S], F32, kind="ExternalOutput").ap()
    xa, xb_ = sc("xa", [1024, S]), sc("xb", [1024, S])
    scr = {}
    for kind in ('gdn', 'mlstm', 'fox'):
        scr[kind] = {k: sc("%s_%s" % (kind, k), shp, dt) for k, (shp, dt) in _p_out_specs(kind, S).items()}
    mixT = sc("mixT", [1024, S], BF16)
    with ExitStack() as st:
        P = Prog(nc, st)
        C = Ctx(nc, st, P)
        cur = xT0
        for li in range(nlayers):
            kind, j = KINDS[li], li // 3
            nxt = yT if li == nlayers - 1 else (xa if li % 2 == 0 else xb_)
            o = scr[kind]
            if kind == 'gdn':
                w_in = W['gdn_w_in'][j]
                prm = dict(a_log=W['gdn_a_log'][j], dt_bias=W['gdn_dt_bias'][j])
            elif kind == 'mlstm':
                w_in = W['mlstm_w_in'][j]
                prm = dict(b_gate=W['mlstm_b_gate'][j])
            else:
                w_in = W['fox_w_in'][j]
                prm = dict(b_f=W['fox_b_f'][j], qk_g=W['fox_qk_g'][j])
            C.stage_begin()
            for tc in range(S // TC):
                tok = slice(tc * TC, (tc + 1) * TC)
                outs = {k: (a[tok, :] if k == 'v' else a[:, tok]) for k, a in o.items()}
                stage_P(C, TC, kind, cur[:, tok], memT, w_in, W['mem_w_kv'][li], prm, outs, C.buf("din"), C.buf("dout"))
            C.stage_end()
            C.stage_begin()
            bi, bo = C.buf("din"), C.buf("dout")
            if kind == 'gdn':
                heads = [(h * 128, 1024 + h * 128, 2048 + h * 128, h, h * 128) for h in range(8)]
                mixer_gdn(C, S, heads, o['qkvT'], W['gdn_conv_w'][j], o['szT'], o['gT'], o['betaT'], W['gdn_norm_g'][j], mixT, bi, bo)
            elif kind == 'mlstm':
                mixer_mlstm(C, S, 8, o['qT'], o['kT'], o['v'], o['sogT'], o['ilogT'], o['flogT'], W['mlstm_norm_g'][j], mixT, bi, bo)
            else:
                mixer_fox(C, S, 16, o['qT'], o['kT'], o['v'], o['sogT'], o['flogT'], mixT, bi, bo)
            C.stage_end()
            C.stage_begin()
            for tc in range(S // TC):
                tok = slice(tc * TC, (tc + 1) * TC)
                stage_F(C, TC, mixT[:, tok], o['memoT'][:, tok], cur[:, tok], nxt[:, tok], W['w_out'][li], W['ln1_g'][li], W['ln1_b'][li],
                        W['w_up'][li], W['w_down'][li], W['ln2_g'][li], W['ln2_b'][li], C.buf("din"), C.buf("dout"))
            C.stage_end()
            cur = nxt
    _progs[key] = nc
    return nc


def kernel_fused(inputs):
    f32 = np.float32
    x = np.asarray(inputs['x'], f32)
    mem = np.asarray(inputs['mem'], f32)
    wts = {}
    for k, shp in _W_SPECS.items():
        wts[k] = _c(np.asarray(inputs[k], f32).reshape(shp))
    ims = []
    for c in range(8):
        b = c % 2
        ims.append(dict(xT=_c(x[b].T), memT=_c(mem[b].T), **wts))
    res = _run(build_fused(), ims)
    out = np.empty((2, SEQ, 1024), f32)
    for b in range(2):
        out[b] = np.asarray(res[b]['yT']).T
    return out


def _run(nc, in_maps):
    res = run_bass_kernel_spmd(nc, in_maps, core_ids=list(range(8)))
    return res.results


def _cat_tok(outs, name, b, axis):
    return np.concatenate([np.asarray(outs[b * 4 + c][name]) for c in range(4)], axis=axis)


def _c(a):
    return np.ascontiguousarray(a)


def kernel(x, mem, gdn_w_in, gdn_conv_w, gdn_a_log, gdn_dt_bias, gdn_norm_g,
           mlstm_w_in, mlstm_b_gate, mlstm_norm_g, fox_w_in, fox_b_f, fox_qk_g,
           mem_w_kv, w_out, ln1_g, ln1_b, w_up, w_down, ln2_g, ln2_b):
    if FUSED:
        return kernel_fused(dict(x=x, mem=mem, gdn_w_in=gdn_w_in, gdn_conv_w=gdn_conv_w, gdn_a_log=gdn_a_log, gdn_dt_bias=gdn_dt_bias,
                                 gdn_norm_g=gdn_norm_g, mlstm_w_in=mlstm_w_in, mlstm_b_gate=mlstm_b_gate, mlstm_norm_g=mlstm_norm_g,
                                 fox_w_in=fox_w_in, fox_b_f=fox_b_f, fox_qk_g=fox_qk_g, mem_w_kv=mem_w_kv, w_out=w_out, ln1_g=ln1_g,
                                 ln1_b=ln1_b, w_up=w_up, w_down=w_down, ln2_g=ln2_g, ln2_b=ln2_b))
    f32 = np.float32
    x = np.asarray(x, f32)
    mem = np.asarray(mem, f32)
    xT = [_c(x[c // 4, (c % 4) * TC:(c % 4 + 1) * TC, :].T) for c in range(8)]
    memT = [_c(mem[b].T) for b in range(2)]
    for li in range(4):
        kind, j = KINDS[li], li // 3
        if kind == 'gdn':
            w_in = np.asarray(gdn_w_in[j], f32)
            prm = dict(a_log=np.asarray(gdn_a_log[j], f32), dt_bias=np.asarray(gdn_dt_bias[j], f32))
        elif kind == 'mlstm':
            w_in = np.asarray(mlstm_w_in[j], f32)
            prm = dict(b_gate=np.asarray(mlstm_b_gate[j], f32))
        else:
            w_in = np.asarray(fox_w_in[j], f32)
            prm = dict(b_f=np.asarray(fox_b_f[j], f32), qk_g=np.asarray(fox_qk_g[j], f32))
        wkv = np.asarray(mem_w_kv[li], f32)
        ims = [dict(xT=xT[c], memT=memT[c // 4], w_in=w_in, w_kv=wkv, **prm) for c in range(8)]
        po = _run(build_P(kind), ims)
        ims = []
        for jc in range(8):
            b, hg = jc // 4, jc % 4
            if kind == 'gdn':
                qkv = _cat_tok(po, 'qkvT', b, 1) if hg == 0 else qkv_cache
                qkv_cache = qkv
                hs = [2 * hg, 2 * hg + 1]
                rows = np.concatenate([np.arange(o + h * 128, o + (h + 1) * 128) for o in (0, 1024, 2048) for h in hs])
                if hg == 0:
                    sz_c = _cat_tok(po, 'szT', b, 1); g_c = _cat_tok(po, 'gT', b, 1); be_c = _cat_tok(po, 'betaT', b, 1)
                ims.append(dict(qkvT=_c(qkv[rows]), conv_w=_c(np.asarray(gdn_conv_w[j], f32)[:, rows]), szT=_c(sz_c[hs[0] * 128:(hs[1] + 1) * 128]),
                                gT=_c(g_c[hs[0]:hs[1] + 1]), betaT=_c(be_c[hs[0]:hs[1] + 1]), ng=np.asarray(gdn_norm_g[j], f32)))
            elif kind == 'mlstm':
                if hg == 0:
                    q_c = _cat_tok(po, 'qT', b, 1); k_c = _cat_tok(po, 'kT', b, 1); v_c = _cat_tok(po, 'v', b, 0)
                    so_c = _cat_tok(po, 'sogT', b, 1); il_c = _cat_tok(po, 'ilogT', b, 1); fl_c = _cat_tok(po, 'flogT', b, 1)
                h0 = 2 * hg
                ims.append(dict(qT=_c(q_c[h0 * 64:(h0 + 2) * 64]), kT=_c(k_c[h0 * 64:(h0 + 2) * 64]), v=_c(v_c[:, h0 * 128:(h0 + 2) * 128]),
                                sogT=_c(so_c[h0 * 128:(h0 + 2) * 128]), ilogT=_c(il_c[h0:h0 + 2]), flogT=_c(fl_c[h0:h0 + 2]),
                                ng=_c(np.asarray(mlstm_norm_g[j], f32)[h0:h0 + 2].reshape(-1))))
            else:
                if hg == 0:
                    q_c = _cat_tok(po, 'qT', b, 1); k_c = _cat_tok(po, 'kT', b, 1); v_c = _cat_tok(po, 'v', b, 0)
                    so_c = _cat_tok(po, 'sogT', b, 1); fl_c = _cat_tok(po, 'flogT', b, 1)
                h0 = 4 * hg
                ims.append(dict(qT=_c(q_c[h0 * 64:(h0 + 4) * 64]), kT=_c(k_c[h0 * 64:(h0 + 4) * 64]), v=_c(v_c[:, h0 * 64:(h0 + 4) * 64]),
                                sogT=_c(so_c[h0 * 64:(h0 + 4) * 64]), flogT=_c(fl_c[h0:h0 + 4])))
        mo = _run(build_M(kind), ims)
        ims = []
        for c in range(8):
            b, sc = c // 4, c % 4
            mixT = np.concatenate([np.asarray(mo[b * 4 + hg]['mixT'])[:, sc * TC:(sc + 1) * TC] for hg in range(4)], axis=0)
            ims.append(dict(mixT=_c(mixT), memoT=_c(np.asarray(po[c]['memoT'])), xT=xT[c],
                            w_out=np.asarray(w_out[li], f32), w_up=np.asarray(w_up[li], f32), w_down=np.asarray(w_down[li], f32),
                            l1g=np.asarray(ln1_g[li], f32), l1b=np.asarray(ln1_b[li], f32), l2g=np.asarray(ln2_g[li], f32), l2b=np.asarray(ln2_b[li], f32)))
        fo = _run(build_F(), ims)
        xT = [_c(np.asarray(fo[c]['yT'])) for c in range(8)]
    out = np.empty((2, SEQ, 1024), f32)
    for c in range(8):
        out[c // 4, (c % 4) * TC:(c % 4 + 1) * TC, :] = xT[c].T
    return out
```

```python
from contextlib import ExitStack
import numpy as np
import concourse.bass as bass
import concourse.mybir as mybir

F32 = mybir.dt.float32
BF16 = mybir.dt.bfloat16
AF = mybir.ActivationFunctionType
ALU = mybir.AluOpType
AX = mybir.AxisListType

ENGS = ['pe', 'act', 'dve', 'pool', 'sp']
SYNC_SAME_ENGINE = True
NDSEM = 32


class Buf:
    __slots__ = ('name', 'last_w', 'readers', 'sem', 'cnt')

    def __init__(self, name):
        self.name = name
        self.last_w = None
        self.readers = []
        self.sem = None
        self.cnt = 0


class Op:
    __slots__ = ('eng', 'fn', 'deps', 'is_dma', 'tok', 'signal', 'seq', 'pe_buf', 'inc')

    def __init__(self, eng, fn):
        self.eng = eng
        self.fn = fn
        self.deps = []
        self.is_dma = False
        self.tok = None
        self.signal = False
        self.seq = None
        self.pe_buf = None
        self.inc = 1


class Prog:
    def __init__(self, nc, stack):
        self.nc = nc
        self.stack = stack
        self.ops = {e: [] for e in ENGS}
        self.esem = {e: stack.enter_context(nc.semaphore('es_' + e)) for e in ENGS}
        self.ecnt = {e: 0 for e in ENGS}
        self.waited = {e: {} for e in ENGS}
        self.bufs = []
        self.dsem = [stack.enter_context(nc.semaphore('ds_%d' % i)) for i in range(NDSEM)]
        self.dlast = [None] * NDSEM
        self.dcount = 0
        self.nops = 0
        self.serial = False
        self.prev = None

    def buf(self, name):
        b = Buf(name)
        self.bufs.append(b)
        return b

    def _deps(self, op, reads, writes):
        if self.serial:
            if self.prev is not None:
                op.deps.append(self.prev)
            self.prev = op
        for b in reads:
            if b.last_w is not None:
                op.deps.append(b.last_w)
        for b in writes:
            if b.last_w is not None:
                op.deps.append(b.last_w)
            op.deps.extend(b.readers)
        for b in reads:
            b.readers.append(op)
        for b in writes:
            b.last_w = op
            b.readers = []

    def op(self, eng, fn, reads=(), writes=(), inc=1):
        o = Op(eng, fn)
        o.inc = inc
        self._deps(o, reads, writes)
        self.ops[eng].append(o)
        self.nops += 1
        return o

    def mm(self, fn, reads, out):
        o = Op('pe', fn)
        o.pe_buf = out
        self._deps(o, reads, [out])
        o.deps = [d for d in o.deps if not (d.eng == 'pe' and d.pe_buf is out)]
        self.ops['pe'].append(o)
        self.nops += 1
        return o

    def dma(self, queue, out, in_, reads, wbuf, **kw):
        n = self.dcount
        self.dcount += 1
        i = n % NDSEM
        sem = self.dsem[i]
        val = 16 * (n // NDSEM + 1)

        def fn(eng, out=out, in_=in_, kw=kw):
            return eng.dma_start(out=out, in_=in_, **kw)
        o = Op(queue, fn)
        o.is_dma = True
        o.tok = (sem, val)
        if self.dlast[i] is not None:
            o.deps.append(self.dlast[i])
        self.dlast[i] = o
        if not isinstance(wbuf, (list, tuple)):
            wbuf = [wbuf]
        self._deps(o, reads, wbuf)
        self.ops[queue].append(o)
        self.nops += 1
        return o

    def barrier(self):
        lasts = []
        for e in ENGS:
            if self.ops[e]:
                for o in reversed(self.ops[e]):
                    if not o.is_dma:
                        lasts.append(o)
                        break
        dl = [o for o in self.dlast if o is not None]
        for e in ENGS:
            o = Op(e, lambda eng: eng.nop())
            o.deps = list(lasts) + dl
            self.ops[e].append(o)
        for b in self.bufs:
            b.last_w = None
            b.readers = []

    def emit(self):
        nc = self.nc
        for e in ENGS:
            for o in self.ops[e]:
                for d in o.deps:
                    if not d.is_dma:
                        if d.eng == e and not SYNC_SAME_ENGINE:
                            continue
                        d.signal = True
        for e in ENGS:
            for o in self.ops[e]:
                if o.signal and not o.is_dma and o.seq is None:
                    self.ecnt[e] += o.inc
                    o.seq = self.ecnt[e]
        engobj = {'pe': nc.tensor, 'act': nc.scalar, 'dve': nc.vector, 'pool': nc.gpsimd, 'sp': nc.sync}
        stats = {'waits': 0}

        def run(e, eng):
            waited = self.waited[e]
            for o in self.ops[e]:
                need = {}
                for d in o.deps:
                    if d.is_dma:
                        sem, val = d.tok
                    else:
                        if d.eng == e and not SYNC_SAME_ENGINE:
                            continue
                        sem, val = self.esem[d.eng], d.seq
                    k = id(sem)
                    if waited.get(k, (None, 0))[1] >= val:
                        continue
                    if k not in need or need[k][1] < val:
                        need[k] = (sem, val)
                for k, (sem, val) in need.items():
                    eng.wait_ge(sem, val)
                    waited[k] = (sem, val)
                    stats['waits'] += 1
                ins = o.fn(eng)
                if o.is_dma:
                    ins.then_inc(o.tok[0], 16)
                elif o.signal:
                    ins.then_inc(self.esem[e], o.inc)
            self.ops[e] = []

        with nc.Block() as block:
            @block.tensor
            def _(eng):
                run('pe', eng)

            @block.scalar
            def _(eng):
                run('act', eng)

            @block.vector
            def _(eng):
                run('dve', eng)

            @block.gpsimd
            def _(eng):
                run('pool', eng)

            @block.sync
            def _(eng):
                run('sp', eng)
        return stats


D = 1024
DFF = 4096
ALPHA = 8 ** 0.25
LN_EPS = 1e-5
NORM_EPS = 1e-6
TT = 512


class Ctx:
    def __init__(self, nc, st, P):
        self.nc, self.st, self.P = nc, st, P
        self.ps = []
        for i in range(8):
            t = st.enter_context(nc.psum_tensor("ps%d" % i, [128, 512], F32))
            self.ps.append((t, P.buf("ps%d" % i)))
        self.psi = 0
        self.nring = 8
        self.cache = {}
        self.pcache = {}
        self.pools = {}
        self.sst = st
        self.stage_no = 0
        self.wr = []
        self.wi = 0
        self.ones32 = self.sbp("ones32", [128, 128], F32)
        self.onesb = self.sbp("onesb", [128, 128], BF16)
        b = P.buf("consts")
        self.bconst = b
        P.op('pool', lambda e: e.memset(self.ones32[:], 1.0), [], [b])
        P.op('pool', lambda e: e.memset(self.onesb[:], 1.0), [], [b])
        self.eps_ln = self.sbp('eps_ln', [128, 1], F32)
        self.eps_n = self.sbp('eps_n', [128, 1], F32)
        P.op('pool', lambda e: e.memset(self.eps_ln[:], LN_EPS), [], [b])
        P.op('pool', lambda e: e.memset(self.eps_n[:], NORM_EPS), [], [b])

    def sb(self, name, shape, dt):
        if name in self.pcache:
            return self.pcache[name]
        if name not in self.cache:
            self.cache[name] = self.sst.enter_context(self.nc.sbuf_tensor("%s_s%d" % (name, self.stage_no), shape, dt))
        return self.cache[name]

    def sbp(self, name, shape, dt):
        if name not in self.pcache:
            self.pcache[name] = self.st.enter_context(self.nc.sbuf_tensor(name, shape, dt))
        return self.pcache[name]

    def stage_begin(self):
        self.stage_no += 1
        self.sst = ExitStack()
        self.cache = {}
        self.wr = []
        self.wi = 0

    def stage_end(self):
        self.P.barrier()
        self.P.emit()
        self.sst.close()
        self.sst = self.st
        self.cache = {}
        self.wr = []

    def buf(self, name):
        k = 'buf:' + name
        if k not in self.cache:
            self.cache[k] = self.P.buf(name)
        return self.cache[k]

    def next_ps(self, pool=None):
        if pool is not None:
            banks, st = self.pools[pool]
            t, b = self.ps[banks[st[0] % len(banks)]]
            st[0] += 1
            return t, b
        t, b = self.ps[self.psi % self.nring]
        self.psi += 1
        return t, b

    def set_pool(self, name, banks):
        self.pools[name] = (list(banks), [0])

    def dbg(self, name, ap, bufs):
        d = getattr(self, 'dbgs', None)
        if d and name in d and name not in self.cache:
            self.cache[name] = True
            self.P.dma('sp', d[name], ap, list(bufs), self.buf('dbg_out'))

    def acc(self, i):
        return self.ps[self.nring + i]

    def load_w(self, w_ap, r0, nrows, c0, ncols):
        kc = nrows // 128
        if not self.wr:
            for i in range(4):
                self.wr.append((self.sb("wr%d" % i, [128, 4096], BF16), self.buf("wr%d" % i)))
        t, b = self.wr[self.wi % 4]
        self.wi += 1
        view = t[:, 0:kc * ncols].rearrange("p (k c) -> p k c", k=kc)
        src = w_ap[r0:r0 + nrows, c0:c0 + ncols].rearrange("(k p) c -> p k c", p=128)
        self.P.dma('pool', view, src, [], b)
        return view, b

    def ring(self, name, n, shape, dt):
        slots = [(self.sb("%s%d" % (name, i), shape, dt), self.buf("%s%d" % (name, i))) for i in range(n)]
        k = 'ring:' + name
        if k not in self.cache:
            self.cache[k] = {'i': 0}
        st = self.cache[k]

        def nxt():
            s = slots[st['i'] % n]
            st['i'] += 1
            return s
        return nxt


def layer_norm_fm(C, x32, bx, xb, bxb, g_t, b_t, bgb, ntok_tiles, t0, sq_ring, st_ring):
    P = C.P
    for tt in range(ntok_tiles):
        sl = slice(t0 + tt * TT, t0 + (tt + 1) * TT)
        s1, bs1 = C.next_ps()
        s2, bs2 = C.next_ps()
        for k in range(8):
            zb, bzb = C.ring("LN_zb", 2, [128, TT], BF16)()
            I(P, 'act', 'copy', [bx[k][tt]], [bzb], out=zb[:], in_=x32[:, k, sl])
            MM(P, s1[:], C.onesb[:], zb[:], k == 0, k == 7, [bzb, C.bconst], bs1)
            sqb, bsqb = C.ring("LN_sqb", 2, [128, TT], BF16)()
            I(P, 'act', 'activation', [bx[k][tt]], [bsqb], out=sqb[:], in_=x32[:, k, sl], func=AF.Square)
            MM(P, s2[:], C.onesb[:], sqb[:], k == 0, k == 7, [bsqb, C.bconst], bs2)
        m, bm = st_ring()
        msq, bmsq = st_ring()
        A, bA = st_ring()
        Bc, bBc = st_ring()
        P.op('dve', lambda e, m=m, s1=s1, sl=sl: e.tensor_scalar(out=m[:], in0=s1[:], scalar1=1.0 / D, scalar2=None, op0=ALU.mult), [bs1], [bm])
        P.op('dve', lambda e, m=m, msq=msq, sl=sl: e.tensor_tensor(out=msq[:], in0=m[:], in1=m[:], op=ALU.mult), [bm], [bmsq])
        P.op('dve', lambda e, msq=msq, s2=s2, sl=sl: e.scalar_tensor_tensor(out=msq[:], in0=s2[:], scalar=1.0 / D, in1=msq[:], op0=ALU.mult, op1=ALU.subtract),
             [bs2, bmsq], [bmsq])
        P.op('act', lambda e, msq=msq, A=A, sl=sl: e.activation(out=A[:], in_=msq[:], func=AF.Ln, bias=C.eps_ln[:, 0:1], scale=1.0), [bmsq, C.bconst], [bA])
        P.op('act', lambda e, A=A, sl=sl: e.activation(out=A[:], in_=A[:], func=AF.Exp, scale=-0.5), [bA], [bA])
        P.op('dve', lambda e, m=m, A=A, Bc=Bc, sl=sl: e.scalar_tensor_tensor(out=Bc[:], in0=m[:], scalar=-1.0, in1=A[:], op0=ALU.mult, op1=ALU.mult),
             [bm, bA], [bBc])
        for k in range(8):
            u, bu = sq_ring()
            P.op('dve', lambda e, k=k, u=u, A=A, sl=sl: e.scalar_tensor_tensor(out=u[:], in0=x32[:, k, sl], scalar=g_t[:, k:k + 1], in1=A[:], op0=ALU.mult, op1=ALU.mult),
                 [bx[k][tt], bA, bgb], [bu])
            P.op('dve', lambda e, k=k, u=u, Bc=Bc, sl=sl: e.scalar_tensor_tensor(out=u[:], in0=Bc[:], scalar=g_t[:, k:k + 1], in1=u[:], op0=ALU.mult, op1=ALU.add),
                 [bBc, bu, bgb], [bu])
            P.op('act', lambda e, k=k, u=u, sl=sl: e.activation(out=x32[:, k, sl], in_=u[:], func=AF.Identity, bias=b_t[:, k:k + 1], scale=1.0),
                 [bu, bgb], [bx[k][tt]])
            P.op('act', lambda e, k=k, u=u, sl=sl: e.activation(out=xb[:, k, sl], in_=u[:], func=AF.Identity, bias=b_t[:, k:k + 1], scale=1.0),
                 [bu, bgb], [bxb[k][tt]])


def stage_F(C, T, mixT, memoT, xT_in, xT_out, w_out, ln1g, ln1b, w_up, w_down, ln2g, ln2b, bdram_in, bdram_out, dbg=None):
    P = C.P
    nc = C.nc
    TH = min(T, 1024)
    ntt = TH // TT
    x32 = C.sb("F_x32", [128, 8, TH], F32)
    xb = C.sb("F_xb", [128, 8, TH], BF16)
    cat = C.sb("F_cat", [128, 12, TH], BF16)
    h = C.sb("F_h", [128, 32, TH], BF16)
    lnp = C.sb("F_lnp", [128, 4, 8], F32)
    blnp = C.buf("lnp")
    for i, v in enumerate([ln1g, ln1b, ln2g, ln2b]):
        P.dma("sp", lnp[:, i, :], v.rearrange("(k p) -> p k", p=128), [], blnp, allow_slow_non_contiguous=True)
    sq_ring = C.ring("F_sq", 2, [128, TT], F32)
    st_ring = C.ring("F_st", 4, [128, TT], F32)
    relu_ring = C.ring("F_relu", 3, [128, TT], F32)
    bx = [[C.buf("x%d_%d" % (k, t)) for t in range(ntt)] for k in range(8)]
    bxb = [[C.buf("xb%d_%d" % (k, t)) for t in range(ntt)] for k in range(8)]
    bcat = [[C.buf("cat%d_%d" % (k, t)) for t in range(ntt)] for k in range(12)]
    bh = [[C.buf("h%d_%d" % (k, t)) for t in range(ntt)] for k in range(32)]
    for half in range(T // TH):
        t0 = half * TH
        for k in range(12):
            src = mixT[k * 128:(k + 1) * 128, t0:t0 + TH] if k < 8 else memoT[(k - 8) * 128:(k - 7) * 128, t0:t0 + TH]
            for tt in range(ntt):
                P.dma('sp', cat[:, k, tt * TT:(tt + 1) * TT], src[:, tt * TT:(tt + 1) * TT], [bdram_in], bcat[k][tt])
        for k in range(8):
            for tt in range(ntt):
                P.dma('sp', x32[:, k, tt * TT:(tt + 1) * TT], xT_in[k * 128:(k + 1) * 128, t0 + tt * TT:t0 + (tt + 1) * TT], [bdram_in], bx[k][tt])
        for og in range(4):
            wt, bw = C.load_w(w_out, 0, 1536, og * 256, 256)
            for oc in range(2):
                o = og * 2 + oc
                for tt in range(ntt):
                    sl = slice(tt * TT, (tt + 1) * TT)
                    ps, bps = C.next_ps()
                    for k in range(12):
                        MM(P, ps[:], wt[:, k, oc * 128:(oc + 1) * 128], cat[:, k, sl], k == 0, k == 11, [bw, bcat[k][tt]], bps)
                    I(P, 'dve', 'scalar_tensor_tensor', [bps, bx[o][tt]], [bx[o][tt]], out=x32[:, o, sl], in0=x32[:, o, sl], scalar=ALPHA, in1=ps[:], op0=ALU.mult, op1=ALU.add)
        def dump(name):
            if dbg and name in dbg:
                for k in range(8):
                    for tt in range(ntt):
                        P.dma('sp', dbg[name][k * 128:(k + 1) * 128, t0 + tt * TT:t0 + (tt + 1) * TT], x32[:, k, tt * TT:(tt + 1) * TT], [bx[k][tt]], bdram_out)
        dump('z1')
        layer_norm_fm(C, x32, bx, xb, bxb, lnp[:, 0, :], lnp[:, 1, :], blnp, ntt, 0, sq_ring, st_ring)
        dump('x1')
        for hg in range(8):
            wt, bw = C.load_w(w_up, 0, 1024, hg * 512, 512)
            for oc in range(4):
                hc = hg * 4 + oc
                for tt in range(ntt):
                    sl = slice(tt * TT, (tt + 1) * TT)
                    ps, bps = C.next_ps()
                    for k in range(8):
                        P.mm(lambda e, ps=ps, wt=wt, k=k, oc=oc, sl=sl: e.matmul(ps[:], wt[:, k, oc * 128:(oc + 1) * 128], xb[:, k, sl], start=(k == 0), stop=(k == 7)),
                             [bw, bxb[k][tt]], bps)
                    r, br = relu_ring()
                    I(P, 'act', 'activation', [bps], [br], out=r[:], in_=ps[:], func=AF.Relu)
                    I(P, 'dve', 'tensor_tensor', [br, bps], [bh[hc][tt]], out=h[:, hc, sl], in0=r[:], in1=ps[:], op=ALU.mult)
        for og in range(4):
            wts = [C.load_w(w_down, kh * 2048, 2048, og * 256, 256) for kh in range(2)]
            for oc in range(2):
                o = og * 2 + oc
                for tt in range(ntt):
                    sl = slice(tt * TT, (tt + 1) * TT)
                    ps, bps = C.next_ps()
                    for k in range(32):
                        wt, bw = wts[k // 16]
                        MM(P, ps[:], wt[:, k % 16, oc * 128:(oc + 1) * 128], h[:, k, sl], k == 0, k == 31, [bw, bh[k][tt]], bps)
                    I(P, 'dve', 'scalar_tensor_tensor', [bps, bx[o][tt]], [bx[o][tt]], out=x32[:, o, sl], in0=x32[:, o, sl], scalar=ALPHA, in1=ps[:], op0=ALU.mult, op1=ALU.add)
        layer_norm_fm(C, x32, bx, xb, bxb, lnp[:, 2, :], lnp[:, 3, :], blnp, ntt, 0, sq_ring, st_ring)
        for k in range(8):
            for tt in range(ntt):
                P.dma('sp', xT_out[k * 128:(k + 1) * 128, t0 + tt * TT:t0 + (tt + 1) * TT], x32[:, k, tt * TT:(tt + 1) * TT], [bx[k][tt]], bdram_out)


def I(P, eng, name, reads, writes, *args, **kw):
    return P.op(eng, lambda e, name=name, args=args, kw=kw: getattr(e, name)(*args, **kw), reads, writes)


def MM(P, out, lhsT, rhs, start, stop, reads, obuf):
    return P.mm(lambda e, out=out, lhsT=lhsT, rhs=rhs, start=start, stop=stop: e.matmul(out, lhsT, rhs, start=start, stop=stop), reads, obuf)


MEM_SCALE = 128 ** -0.5


def stage_P(C, T, kind, xT_in, memT, w_in, w_kv, prm, outs, bdram_in, bdram_out):
    P = C.P
    ntt = T // TT
    pi = C.cache.setdefault("P_parity", [0])
    pi[0] ^= 1
    xb = C.sb("P_xb%d" % pi[0], [128, 8, T], BF16)
    memb = C.sb("P_memb", [128, 8, 256], BF16)
    kmT = C.sb("P_kmT", [128, 4, 256], BF16)
    vm = C.sb("P_vm", [128, 2, 512], BF16)
    bxb = [C.buf("Pxb%d_%d" % (pi[0], k)) for k in range(8)]
    bmemb, bkmT, bvm = C.buf("memb"), C.buf("kmT"), C.buf("vm")
    for k in range(8):
        P.dma('pool', xb[:, k, :], xT_in[k * 128:(k + 1) * 128, :], [bdram_in], bxb[k])
    P.dma('pool', memb[:], memT.rearrange("(k p) m -> p k m", p=128), [], bmemb)
    wt, bw = C.load_w(w_kv, 0, 1024, 0, 512)
    for hd in range(4):
        ps, bps = C.next_ps()
        for k in range(8):
            MM(P, ps[:, 0:256], wt[:, k, hd * 128:(hd + 1) * 128], memb[:, k, :], k == 0, k == 7, [bw, bmemb], bps)
        I(P, 'act', 'copy', [bps], [bkmT], out=kmT[:, hd, :], in_=ps[:, 0:256])
    wt, bw = C.load_w(w_kv, 0, 1024, 512, 512)
    for mt in range(2):
        ps, bps = C.next_ps()
        for k in range(8):
            MM(P, ps[:], memb[:, k, mt * 128:(mt + 1) * 128], wt[:, k, :], k == 0, k == 7, [bw, bmemb], bps)
        I(P, 'act', 'copy', [bps], [bvm], out=vm[:, mt, :], in_=ps[:])

    stg32 = C.ring("P_s32", 3, [128, TT], F32)
    stgb = C.ring("P_sb", 3, [128, TT], BF16)
    scr32 = C.ring("P_c32", 3, [128, TT], F32)

    def proj_fm(c0, ncols, post):
        done = 0
        while done < ncols:
            g = min(512, ncols - done)
            wt, bw = C.load_w(w_in, 0, 1024, c0 + done, g)
            for oc in range((g + 127) // 128):
                nrow = min(128, g - oc * 128)
                for tt in range(ntt):
                    ps, bps = C.next_ps()
                    for k in range(8):
                        MM(P, ps[0:nrow, :], wt[:, k, oc * 128:oc * 128 + nrow], xb[:, k, tt * TT:(tt + 1) * TT], k == 0, k == 7, [bw, bxb[k]], bps)
                    post(ps, bps, (done + oc * 128) // 128, nrow, tt)
            done += g

    def proj_tm(c0, ncols, dst):
        for g0 in range(0, ncols, 512):
            wt, bw = C.load_w(w_in, 0, 1024, c0 + g0, 512)
            for t in range(T // 128):
                ps, bps = C.next_ps()
                for k in range(8):
                    MM(P, ps[:], xb[:, k, t * 128:(t + 1) * 128], wt[:, k, :], k == 0, k == 7, [bw, bxb[k]], bps)
                s, bs = stgb()
                I(P, 'act', 'copy', [bps], [bs], out=s[:], in_=ps[:])
                P.dma('sp', dst[t * 128:(t + 1) * 128, g0:g0 + 512], s[:], [bs], bdram_out)

    def store_fm(dst, dt_ring):
        def post(ps, bps, ci, nrow, tt, func=None):
            s, bs = dt_ring()
            I(P, 'act', 'activation', [bps], [bs], out=s[0:nrow, :], in_=ps[0:nrow, :], func=(func or AF.Copy))
            P.dma('sp', dst[ci * 128:ci * 128 + nrow, tt * TT:(tt + 1) * TT], s[0:nrow, :], [bs], bdram_out)
        return post

    def post_act(dst, func, ring):
        base = store_fm(dst, ring)
        return lambda ps, bps, ci, nrow, tt: base(ps, bps, ci, nrow, tt, func=func)

    gp = C.sb("P_gp", [16, 4], F32)
    bgp = C.buf("gp")

    def gates_post(spec):
        def post(ps, bps, ci, nrow, tt):
            spec(ps, bps, tt)
        return post

    if kind == 'gdn':
        proj_fm(0, 3072, store_fm(outs['qkvT'], stg32))
        proj_fm(3072, 1024, post_act(outs['szT'], AF.Silu, stg32))
        P.dma('sp', gp[0:8, 0:1], prm['a_log'].rearrange("(h o) -> h o", o=1), [], bgp)
        P.dma('sp', gp[0:8, 1:2], prm['dt_bias'].rearrange("(h o) -> h o", o=1), [], bgp)
        I(P, 'act', 'activation', [bgp], [bgp], out=gp[0:8, 2:3], in_=gp[0:8, 0:1], func=AF.Exp)
        I(P, 'dve', 'tensor_scalar', [bgp], [bgp], out=gp[0:8, 2:3], in0=gp[0:8, 2:3], scalar1=-1.0, scalar2=None, op0=ALU.mult)

        def gspec(ps, bps, tt):
            s, bs = scr32()
            I(P, 'act', 'activation', [bps, bgp], [bs], out=s[0:8, :], in_=ps[0:8, :], func=AF.Exp, bias=gp[0:8, 1:2], scale=1.0)
            I(P, 'act', 'activation', [bs], [bs], out=s[0:8, :], in_=s[0:8, :], func=AF.Ln, bias=1.0, scale=1.0)
            I(P, 'dve', 'tensor_scalar', [bs, bgp], [bs], out=s[0:8, :], in0=s[0:8, :], scalar1=gp[0:8, 2:3], scalar2=None, op0=ALU.mult)
            P.dma('sp', outs['gT'][:, tt * TT:(tt + 1) * TT], s[0:8, :], [bs], bdram_out)
            s2, bs2 = scr32()
            I(P, 'act', 'activation', [bps], [bs2], out=s2[0:16, :], in_=ps[0:16, :], func=AF.Sigmoid)
            P.dma('sp', outs['betaT'][:, tt * TT:(tt + 1) * TT], s2[8:16, :], [bs2], bdram_out)
        proj_fm(4096, 16, gates_post(gspec))
        memq0 = 4112
    elif kind == 'mlstm':
        proj_fm(0, 512, store_fm(outs['qT'], stg32))
        proj_fm(512, 512, store_fm(outs['kT'], stg32))
        proj_tm(1024, 1024, outs['v'])
        proj_fm(2048, 1024, post_act(outs['sogT'], AF.Sigmoid, stg32))
        P.dma('sp', gp[0:8, 0:1], prm['b_gate'][0, :].rearrange("(h o) -> h o", o=1), [], bgp)
        P.dma('sp', gp[8:16, 0:1], prm['b_gate'][1, :].rearrange("(h o) -> h o", o=1), [], bgp)
        I(P, 'dve', 'tensor_scalar', [bgp], [bgp], out=gp[0:16, 1:2], in0=gp[0:16, 0:1], scalar1=-1.0, scalar2=None, op0=ALU.mult)

        def gspec(ps, bps, tt):
            s, bs = scr32()
            I(P, 'act', 'activation', [bps, bgp], [bs], out=s[0:16, :], in_=ps[0:16, :], func=AF.Identity, bias=gp[0:16, 0:1], scale=1.0)
            P.dma('sp', outs['ilogT'][:, tt * TT:(tt + 1) * TT], s[0:8, :], [bs], bdram_out)
            s2, bs2 = scr32()
            I(P, 'act', 'activation', [bps, bgp], [bs2], out=s2[0:16, :], in_=ps[0:16, :], func=AF.Exp, bias=gp[0:16, 1:2], scale=-1.0)
            I(P, 'act', 'activation', [bs2], [bs2], out=s2[0:16, :], in_=s2[0:16, :], func=AF.Ln, bias=1.0, scale=1.0)
            I(P, 'dve', 'tensor_scalar', [bs2], [bs2], out=s2[0:16, :], in0=s2[0:16, :], scalar1=-1.0, scalar2=None, op0=ALU.mult)
            P.dma('sp', outs['flogT'][:, tt * TT:(tt + 1) * TT], s2[8:16, :], [bs2], bdram_out)
        proj_fm(3072, 16, gates_post(gspec))
        memq0 = 3088
    else:
        blk = C.sb("P_blk", [128, 128], F32)
        qkg = C.sb("P_qkg", [128, 2], F32)
        bblk = C.buf("blk")
        I(P, 'pool', 'memset', [], [bblk], blk[:], 0.0)
        I(P, 'pool', 'memset', [bblk], [bblk], blk[0:64, 0:64], 1.0 / 64)
        I(P, 'pool', 'memset', [bblk], [bblk], blk[64:128, 64:128], 1.0 / 64)
        for j in range(2):
            for hh in range(2):
                P.dma('sp', qkg[hh * 64:(hh + 1) * 64, j:j + 1], prm['qk_g'][j, :].rearrange("(d o) -> d o", o=1), [], bblk)
        I(P, 'dve', 'tensor_scalar', [bblk], [bblk], out=qkg[:, 0:1], in0=qkg[:, 0:1], scalar1=64 ** -0.5, scalar2=None, op0=ALU.mult)

        def rms_post(dst, j):
            def post(ps, bps, ci, nrow, tt):
                raw, braw = scr32()
                sq, bsq = scr32()
                I(P, 'act', 'copy', [bps], [braw], out=raw[:], in_=ps[:])
                I(P, 'act', 'activation', [bps], [bsq], out=sq[:], in_=ps[:], func=AF.Square)
                ps2, bps2 = C.next_ps()
                MM(P, ps2[:], blk[:], sq[:], True, True, [bblk, bsq], bps2)
                I(P, 'act', 'activation', [bps2, C.bconst], [bsq], out=sq[:], in_=ps2[:], func=AF.Ln, bias=C.eps_n[:, 0:1], scale=1.0)
                I(P, 'act', 'activation', [bsq], [bsq], out=sq[:], in_=sq[:], func=AF.Exp, scale=-0.5)
                s, bs = stgb()
                I(P, 'dve', 'scalar_tensor_tensor', [braw, bsq, bblk], [bs], out=s[:], in0=raw[:], scalar=qkg[:, j:j + 1], in1=sq[:], op0=ALU.mult, op1=ALU.mult)
                P.dma('sp', dst[ci * 128:(ci + 1) * 128, tt * TT:(tt + 1) * TT], s[:], [bs], bdram_out)
            return post
        proj_fm(0, 1024, rms_post(outs['qT'], 0))
        proj_fm(1024, 1024, rms_post(outs['kT'], 1))
        proj_tm(2048, 1024, outs['v'])
        proj_fm(3072, 1024, post_act(outs['sogT'], AF.Sigmoid, stg32))
        P.dma('sp', gp[0:16, 0:1], prm['b_f'].rearrange("(h o) -> h o", o=1), [], bgp)
        I(P, 'dve', 'tensor_scalar', [bgp], [bgp], out=gp[0:16, 1:2], in0=gp[0:16, 0:1], scalar1=-1.0, scalar2=None, op0=ALU.mult)

        def gspec(ps, bps, tt):
            s2, bs2 = scr32()
            I(P, 'act', 'activation', [bps, bgp], [bs2], out=s2[0:16, :], in_=ps[0:16, :], func=AF.Exp, bias=gp[0:16, 1:2], scale=-1.0)
            I(P, 'act', 'activation', [bs2], [bs2], out=s2[0:16, :], in_=s2[0:16, :], func=AF.Ln, bias=1.0, scale=1.0)
            I(P, 'dve', 'tensor_scalar', [bs2], [bs2], out=s2[0:16, :], in0=s2[0:16, :], scalar1=-1.0, scalar2=None, op0=ALU.mult)
            P.dma('sp', outs['flogT'][:, tt * TT:(tt + 1) * TT], s2[0:16, :], [bs2], bdram_out)
        proj_fm(4096, 16, gates_post(gspec))
        memq0 = 4112

    pT = C.ring("P_pT", 2, [128, 2, TT], BF16)

    for hd in range(4):
        wt, bw = C.load_w(w_in, 0, 1024, memq0 + hd * 128, 128)
        for tt in range(ntt):
            ps, bps = C.next_ps()
            for k in range(8):
                MM(P, ps[:], wt[:, k, :], xb[:, k, tt * TT:(tt + 1) * TT], k == 0, k == 7, [bw, bxb[k]], bps)
            q, bq = stgb()
            I(P, 'act', 'copy', [bps], [bq], out=q[:], in_=ps[:])
            p, bp = pT()
            for mt in range(2):
                ps2, bps2 = C.next_ps()
                MM(P, ps2[:], kmT[:, hd, mt * 128:(mt + 1) * 128], q[:], True, True, [bkmT, bq], bps2)
                I(P, 'act', 'activation', [bps2], [bp], out=p[:, mt, :], in_=ps2[:], func=AF.Exp, scale=MEM_SCALE)
            po, bpo = C.next_ps()
            pd, bpd = C.next_ps()
            for mt in range(2):
                MM(P, po[:], vm[:, mt, hd * 128:(hd + 1) * 128], p[:, mt, :], mt == 0, mt == 1, [bvm, bp], bpo)
            for mt in range(2):
                MM(P, pd[:], C.onesb[:], p[:, mt, :], mt == 0, mt == 1, [C.bconst, bp], bpd)
            r, br = scr32()
            I(P, 'dve', 'reciprocal', [bpd], [br], out=r[:], in_=pd[:])
            s, bs = stgb()
            I(P, 'dve', 'tensor_tensor', [bpo, br], [bs], out=s[:], in0=po[:], in1=r[:], op=ALU.mult)
            P.dma('sp', outs['memoT'][hd * 128:(hd + 1) * 128, tt * TT:(tt + 1) * TT], s[:], [bs], bdram_out)


def consts_attn(C):
    P = C.P
    if 'tri01' in C.pcache:
        return
    tri = C.sbp("tri01", [128, 128], BF16)
    onesrow = C.sbp("onesrow", [1, 1024], F32)
    negone = C.sbp("negone", [1, 1], F32)
    b = C.bconst
    I(P, 'pool', 'memset', [], [b], tri[:], 1.0)
    I(P, 'pool', 'affine_select', [b], [b], out=tri[:], in_=tri[:], pattern=[[1, 128]], compare_op=ALU.is_ge, fill=0.0, base=0, channel_multiplier=-1)
    I(P, 'pool', 'memset', [], [b], onesrow[:], 1.0)
    I(P, 'pool', 'memset', [], [b], negone[:], -1.0)
    C.tri01, C.onesrow, C.negone = tri, onesrow, negone


def cumsum_row(C, dst_row, src_row, n, rb, wb):
    I(C.P, 'dve', 'tensor_tensor_scan', rb + [C.bconst], wb, out=dst_row, data0=C.onesrow[0:1, 0:n], data1=src_row, initial=0.0, op0=ALU.mult, op1=ALU.add)


def cumsum_dram_row(C, src, S, crow, bcrow, stg_r, bdram_in):
    P = C.P
    PC = min(1024, S)
    for pc in range(S // PC):
        sg, bsg = stg_r()
        P.dma('sp', sg[0:1, 0:PC], src[:, pc * PC:(pc + 1) * PC], [bdram_in], bsg)
        init = 0.0 if pc == 0 else crow[0:1, pc * PC - 1:pc * PC]
        I(P, 'dve', 'tensor_tensor_scan', [bsg, bcrow, C.bconst], [bcrow], out=crow[0:1, pc * PC:(pc + 1) * PC], data0=C.onesrow[0:1, 0:PC], data1=sg[0:1, 0:PC],
          initial=init, op0=ALU.mult, op1=ALU.add)


def attn_loop(C, S, KA, bKA, QA, bQA, kdim, VA, bVA, vcols, negc, E, bside, nacc, den_ones, epilogue, LA=2, ED=3, score_banks=(0, 1, 2)):
    P = C.P
    nq = S // 512
    pT = C.ring("A_pT", 5, [128, 512], BF16)
    C.set_pool('score', list(score_banks))
    blocks = []
    for qi in range(nq):
        nk = 4 * qi + 4
        for kt in range(nk):
            blocks.append((qi, kt, nk))
    sc = {}

    def rec_score(n):
        qi, kt, nk = blocks[n]
        j = kt - 4 * qi
        c0 = 128 * j if j > 0 else 0
        ps, bps = C.next_ps('score')
        MM(P, ps[:, c0:512], KA[0:kdim, kt * 128:(kt + 1) * 128], QA[0:kdim, qi * 512 + c0:(qi + 1) * 512], True, True, [bKA, bQA], bps)
        sc[n] = (ps, bps, c0, j)

    pend = []
    for n in range(min(LA, len(blocks))):
        rec_score(n)
    for n in range(len(blocks)):
        if n + LA < len(blocks):
            rec_score(n + LA)
        qi, kt, nk = blocks[n]
        ps, bps, c0, j = sc.pop(n)
        accs = [C.acc((qi % 2) * nacc + i) for i in range(nacc)]
        p, bp = pT()
        if negc is not None:
            I(P, 'act', 'activation', [bps, bside], [bp], out=p[:, c0:512], in_=ps[:, c0:512], func=AF.Exp, bias=negc[:, kt:kt + 1], scale=1.0)
        else:
            I(P, 'act', 'activation', [bps, bside], [bp], out=p[:, c0:512], in_=ps[:, c0:512], func=AF.Copy, scale=E[:, qi * (S // 128) + kt:qi * (S // 128) + kt + 1])
        if j >= 0:
            I(P, 'pool', 'tensor_tensor', [bp, C.bconst], [bp], out=p[:, c0:c0 + 128], in0=p[:, c0:c0 + 128], in1=C.tri01[:], op=ALU.mult)
        MM(P, accs[0][0][0:vcols, c0:512], VA[:, kt, :], p[:, c0:512], kt == 0, kt == nk - 1, [bVA, bp], accs[0][1])
        if den_ones:
            MM(P, accs[1][0][:, c0:512], C.onesb[:], p[:, c0:512], kt == 0, kt == nk - 1, [C.bconst, bp], accs[1][1])
        for e in pend:
            e[0] -= 1
        while pend and pend[0][0] <= 0:
            g = pend.pop(0)[1]
            for _ in g:
                pass
        if kt == nk - 1:
            g = epilogue(qi, accs)
            next(g)
            pend.append([ED, g])
    for e in pend:
        for _ in e[1]:
            pass


def mixer_fox(C, S, nh, qT, kT, v, sogT, flogT, mixT, bdram_in, bdram_out):
    P = C.P
    consts_attn(C)
    C.nring = 4
    C.set_pool('misc', [6, 7])
    KA = [C.sb("X_KA%d" % i, [65, S], BF16) for i in range(2)]
    QA = [C.sb("X_QA%d" % i, [65, S], BF16) for i in range(2)]
    VA = [C.sb("X_VA%d" % i, [128, S // 128, 65], BF16) for i in range(2)]
    crow = C.sb("X_crow", [1, S], F32)
    cb = C.sb("X_cb", [1, S], BF16)
    negc = [C.sb("X_negc%d" % i, [128, S // 128], F32) for i in range(2)]
    bKA = [C.buf("X_KA%d" % i) for i in range(2)]
    bQA = [C.buf("X_QA%d" % i) for i in range(2)]
    bVA = [C.buf("X_VA%d" % i) for i in range(2)]
    bcrow = C.buf("X_crow")
    bneg = [C.buf("X_negc%d" % i) for i in range(2)]
    stg_r = C.ring("X_stg", 2, [1, 1024], F32)
    sog_r = C.ring("X_sog", 3, [64, 512], F32)
    o_r = C.ring("X_o", 3, [65, 512], F32)
    ob_r = C.ring("X_ob", 2, [64, 512], BF16)
    for h in range(nh):
        i = h % 2
        P.dma('sp', KA[i][0:64, :], kT[h * 64:(h + 1) * 64, :], [bdram_in], bKA[i])
        I(P, 'pool', 'memset', [bKA[i]], [bKA[i]], KA[i][64:65, :], 1.0)
        P.dma('sp', QA[i][0:64, :], qT[h * 64:(h + 1) * 64, :], [bdram_in], bQA[i])
        P.dma('sp', VA[i][:, :, 0:64], v[:, h * 64:(h + 1) * 64].rearrange("(t p) d -> p t d", p=128), [bdram_in], bVA[i])
        I(P, 'pool', 'memset', [bVA[i]], [bVA[i]], VA[i][:, :, 64:65], 1.0)
        cumsum_dram_row(C, flogT[h:h + 1, :], S, crow, bcrow, stg_r, bdram_in)
        I(P, 'dve', 'tensor_copy', [bcrow], [bcrow], out=cb[:], in_=crow[:])
        P.dma('sp', QA[i][64:65, :], cb[:], [bcrow], bQA[i])
        ps, bps = C.next_ps()
        for t in range(S // 128):
            MM(P, ps[:, t:t + 1], crow[0:1, t * 128:(t + 1) * 128], C.negone[0:1, 0:1], True, True, [bcrow, C.bconst], bps)
        I(P, 'dve', 'tensor_copy', [bps], [bneg[i]], out=negc[i][:], in_=ps[:, 0:S // 128])

        def epi(qi, accs, h=h, i=i):
            po, bpo = accs[0]
            o, bo = o_r()
            I(P, 'dve', 'reciprocal', [bpo], [bo], out=o[64:65, :], in_=po[64:65, :])
            I(P, 'act', 'copy', [bpo], [bo], out=o[0:64, :], in_=po[0:64, :])
            sg, bsg = sog_r()
            P.dma('sp', sg[:], sogT[h * 64:(h + 1) * 64, qi * 512:(qi + 1) * 512], [bdram_in], bsg)
            yield
            pb, bpb = C.next_ps('misc')
            MM(P, pb[0:64, :], C.ones32[64:65, 0:64], o[64:65, :], True, True, [C.bconst, bo], bpb)
            I(P, 'dve', 'tensor_tensor', [bo, bpb], [bo], out=o[0:64, :], in0=o[0:64, :], in1=pb[0:64, :], op=ALU.mult)
            ob, bob = ob_r()
            I(P, 'dve', 'tensor_tensor', [bo, bsg], [bob], out=ob[:], in0=o[0:64, :], in1=sg[:], op=ALU.mult)
            P.dma('sp', mixT[h * 64:(h + 1) * 64, qi * 512:(qi + 1) * 512], ob[:], [bob], bdram_out)
            yield
        attn_loop(C, S, KA[i], bKA[i], QA[i], bQA[i], 65, VA[i], bVA[i], 65, negc[i], None, bneg[i], 1, False, epi, LA=3, ED=4, score_banks=(0, 1, 2, 3))
    C.nring = 8


def mixer_mlstm(C, S, nh, qT, kT, v, sogT, ilogT, flogT, ng, mixT, bdram_in, bdram_out):
    P = C.P
    consts_attn(C)
    C.nring = 4
    C.set_pool('misc', [3])
    nq, nkb = S // 512, S // 128
    KA = [C.sb("X_KA%d" % i, [65, S], BF16) for i in range(2)]
    QA = [C.sb("X_QA%d" % i, [65, S], BF16) for i in range(2)]
    VA = [C.sb("M_VA%d" % i, [128, nkb, 128], BF16) for i in range(2)]
    Brow = C.sb("M_Brow", [1, S], F32)
    xrow = C.sb("M_xrow", [1, 1024], F32)
    bBrow = C.buf("M_Brow")
    stg_r = C.ring("M_stg", 2, [1, 1024], F32)
    f_r = C.ring("M_f", 4, [1, 512], F32)
    refs = [C.sb("M_refs%d" % i, [1, 128], F32) for i in range(2)]
    E = [C.sb("M_E%d" % i, [128, 16 * 64], F32) for i in range(2)]
    gcol = C.sb("M_g", [128, 8], F32)
    avg = C.sb("M_avg", [128, 128], F32)
    bKA = [C.buf("X_KA%d" % i) for i in range(2)]
    bQA = [C.buf("X_QA%d" % i) for i in range(2)]
    bVA = [C.buf("M_VA%d" % i) for i in range(2)]
    bE = [C.buf("M_E%d" % i) for i in range(2)]
    bg = C.buf("M_g")
    I(P, 'pool', 'memset', [], [bg], avg[:], 1.0 / 128)
    P.dma('sp', gcol[:, 0:nh], ng.rearrange("(h d) -> d h", d=128), [], bg, allow_slow_non_contiguous=True)
    ld_r = C.ring("M_ld", 2, [64, 512], F32)
    sog_r = C.ring("M_sog", 3, [128, 512], F32)
    w_r = C.ring("M_w", 5, [128, 512], F32)
    ob_r = C.ring("M_ob", 2, [128, 512], BF16)
    for h in range(nh):
        i = h % 2
        cumsum_dram_row(C, flogT[h:h + 1, :], S, Brow, bBrow, stg_r, bdram_in)
        B3q = Brow[0:1, :].rearrange("p (t c) -> p t c", c=512)
        B3k = Brow[0:1, :].rearrange("p (t c) -> p t c", c=128)
        brf = bE[i]
        I(P, 'dve', 'tensor_copy', [bBrow], [brf], out=refs[i][0:1, 0:nq], in_=B3q[:, :, 0])
        I(P, 'dve', 'tensor_copy', [bBrow], [brf], out=refs[i][0:1, 64:64 + nkb], in_=B3k[:, :, 0])
        X3 = xrow[0:1, 0:nq * nkb].rearrange("p (a b) -> p a b", b=nkb)
        I(P, 'dve', 'tensor_tensor', [brf], [bBrow], out=X3, in0=refs[i][0:1, 0:nq].unsqueeze(2).to_broadcast([1, nq, nkb]),
          in1=refs[i][0:1, 64:64 + nkb].unsqueeze(1).to_broadcast([1, nq, nkb]), op=ALU.subtract)
        I(P, 'dve', 'tensor_scalar', [bBrow], [bBrow], out=xrow[0:1, 0:nq * nkb], in0=xrow[0:1, 0:nq * nkb], scalar1=60.0, scalar2=None, op0=ALU.min)
        for c in range((nq * nkb + 511) // 512):
            n = min(512, nq * nkb - c * 512)
            ps, bps = C.next_ps()
            MM(P, ps[:, 0:n], C.ones32[0:1, :], xrow[0:1, c * 512:c * 512 + n], True, True, [C.bconst, bBrow], bps)
            I(P, 'act', 'activation', [bps], [bE[i]], out=E[i][:, c * 512:c * 512 + n], in_=ps[:, 0:n], func=AF.Exp)
        for c in range(S // 512):
            fq, bfq = f_r()
            I(P, 'dve', 'tensor_scalar', [bBrow, brf], [bfq], out=fq[:], in0=Brow[0:1, c * 512:(c + 1) * 512], scalar1=refs[i][0:1, c:c + 1], scalar2=None, op0=ALU.subtract)
            I(P, 'act', 'activation', [bfq], [bfq], out=fq[:], in_=fq[:], func=AF.Exp)
            fk, bfk = f_r()
            P.dma('sp', fk[:], ilogT[h:h + 1, c * 512:(c + 1) * 512], [bdram_in], bfk)
            I(P, 'dve', 'tensor_tensor', [bfk, bBrow], [bfk], out=fk[:], in0=fk[:], in1=Brow[0:1, c * 512:(c + 1) * 512], op=ALU.subtract)
            I(P, 'dve', 'tensor_tensor', [bfk, brf], [bfk], out=fk[:].rearrange("p (t c) -> p t c", c=128), in0=fk[:].rearrange("p (t c) -> p t c", c=128),
              in1=refs[i][0:1, 64 + c * 4:64 + c * 4 + 4].unsqueeze(2).to_broadcast([1, 4, 128]), op=ALU.add)
            I(P, 'act', 'activation', [bfk], [bfk], out=fk[:], in_=fk[:], func=AF.Exp)
            for (src, dstA, bdst, frow, bfrow, sc) in ((qT, QA[i], bQA[i], fq, bfq, 0.125), (kT, KA[i], bKA[i], fk, bfk, 1.0)):
                ld, bld = ld_r()
                P.dma('sp', ld[:], src[h * 64:(h + 1) * 64, c * 512:(c + 1) * 512], [bdram_in], bld)
                ps, bps = C.next_ps()
                MM(P, ps[0:64, :], C.ones32[0:1, 0:64], frow[:], True, True, [C.bconst, bfrow], bps)
                I(P, 'dve', 'scalar_tensor_tensor', [bld, bps], [bdst], out=dstA[0:64, c * 512:(c + 1) * 512], in0=ld[:], scalar=sc, in1=ps[0:64, :], op0=ALU.mult, op1=ALU.mult)
        P.dma('sp', VA[i][:], v[:, h * 128:(h + 1) * 128].rearrange("(t p) d -> p t d", p=128), [bdram_in], bVA[i])

        def epi(qi, accs, h=h, i=i):
            (pn, bpn), (pd, bpd) = accs
            r, br = w_r()
            I(P, 'act', 'activation', [bpd], [br], out=r[:], in_=pd[:], func=AF.Abs)
            I(P, 'dve', 'tensor_scalar', [br], [br], out=r[:], in0=r[:], scalar1=1.0, scalar2=None, op0=ALU.max)
            I(P, 'dve', 'reciprocal', [br], [br], out=r[:], in_=r[:])
            hh, bh = w_r()
            I(P, 'dve', 'tensor_tensor', [bpn, br], [bh], out=hh[:], in0=pn[:], in1=r[:], op=ALU.mult)
            sq, bsq = w_r()
            I(P, 'act', 'activation', [bh], [bsq], out=sq[:], in_=hh[:], func=AF.Square)
            sg, bsg = sog_r()
            P.dma('sp', sg[:], sogT[h * 128:(h + 1) * 128, qi * 512:(qi + 1) * 512], [bdram_in], bsg)
            yield
            pm, bpm = C.next_ps('misc')
            MM(P, pm[:, 0:256], avg[:], hh[:, 0:256], True, True, [bg, bh], bpm)
            MM(P, pm[:, 256:512], avg[:], hh[:, 256:512], True, True, [bg, bh], bpm)
            mean, bmean = w_r()
            I(P, 'act', 'copy', [bpm], [bmean], out=mean[:], in_=pm[:])
            pv, bpv = C.next_ps('misc')
            MM(P, pv[:], avg[:], sq[:], True, True, [bg, bsq], bpv)
            I(P, 'act', 'activation', [bmean], [br], out=r[:], in_=mean[:], func=AF.Square)
            I(P, 'dve', 'tensor_tensor', [bpv, br], [br], out=r[:], in0=pv[:], in1=r[:], op=ALU.subtract)
            I(P, 'act', 'activation', [br, C.bconst], [br], out=r[:], in_=r[:], func=AF.Ln, bias=C.eps_n[:, 0:1], scale=1.0)
            I(P, 'act', 'activation', [br], [br], out=r[:], in_=r[:], func=AF.Exp, scale=-0.5)
            I(P, 'dve', 'tensor_tensor', [bh, bmean], [bh], out=hh[:], in0=hh[:], in1=mean[:], op=ALU.subtract)
            I(P, 'dve', 'scalar_tensor_tensor', [bh, br, bg], [bh], out=hh[:], in0=hh[:], scalar=gcol[:, h:h + 1], in1=r[:], op0=ALU.mult, op1=ALU.mult)
            ob, bob = ob_r()
            I(P, 'dve', 'tensor_tensor', [bh, bsg], [bob], out=ob[:], in0=hh[:], in1=sg[:], op=ALU.mult)
            P.dma('sp', mixT[h * 128:(h + 1) * 128, qi * 512:(qi + 1) * 512], ob[:], [bob], bdram_out)
            yield
        attn_loop(C, S, KA[i], bKA[i], QA[i], bQA[i], 64, VA[i], bVA[i], 128, None, E[i], bE[i], 2, True, epi)
    C.nring = 8


GC = 64
GB = 8


def consts_gdn(C):
    P = C.P
    if 'g_tri' in C.pcache:
        return
    b = C.bconst
    tri = C.sbp("g_tri", [64, 64], F32)
    idn = C.sbp("g_idn", [64, 64], F32)
    off = C.sbp("g_off", [64, 64], F32)
    mS = C.sbp("g_mS", [64, GB, 64], F32)
    mI = C.sbp("g_mI", [64, GB, 64], F32)
    idb = C.sbp("g_idb", [128, 128], BF16)
    offb = C.sbp("g_offb", [64, 64], BF16)
    mSb = C.sbp("g_mSb", [64, GB, 64], BF16)
    mIb = C.sbp("g_mIb", [64, GB, 64], BF16)
    avgb = C.sbp("g_avgb", [128, 128], BF16)
    avg = C.sbp("g_avg", [128, 128], F32)
    for t, op in ((tri, ALU.is_ge), (idn, ALU.is_equal), (off, ALU.not_equal)):
        I(P, 'pool', 'memset', [], [b], t[:], 1.0)
        I(P, 'pool', 'affine_select', [b], [b], out=t[:], in_=t[:], pattern=[[1, 64]], compare_op=op, fill=0.0, base=0, channel_multiplier=-1)
    I(P, 'pool', 'memset', [], [b], mS[:], 0.0)
    I(P, 'pool', 'affine_select', [b], [b], out=mS[:], in_=mS[:], pattern=[[0, GB], [-1, 64]], compare_op=ALU.is_gt, fill=-1.0e4, base=0, channel_multiplier=1)
    I(P, 'pool', 'memset', [], [b], mI[:], 0.0)
    I(P, 'pool', 'affine_select', [b], [b], out=mI[:], in_=mI[:], pattern=[[0, GB], [1, 64]], compare_op=ALU.is_ge, fill=-1.0e4, base=0, channel_multiplier=-1)
    I(P, 'pool', 'memset', [], [b], idb[:], 1.0)
    I(P, 'pool', 'affine_select', [b], [b], out=idb[:], in_=idb[:], pattern=[[1, 128]], compare_op=ALU.is_equal, fill=0.0, base=0, channel_multiplier=-1)
    I(P, 'pool', 'memset', [], [b], avg[:], 1.0 / 128)
    I(P, 'pool', 'tensor_copy', [b], [b], out=offb[:], in_=off[:])
    I(P, 'pool', 'tensor_copy', [b], [b], out=mSb[:], in_=mS[:])
    I(P, 'pool', 'tensor_copy', [b], [b], out=mIb[:], in_=mI[:])
    I(P, 'pool', 'memset', [], [b], avgb[:], 1.0 / 128)
    C.g_tri, C.g_idn, C.g_off, C.g_mS, C.g_mI, C.g_idb, C.g_avg = tri, idn, off, mS, mI, idb, avg
    C.g_offb, C.g_mSb, C.g_mIb, C.g_avgb = offb, mSb, mIb, avgb


def mixer_gdn(C, S, heads, qkvT, conv_w, szT, gT, betaT, norm_g, mixT, bdram_in, bdram_out):
    P = C.P
    consts_gdn(C)
    C.nring = 4
    NCH = S // GC
    NB = S // (GC * GB)
    nh = len(heads)
    CW = min(1024, S)
    ngl = C.sb("G_ng", [128, 1], F32)
    bng = C.buf("G_ng")
    P.dma('sp', ngl[:], norm_g.rearrange("(d o) -> d o", o=1), [], bng)
    cst = C.ring("G_cst", 2, [128, CW + 3], F32)
    cac = C.ring("G_cac", 2, [128, CW], F32)
    csq = C.ring("G_csq", 1, [128, 512], F32)
    grow = C.ring("G_grow", 1, [1, 512], F32)
    HT = []
    for hi in range(2):
        d = {}
        for nm in ('qb', 'kb', 'vb'):
            d[nm] = C.sb("G_%s%d" % (nm, hi), [128, S], BF16)
            d['b' + nm] = C.buf("G_%s%d" % (nm, hi))
        d['cw'] = C.sb("G_cw%d" % hi, [128, 3, 4], F32)
        d['tm'] = C.sb("G_tm%d" % hi, [64, 8, NCH], F32)
        d['dl'] = C.sb("G_dl%d" % hi, [128, NCH], F32)
        d['S32'] = C.sb("G_S32_%d" % hi, [128, 128], F32)
        d['Sb'] = C.sb("G_Sb_%d" % hi, [128, 128], BF16)
        for nm in ('cw', 'tm', 'dl', 'S32', 'Sb'):
            d['b' + nm] = C.buf("G_%s%d" % (nm, hi))
        HT.append(d)

    def phaseA(hidx):
        hi = hidx % 2
        (q0, k0, v0, gr, o0) = heads[hidx]
        d = HT[hi]
        for xi, r0 in enumerate((q0, k0, v0)):
            P.dma('sp', d['cw'][:, xi, :], conv_w[:, r0:r0 + 128].rearrange("j c -> c j"), [], d['bcw'], allow_slow_non_contiguous=True)
        for xi, (r0, nm) in enumerate(((q0, 'qb'), (k0, 'kb'), (v0, 'vb'))):
            dst, bdst = d[nm], d['b' + nm]
            for cc in range(S // CW):
                st, bst = cst()
                if cc == 0:
                    I(P, 'pool', 'memset', [bst], [bst], st[:, 0:3], 0.0)
                    P.dma('sp', st[:, 3:3 + CW], qkvT[r0:r0 + 128, 0:CW], [bdram_in], bst)
                else:
                    P.dma('sp', st[:, 0:3 + CW], qkvT[r0:r0 + 128, cc * CW - 3:(cc + 1) * CW], [bdram_in], bst)
                ac, bac = cac()
                I(P, 'dve', 'tensor_scalar', [bst, d['bcw']], [bac], out=ac[:], in0=st[:, 3:3 + CW], scalar1=d['cw'][:, xi, 3:4], scalar2=None, op0=ALU.mult)
                for j in range(3):
                    I(P, 'dve', 'scalar_tensor_tensor', [bst, bac, d['bcw']], [bac], out=ac[:], in0=st[:, j:j + CW], scalar=d['cw'][:, xi, j:j + 1], in1=ac[:], op0=ALU.mult, op1=ALU.add)
                I(P, 'act', 'activation', [bac], [bac], out=ac[:], in_=ac[:], func=AF.Silu)
                if nm == 'vb':
                    I(P, 'act', 'copy', [bac], [bdst], out=dst[:, cc * CW:(cc + 1) * CW], in_=ac[:])
                else:
                    for t in range(CW // 512):
                        sq, bsq = csq()
                        I(P, 'act', 'activation', [bac], [bsq], out=sq[:], in_=ac[:, t * 512:(t + 1) * 512], func=AF.Square)
                        ps, bps = C.next_ps()
                        MM(P, ps[:], C.ones32[:], sq[:], True, True, [C.bconst, bsq], bps)
                        I(P, 'act', 'activation', [bps, C.bconst], [bsq], out=sq[:], in_=ps[:], func=AF.Ln, bias=C.eps_n[:, 0:1], scale=1.0)
                        I(P, 'act', 'activation', [bsq], [bsq], out=sq[:], in_=sq[:], func=AF.Exp, scale=-0.5)
                        sc = (128 ** -0.5) if nm == 'qb' else 1.0
                        I(P, 'dve', 'scalar_tensor_tensor', [bac, bsq], [bdst], out=dst[:, cc * CW + t * 512:cc * CW + (t + 1) * 512], in0=ac[:, t * 512:(t + 1) * 512], scalar=sc, in1=sq[:], op0=ALU.mult, op1=ALU.mult)
                yield
        tm = d['tm']
        PC = 512
        for w, src in enumerate((gT, betaT)):
            ps, bps = C.next_ps()
            for pc in range(S // PC):
                sg, bsg = grow()
                P.dma('sp', sg[0:1, 0:PC], src[gr:gr + 1, pc * PC:(pc + 1) * PC], [bdram_in], bsg)
                for nn in range(PC // 64):
                    col = pc * (PC // 64) + nn
                    MM(P, ps[0:64, col:col + 1], sg[0:1, nn * 64:(nn + 1) * 64], C.ones32[0:1, 0:1], True, True, [bsg, C.bconst], bps)
            I(P, 'dve', 'tensor_copy', [bps], [d['btm']], out=tm[:, w, :], in_=ps[0:64, 0:NCH])
        ps, bps = C.next_ps()
        MM(P, ps[0:64, 0:NCH], C.g_tri[:], tm[:, 0, :], True, True, [C.bconst, d['btm']], bps)
        I(P, 'dve', 'tensor_copy', [bps], [d['btm']], out=tm[:, 2, :], in_=ps[0:64, 0:NCH])
        ps2, bps2 = C.next_ps()
        MM(P, ps2[:, 0:NCH], C.ones32[0:64, :], tm[:, 0, :], True, True, [C.bconst, d['btm']], bps2)
        I(P, 'act', 'activation', [bps2], [d['bdl']], out=d['dl'][:], in_=ps2[:, 0:NCH], func=AF.Exp)
        I(P, 'act', 'activation', [d['btm']], [d['btm']], out=tm[:, 3, :], in_=tm[:, 2, :], func=AF.Exp)
        I(P, 'dve', 'tensor_scalar', [d['btm']], [d['btm']], out=tm[:, 5, :], in0=tm[:, 1, :], scalar1=-1.0, scalar2=None, op0=ALU.mult)
        I(P, 'dve', 'tensor_tensor', [d['btm']], [d['btm']], out=tm[:, 7, :], in0=tm[:, 3, :], in1=tm[:, 5, :], op=ALU.mult)
        I(P, 'dve', 'tensor_tensor', [d['btm'], bps2], [d['btm']], out=tm[:, 4, :], in0=ps2[0:64, 0:NCH], in1=tm[:, 2, :], op=ALU.subtract)
        I(P, 'act', 'activation', [d['btm']], [d['btm']], out=tm[:, 4, :], in_=tm[:, 4, :], func=AF.Exp)
        I(P, 'pool', 'memset', [], [d['bS32']], d['S32'][:], 0.0)
        yield
        if hidx == 0:
            C.dbg('d_qb', d['qb'][:, 0:512], [d['bqb']]); C.dbg('d_kb', d['kb'][:, 0:512], [d['bkb']]); C.dbg('d_vb', d['vb'][:, 0:512], [d['bvb']])
            C.dbg('d_tm', d['tm'][:, :, 0:8], [d['btm']]); C.dbg('d_dl', d['dl'][:, 0:8], [d['bdl']])

    f32r = C.ring("G_f32", 4, [64, 512], F32)
    b16r = C.ring("G_b16", 8, [64, 512], BF16)
    Xr = C.ring("G_X", 2, [64, 512], F32)
    gtr = C.ring("G_gtr", 2, [64, 512], F32)
    gtrb = C.ring("G_gtrb", 2, [64, 512], BF16)
    Rr = C.ring("G_R", 2, [64, GB, 256], BF16)
    kgr = C.ring("G_kg", 2, [64, GB, 128], BF16)
    UWr = C.ring("G_UW", 2, [64, GB, 256], BF16)
    atr = C.ring("G_at", 2, [64, 512], BF16)
    NTr = C.ring("G_NT", 2, [128, GB, 128], BF16)
    Qpr = C.ring("G_Qp", 2, [128, 512], BF16)
    q32r = C.ring("G_q32", 1, [128, 512], F32)
    o32r = C.ring("G_o32", 2, [128, 512], F32)
    sqbr = C.ring("G_sqb", 1, [128, 512], BF16)
    obr = C.ring("G_ob", 1, [128, 512], BF16)
    szr = C.ring("G_sz", 1, [128, 512], F32)
    Xbr = C.ring("G_Xb", 2, [64, 512], BF16)

    def bc(t, w, n0):
        return t[:, w, n0:n0 + GB].unsqueeze(2).to_broadcast([64, GB, 64])

    def v3(t):
        return t[:, :].rearrange("p (n x) -> p n x", x=64)

    def pre(hidx, b):
        hi = hidx % 2
        d = HT[hi]
        tm, btm = d['tm'], d['btm']
        n0 = b * GB
        tok = slice(b * 512, (b + 1) * 512)
        gtri, bgtri = gtr()
        I(P, 'dve', 'tensor_tensor', [btm, C.bconst], [bgtri], out=v3(gtri), in0=C.g_tri[:, :].unsqueeze(1).to_broadcast([64, GB, 64]), in1=bc(tm, 0, n0), op=ALU.mult)
        p1, bp1 = C.next_ps()
        MM(P, p1[0:64, :], C.ones32[0:64, 0:64], gtri[:], True, False, [C.bconst, bgtri], bp1)
        MM(P, p1[0:64, :], C.g_idb[0:64, 0:64], C.g_mIb[:].rearrange("p n x -> p (n x)"), False, True, [C.bconst], bp1)
        E2, bE2 = f32r()
        I(P, 'dve', 'tensor_tensor', [bp1, btm], [bE2], out=v3(E2), in0=p1[0:64, :].rearrange("p (n x) -> p n x", x=64), in1=bc(tm, 2, n0), op=ALU.subtract)
        I(P, 'act', 'activation', [bE2], [bE2], out=E2[:], in_=E2[:], func=AF.Exp)
        yield
        ngt, bngt = gtr()
        I(P, 'dve', 'tensor_scalar', [bgtri], [bngt], out=ngt[:], in0=gtri[:], scalar1=-1.0, scalar2=None, op0=ALU.mult)
        p2, bp2 = C.next_ps()
        MM(P, p2[0:64, :], C.ones32[0:64, 0:64], ngt[:], True, False, [C.bconst, bngt], bp2)
        MM(P, p2[0:64, :], C.g_idb[0:64, 0:64], C.g_mSb[:].rearrange("p n x -> p (n x)"), False, True, [C.bconst], bp2)
        E1, bE1 = f32r()
        I(P, 'dve', 'tensor_tensor', [bp2, btm], [bE1], out=v3(E1), in0=p2[0:64, :].rearrange("p (n x) -> p n x", x=64), in1=bc(tm, 2, n0), op=ALU.add)
        I(P, 'act', 'activation', [bE1], [bE1], out=E1[:], in_=E1[:], func=AF.Exp)
        yield
        bdg, bbdg = gtrb()
        I(P, 'dve', 'tensor_tensor', [btm, C.bconst], [bbdg], out=v3(bdg), in0=C.g_idn[:, :].unsqueeze(1).to_broadcast([64, GB, 64]), in1=bc(tm, 5, n0), op=ALU.mult)
        p3, bp3 = C.next_ps()
        MM(P, p3[0:64, :], C.g_offb[:], bdg[:], True, True, [C.bconst, bbdg], bp3)
        pk, bpk = C.next_ps()
        pq, bpq = C.next_ps()
        for n in range(GB):
            cs = slice(b * 512 + n * 64, b * 512 + (n + 1) * 64)
            MM(P, pk[0:64, n * 64:(n + 1) * 64], d['kb'][:, cs], d['kb'][:, cs], True, True, [d['bkb']], bpk)
        for n in range(GB):
            cs = slice(b * 512 + n * 64, b * 512 + (n + 1) * 64)
            MM(P, pq[0:64, n * 64:(n + 1) * 64], d['kb'][:, cs], d['qb'][:, cs], True, True, [d['bkb'], d['bqb']], bpq)
        N0t, bN0t = f32r()
        I(P, 'dve', 'tensor_tensor', [bpk, bE1], [bN0t], out=N0t[:], in0=pk[0:64, :], in1=E1[:], op=ALU.mult)
        N0, bN0 = b16r()
        I(P, 'dve', 'tensor_tensor', [bN0t, btm], [bN0], out=v3(N0), in0=v3(N0t), in1=bc(tm, 5, n0), op=ALU.mult)
        NT0t, bNT0t = f32r()
        I(P, 'dve', 'tensor_tensor', [bpk, bE2], [bNT0t], out=NT0t[:], in0=pk[0:64, :], in1=E2[:], op=ALU.mult)
        X, bX = Xr()
        I(P, 'dve', 'tensor_tensor', [bNT0t, bp3], [bX], out=X[:], in0=NT0t[:], in1=p3[0:64, :], op=ALU.mult)
        NT0, bNT0 = b16r()
        I(P, 'act', 'copy', [bX], [bNT0], out=NT0[:], in_=X[:])
        at, bat = atr()
        I(P, 'dve', 'tensor_tensor', [bpq, bE2], [bat], out=at[:], in0=pq[0:64, :], in1=E2[:], op=ALU.mult)
        if hi == 0 and b == 0:
            C.dbg('d_E1', E1[:], [bE1]); C.dbg('d_E2', E2[:], [bE2]);  C.dbg('d_at', at[:], [bat])
        yield
        I(P, 'dve', 'tensor_tensor', [bX, C.bconst], [bX], out=v3(X), in0=v3(X), in1=C.g_idn[:, :].unsqueeze(1).to_broadcast([64, GB, 64]), op=ALU.add)
        Xs, bXs = Xbr()
        I(P, 'act', 'copy', [bX], [bXs], out=Xs[:], in_=X[:])
        Pj, bPj, PTj, bPTj = N0, bN0, NT0, bNT0
        for lvl in range(1, 6):
            pa, bpa = C.next_ps()
            for n in range(GB):
                sl = slice(n * 64, (n + 1) * 64)
                MM(P, pa[0:64, sl], PTj[:, sl], Pj[:, sl], True, True, [bPTj, bPj], bpa)
            Pn, bPn = b16r()
            I(P, 'act', 'copy', [bpa], [bPn], out=Pn[:], in_=pa[0:64, :])
            if lvl < 5:
                pb_, bpb_ = C.next_ps()
                for n in range(GB):
                    sl = slice(n * 64, (n + 1) * 64)
                    MM(P, pb_[0:64, sl], Pj[:, sl], PTj[:, sl], True, True, [bPTj, bPj], bpb_)
                PTn, bPTn = b16r()
                I(P, 'act', 'copy', [bpb_], [bPTn], out=PTn[:], in_=pb_[0:64, :])
            px, bpx = C.next_ps()
            for n in range(GB):
                sl = slice(n * 64, (n + 1) * 64)
                MM(P, px[0:64, sl], Pn[:, sl], Xs[:, sl], True, True, [bPn, bXs], bpx)
            I(P, 'dve', 'tensor_tensor', [bpx, bX], [bX], out=X[:], in0=X[:], in1=px[0:64, :], op=ALU.add)
            Xs, bXs = Xbr()
            I(P, 'act', 'copy', [bX], [bXs], out=Xs[:], in_=X[:])
            Pj, bPj = Pn, bPn
            if lvl < 5:
                PTj, bPTj = PTn, bPTn
            yield
        if hi == 0 and b == 0:
            C.dbg('d_X', X[:], [bX])
        Xb, bXb = Xs, bXs
        Rt, bRt = Rr()
        kg, bkg = kgr()
        for (src, bsrc, which) in ((d['kb'], d['bkb'], 'k'), (d['vb'], d['bvb'], 'v')):
            for half in range(2):
                pt, bpt = C.next_ps()
                ptb = pt[:].bitcast(BF16)
                for n in range(4):
                    nn = half * 4 + n
                    cs = slice(b * 512 + nn * 64, b * 512 + (nn + 1) * 64)
                    P.mm(lambda e, o=ptb[0:64, n * 128:(n + 1) * 128], i_=src[:, cs]: e.transpose(o, i_, C.g_idb[:]), [bsrc, C.bconst], bpt)
                pv3 = ptb[0:64, 0:512].rearrange("p (n x) -> p n x", x=128)
                hs = slice(half * 4, half * 4 + 4)

                def bc4(w):
                    return tm[:, w, n0 + half * 4:n0 + half * 4 + 4].unsqueeze(2).to_broadcast([64, 4, 128])
                if which == 'k':
                    I(P, 'dve', 'tensor_tensor', [bpt, btm], [bRt], out=Rt[:, hs, 128:256], in0=pv3, in1=bc4(7), op=ALU.mult)
                    I(P, 'dve', 'tensor_tensor', [bpt, btm], [bkg], out=kg[:, hs, :], in0=pv3, in1=bc4(4), op=ALU.mult)
                else:
                    I(P, 'dve', 'tensor_tensor', [bpt, btm], [bRt], out=Rt[:, hs, 0:128], in0=pv3, in1=bc4(1), op=ALU.mult)
            yield
        UW, bUW = UWr()
        for pr in range(4):
            pu, bpu = C.next_ps()
            for n2 in range(2):
                n = pr * 2 + n2
                MM(P, pu[0:64, n2 * 256:(n2 + 1) * 256], Xb[:, n * 64:(n + 1) * 64], Rt[:, n, :], True, True, [bXb, bRt], bpu)
            I(P, 'act', 'copy', [bpu], [bUW], out=UW[:, pr * 2:pr * 2 + 2, :].rearrange("p n x -> p (n x)"), in_=pu[0:64, :])
        yield
        NT, bNT = NTr()
        for pr in range(2):
            pn, bpn = C.next_ps()
            for n4 in range(4):
                n = pr * 4 + n4
                MM(P, pn[:, n4 * 128:(n4 + 1) * 128], UW[:, n, 128:256], kg[:, n, :], True, True, [bUW, bkg], bpn)
            I(P, 'act', 'copy', [bpn], [bNT], out=NT[:, pr * 4:pr * 4 + 4, :].rearrange("p n x -> p (n x)"), in_=pn[:])
        yield
        edd, bedd = gtrb()
        I(P, 'dve', 'tensor_tensor', [btm, C.bconst], [bedd], out=v3(edd), in0=C.g_idn[:, :].unsqueeze(1).to_broadcast([64, GB, 64]), in1=bc(tm, 3, n0), op=ALU.mult)
        pe_, bpe_ = C.next_ps()
        MM(P, pe_[:], C.onesb[0:64, :], edd[:], True, True, [C.bconst, bedd], bpe_)
        q32, bq32 = q32r()
        I(P, 'dve', 'tensor_tensor', [d['bqb'], bpe_], [bq32], out=q32[:], in0=d['qb'][:, tok], in1=pe_[:], op=ALU.mult)
        pw, bpw = C.next_ps()
        for n in range(GB):
            MM(P, pw[:, n * 64:(n + 1) * 64], UW[:, n, 128:256], at[:, n * 64:(n + 1) * 64], True, True, [bUW, bat], bpw)
        Qp, bQp = Qpr()
        I(P, 'dve', 'tensor_tensor', [bq32, bpw], [bQp], out=Qp[:], in0=q32[:], in1=pw[:], op=ALU.add)
        if hi == 0 and b == 0:
            C.dbg('d_R', Rt[:].rearrange("p n x -> p (n x)"), [bRt]); C.dbg('d_kg', kg[:].rearrange("p n x -> p (n x)"), [bkg])
            C.dbg('d_UW', UW[:].rearrange("p n x -> p (n x)"), [bUW]); C.dbg('d_NT', NT[:].rearrange("p n x -> p (n x)"), [bNT]); C.dbg('d_Qp', Qp[:], [bQp])
        d['cur'] = dict(UW=UW, bUW=bUW, kg=kg, bkg=bkg, at=at, bat=bat, NT=NT, bNT=bNT, Qp=Qp, bQp=bQp)
        yield

    def chain(hidx, b, cur):
        hi = hidx % 2
        d = HT[hi]
        po, bpo = C.acc_o[hi]
        for n in range(GB):
            gn = b * GB + n
            first = (gn == 0)
            osl = po[:, n * 64:(n + 1) * 64]
            if not first:
                MM(P, osl, d['Sb'][:], cur['Qp'][:, n * 64:(n + 1) * 64], True, False, [d['bSb'], cur['bQp']], bpo)
            MM(P, osl, cur['UW'][:, n, 0:128], cur['at'][:, n * 64:(n + 1) * 64], first, True, [cur['bUW'], cur['bat']], bpo)
            ps, bps = C.acc_s[hi]
            if not first:
                MM(P, ps[:, 0:128], cur['NT'][:, n, :], d['Sb'][:], True, False, [cur['bNT'], d['bSb']], bps)
            MM(P, ps[:, 0:128], cur['kg'][:, n, :], cur['UW'][:, n, 0:128], first, True, [cur['bkg'], cur['bUW']], bps)
            I(P, 'dve', 'scalar_tensor_tensor', [bps, d['bS32'], d['bdl']], [d['bS32']], out=d['S32'][:], in0=d['S32'][:], scalar=d['dl'][:, gn:gn + 1], in1=ps[:, 0:128], op0=ALU.mult, op1=ALU.add)
            I(P, 'act', 'copy', [d['bS32']], [d['bSb']], out=d['Sb'][:], in_=d['S32'][:])
            yield
        (q0, k0, v0, gr, o0) = heads[hidx]
        o32, bo32 = o32r()
        I(P, 'act', 'copy', [bpo], [bo32], out=o32[:], in_=po[:])
        sqb, bsqb = sqbr()
        I(P, 'act', 'activation', [bpo], [bsqb], out=sqb[:], in_=po[:], func=AF.Square)
        pm, bpm = C.next_ps()
        MM(P, pm[:], C.g_avgb[:], sqb[:], True, True, [C.bconst, bsqb], bpm)
        sq, bsq = o32r()
        I(P, 'act', 'activation', [bpm, C.bconst], [bsq], out=sq[:], in_=pm[:], func=AF.Ln, bias=C.eps_n[:, 0:1], scale=1.0)
        I(P, 'act', 'activation', [bsq], [bsq], out=sq[:], in_=sq[:], func=AF.Exp, scale=-0.5)
        I(P, 'dve', 'scalar_tensor_tensor', [bo32, bsq, bng], [bo32], out=o32[:], in0=o32[:], scalar=ngl[:, 0:1], in1=sq[:], op0=ALU.mult, op1=ALU.mult)
        sz, bsz = szr()
        P.dma('sp', sz[:], szT[o0:o0 + 128, b * 512:(b + 1) * 512], [bdram_in], bsz)
        ob, bob = obr()
        I(P, 'dve', 'tensor_tensor', [bo32, bsz], [bob], out=ob[:], in0=o32[:], in1=sz[:], op=ALU.mult)
        P.dma('sp', mixT[o0:o0 + 128, b * 512:(b + 1) * 512], ob[:], [bob], bdram_out)
        yield

    C.acc_o = [C.acc(0), C.acc(1)]
    C.acc_s = [C.acc(2), C.acc(3)]
    def headB(hidx):
        hi = hidx % 2
        for _ in pre(hidx, 0):
            yield
        for b in range(NB):
            cg = chain(hidx, b, HT[hi]['cur'])
            pg = pre(hidx, b + 1) if b + 1 < NB else iter(())
            c_alive = p_alive = True
            while c_alive or p_alive:
                if c_alive:
                    try:
                        next(cg)
                    except StopIteration:
                        c_alive = False
                if p_alive:
                    for _ in range(2):
                        try:
                            next(pg)
                        except StopIteration:
                            p_alive = False
                            break
                yield

    for _ in phaseA(0):
        pass
    for hidx in range(nh):
        bg = phaseA(hidx + 1) if hidx + 1 < nh else None
        tick = 0
        for _ in headB(hidx):
            tick += 1
            if bg is not None and tick % 3 == 0:
                try:
                    next(bg)
                except StopIteration:
                    bg = None
        if bg is not None:
            for _ in bg:
                pass
    C.nring = 8


import ml_dtypes
from concourse.bass_utils import run_bass_kernel_spmd

KINDS = ['gdn', 'mlstm', 'fox', 'gdn']
NCOLS = {'gdn': 4624, 'mlstm': 3600, 'fox': 4624}
TC = 2048
SEQ = 8192
_BF = ml_dtypes.bfloat16
_progs = {}


def _p_out_specs(kind, T):
    if kind == 'gdn':
        return dict(qkvT=([3072, T], F32), szT=([1024, T], F32), gT=([8, T], F32), betaT=([8, T], F32), memoT=([512, T], BF16))
    if kind == 'mlstm':
        return dict(qT=([512, T], F32), kT=([512, T], F32), v=([T, 1024], BF16), sogT=([1024, T], F32), ilogT=([8, T], F32), flogT=([8, T], F32), memoT=([512, T], BF16))
    return dict(qT=([1024, T], BF16), kT=([1024, T], BF16), v=([T, 1024], BF16), sogT=([1024, T], F32), flogT=([16, T], F32), memoT=([512, T], BF16))


def _p_prm_specs(kind):
    if kind == 'gdn':
        return dict(a_log=[8], dt_bias=[8])
    if kind == 'mlstm':
        return dict(b_gate=[2, 8])
    return dict(b_f=[16], qk_g=[2, 64])


def _new_nc():
    return bass.Bass("TRN2", target_bir_lowering=False)


def build_P(kind):
    key = ('P', kind)
    if key in _progs:
        return _progs[key]
    nc = _new_nc()
    di = lambda n, s, dt=F32: nc.dram_tensor(n, s, dt, kind="ExternalInput").ap()
    xT = di("xT", [1024, TC]); memT = di("memT", [1024, 256]); w_in = di("w_in", [1024, NCOLS[kind]]); w_kv = di("w_kv", [1024, 1024])
    prm = {k: di(k, s) for k, s in _p_prm_specs(kind).items()}
    outs = {k: nc.dram_tensor(k, s, dt, kind="ExternalOutput").ap() for k, (s, dt) in _p_out_specs(kind, TC).items()}
    with ExitStack() as st:
        P = Prog(nc, st)
        C = Ctx(nc, st, P)
        stage_P(C, TC, kind, xT, memT, w_in, w_kv, prm, outs, P.buf("din"), P.buf("dout"))
        P.barrier()
        P.emit()
    _progs[key] = nc
    return nc


def build_F():
    key = ('F',)
    if key in _progs:
        return _progs[key]
    nc = _new_nc()
    di = lambda n, s, dt=F32: nc.dram_tensor(n, s, dt, kind="ExternalInput").ap()
    mixT = di("mixT", [1024, TC], BF16); memoT = di("memoT", [512, TC], BF16); xT = di("xT", [1024, TC])
    w_out = di("w_out", [1536, 1024]); w_up = di("w_up", [1024, 4096]); w_down = di("w_down", [4096, 1024])
    l1g = di("l1g", [1024]); l1b = di("l1b", [1024]); l2g = di("l2g", [1024]); l2b = di("l2b", [1024])
    yT = nc.dram_tensor("yT", [1024, TC], F32, kind="ExternalOutput").ap()
    with ExitStack() as st:
        P = Prog(nc, st)
        C = Ctx(nc, st, P)
        stage_F(C, TC, mixT, memoT, xT, yT, w_out, l1g, l1b, w_up, w_down, l2g, l2b, P.buf("din"), P.buf("dout"))
        P.barrier()
        P.emit()
    _progs[key] = nc
    return nc


def build_M(kind):
    key = ('M', kind)
    if key in _progs:
        return _progs[key]
    nc = _new_nc()
    S = SEQ
    di = lambda n, s, dt=F32: nc.dram_tensor(n, s, dt, kind="ExternalInput").ap()
    with ExitStack() as st:
        P = Prog(nc, st)
        C = Ctx(nc, st, P)
        bin_, bout = P.buf("din"), P.buf("dout")
        mixT = nc.dram_tensor("mixT", [256, S], BF16, kind="ExternalOutput").ap()
        if kind == 'gdn':
            qkvT = di("qkvT", [768, S]); conv_w = di("conv_w", [4, 768]); szT = di("szT", [256, S]); gT = di("gT", [2, S]); betaT = di("betaT", [2, S]); ng = di("ng", [128])
            heads = [(h * 128, 256 + h * 128, 512 + h * 128, h, h * 128) for h in range(2)]
            mixer_gdn(C, S, heads, qkvT, conv_w, szT, gT, betaT, ng, mixT, bin_, bout)
        elif kind == 'mlstm':
            qT = di("qT", [128, S]); kT = di("kT", [128, S]); v = di("v", [S, 256], BF16)
            sogT = di("sogT", [256, S]); flogT = di("flogT", [2, S]); ilogT = di("ilogT", [2, S]); ng = di("ng", [256])
            mixer_mlstm(C, S, 2, qT, kT, v, sogT, ilogT, flogT, ng, mixT, bin_, bout)
        else:
            qT = di("qT", [256, S], BF16); kT = di("kT", [256, S], BF16); v = di("v", [S, 256], BF16)
            sogT = di("sogT", [256, S]); flogT = di("flogT", [4, S])
            mixer_fox(C, S, 4, qT, kT, v, sogT, flogT, mixT, bin_, bout)
        P.barrier()
        P.emit()
    _progs[key] = nc
    return nc


FUSED = True
_W_SPECS = dict(gdn_w_in=[2, 1024, 4624], gdn_conv_w=[2, 4, 3072], gdn_a_log=[2, 8], gdn_dt_bias=[2, 8], gdn_norm_g=[2, 128],
                mlstm_w_in=[1, 1024, 3600], mlstm_b_gate=[1, 2, 8], mlstm_norm_g=[1, 1024], fox_w_in=[1, 1024, 4624], fox_b_f=[1, 16],
                fox_qk_g=[1, 2, 64], mem_w_kv=[4, 1024, 1024], w_out=[4, 1536, 1024], ln1_g=[4, 1024], ln1_b=[4, 1024],
                w_up=[4, 1024, 4096], w_down=[4, 4096, 1024], ln2_g=[4, 1024], ln2_b=[4, 1024])


def build_fused(nlayers=4):
    key = ('fused', nlayers)
    if key in _progs:
        return _progs[key]
    nc = _new_nc()
    S = SEQ
    di = lambda n, s, dt=F32: nc.dram_tensor(n, s, dt, kind="ExternalInput").ap()
    sc = lambda n, s, dt=F32: nc.dram_tensor(n, s, dt).ap()
    xT0 = di("xT", [1024, S]); memT = di("memT", [1024, 256])
    W = {k: di(k, shp) for k, shp in _W_SPECS.items()}
    yT = nc.dram_tensor("yT", [1024, S], F32, kind="ExternalOutput").ap()
    xa, xb_ = sc("xa", [1024, S]), sc("xb", [1024, S])
    scr = {}
    for kind in ('gdn', 'mlstm', 'fox'):
        scr[kind] = {k: sc("%s_%s" % (kind, k), shp, dt) for k, (shp, dt) in _p_out_specs(kind, S).items()}
    mixT = sc("mixT", [1024, S], BF16)
    with ExitStack() as st:
        P = Prog(nc, st)
        C = Ctx(nc, st, P)
        cur = xT0
        for li in range(nlayers):
            kind, j = KINDS[li], li // 3
            nxt = yT if li == nlayers - 1 else (xa if li % 2 == 0 else xb_)
            o = scr[kind]
            if kind == 'gdn':
                w_in = W['gdn_w_in'][j]
                prm = dict(a_log=W['gdn_a_log'][j], dt_bias=W['gdn_dt_bias'][j])
            elif kind == 'mlstm':
                w_in = W['mlstm_w_in'][j]
                prm = dict(b_gate=W['mlstm_b_gate'][j])
            else:
                w_in = W['fox_w_in'][j]
                prm = dict(b_f=W['fox_b_f'][j], qk_g=W['fox_qk_g'][j])
            C.stage_begin()
            for tc in range(S // TC):
                tok = slice(tc * TC, (tc + 1) * TC)
                outs = {k: (a[tok, :] if k == 'v' else a[:, tok]) for k, a in o.items()}
                stage_P(C, TC, kind, cur[:, tok], memT, w_in, W['mem_w_kv'][li], prm, outs, C.buf("din"), C.buf("dout"))
            C.stage_end()
            C.stage_begin()
            bi, bo = C.buf("din"), C.buf("dout")
            if kind == 'gdn':
                heads = [(h * 128, 1024 + h * 128, 2048 + h * 128, h, h * 128) for h in range(8)]
                mixer_gdn(C, S, heads, o['qkvT'], W['gdn_conv_w'][j], o['szT'], o['gT'], o['betaT'], W['gdn_norm_g'][j], mixT, bi, bo)
            elif kind == 'mlstm':
                mixer_mlstm(C, S, 8, o['qT'], o['kT'], o['v'], o['sogT'], o['ilogT'], o['flogT'], W['mlstm_norm_g'][j], mixT, bi, bo)
            else:
                mixer_fox(C, S, 16, o['qT'], o['kT'], o['v'], o['sogT'], o['flogT'], mixT, bi, bo)
            C.stage_end()
            C.stage_begin()
            for tc in range(S // TC):
                tok = slice(tc * TC, (tc + 1) * TC)
                stage_F(C, TC, mixT[:, tok], o['memoT'][:, tok], cur[:, tok], nxt[:, tok], W['w_out'][li], W['ln1_g'][li], W['ln1_b'][li],
                        W['w_up'][li], W['w_down'][li], W['ln2_g'][li], W['ln2_b'][li], C.buf("din"), C.buf("dout"))
            C.stage_end()
            cur = nxt
    _progs[key] = nc
    return nc


def kernel_fused(inputs):
    f32 = np.float32
    x = np.asarray(inputs['x'], f32)
    mem = np.asarray(inputs['mem'], f32)
    wts = {}
    for k, shp in _W_SPECS.items():
        wts[k] = _c(np.asarray(inputs[k], f32).reshape(shp))
    ims = []
    for c in range(8):
        b = c % 2
        ims.append(dict(xT=_c(x[b].T), memT=_c(mem[b].T), **wts))
    res = _run(build_fused(), ims)
    out = np.empty((2, SEQ, 1024), f32)
    for b in range(2):
        out[b] = np.asarray(res[b]['yT']).T
    return out


def _run(nc, in_maps):
    res = run_bass_kernel_spmd(nc, in_maps, core_ids=list(range(8)))
    return res.results


def _cat_tok(outs, name, b, axis):
    return np.concatenate([np.asarray(outs[b * 4 + c][name]) for c in range(4)], axis=axis)


def _c(a):
    return np.ascontiguousarray(a)


def kernel(x, mem, gdn_w_in, gdn_conv_w, gdn_a_log, gdn_dt_bias, gdn_norm_g,
           mlstm_w_in, mlstm_b_gate, mlstm_norm_g, fox_w_in, fox_b_f, fox_qk_g,
           mem_w_kv, w_out, ln1_g, ln1_b, w_up, w_down, ln2_g, ln2_b):
    if FUSED:
        return kernel_fused(dict(x=x, mem=mem, gdn_w_in=gdn_w_in, gdn_conv_w=gdn_conv_w, gdn_a_log=gdn_a_log, gdn_dt_bias=gdn_dt_bias,
                                 gdn_norm_g=gdn_norm_g, mlstm_w_in=mlstm_w_in, mlstm_b_gate=mlstm_b_gate, mlstm_norm_g=mlstm_norm_g,
                                 fox_w_in=fox_w_in, fox_b_f=fox_b_f, fox_qk_g=fox_qk_g, mem_w_kv=mem_w_kv, w_out=w_out, ln1_g=ln1_g,
                                 ln1_b=ln1_b, w_up=w_up, w_down=w_down, ln2_g=ln2_g, ln2_b=ln2_b))
    f32 = np.float32
    x = np.asarray(x, f32)
    mem = np.asarray(mem, f32)
    xT = [_c(x[c // 4, (c % 4) * TC:(c % 4 + 1) * TC, :].T) for c in range(8)]
    memT = [_c(mem[b].T) for b in range(2)]
    for li in range(4):
        kind, j = KINDS[li], li // 3
        if kind == 'gdn':
            w_in = np.asarray(gdn_w_in[j], f32)
            prm = dict(a_log=np.asarray(gdn_a_log[j], f32), dt_bias=np.asarray(gdn_dt_bias[j], f32))
        elif kind == 'mlstm':
            w_in = np.asarray(mlstm_w_in[j], f32)
            prm = dict(b_gate=np.asarray(mlstm_b_gate[j], f32))
        else:
            w_in = np.asarray(fox_w_in[j], f32)
            prm = dict(b_f=np.asarray(fox_b_f[j], f32), qk_g=np.asarray(fox_qk_g[j], f32))
        wkv = np.asarray(mem_w_kv[li], f32)
        ims = [dict(xT=xT[c], memT=memT[c // 4], w_in=w_in, w_kv=wkv, **prm) for c in range(8)]
        po = _run(build_P(kind), ims)
        ims = []
        for jc in range(8):
            b, hg = jc // 4, jc % 4
            if kind == 'gdn':
                qkv = _cat_tok(po, 'qkvT', b, 1) if hg == 0 else qkv_cache
                qkv_cache = qkv
                hs = [2 * hg, 2 * hg + 1]
                rows = np.concatenate([np.arange(o + h * 128, o + (h + 1) * 128) for o in (0, 1024, 2048) for h in hs])
                if hg == 0:
                    sz_c = _cat_tok(po, 'szT', b, 1); g_c = _cat_tok(po, 'gT', b, 1); be_c = _cat_tok(po, 'betaT', b, 1)
                ims.append(dict(qkvT=_c(qkv[rows]), conv_w=_c(np.asarray(gdn_conv_w[j], f32)[:, rows]), szT=_c(sz_c[hs[0] * 128:(hs[1] + 1) * 128]),
                                gT=_c(g_c[hs[0]:hs[1] + 1]), betaT=_c(be_c[hs[0]:hs[1] + 1]), ng=np.asarray(gdn_norm_g[j], f32)))
            elif kind == 'mlstm':
                if hg == 0:
                    q_c = _cat_tok(po, 'qT', b, 1); k_c = _cat_tok(po, 'kT', b, 1); v_c = _cat_tok(po, 'v', b, 0)
                    so_c = _cat_tok(po, 'sogT', b, 1); il_c = _cat_tok(po, 'ilogT', b, 1); fl_c = _cat_tok(po, 'flogT', b, 1)
                h0 = 2 * hg
                ims.append(dict(qT=_c(q_c[h0 * 64:(h0 + 2) * 64]), kT=_c(k_c[h0 * 64:(h0 + 2) * 64]), v=_c(v_c[:, h0 * 128:(h0 + 2) * 128]),
                                sogT=_c(so_c[h0 * 128:(h0 + 2) * 128]), ilogT=_c(il_c[h0:h0 + 2]), flogT=_c(fl_c[h0:h0 + 2]),
                                ng=_c(np.asarray(mlstm_norm_g[j], f32)[h0:h0 + 2].reshape(-1))))
            else:
                if hg == 0:
                    q_c = _cat_tok(po, 'qT', b, 1); k_c = _cat_tok(po, 'kT', b, 1); v_c = _cat_tok(po, 'v', b, 0)
                    so_c = _cat_tok(po, 'sogT', b, 1); fl_c = _cat_tok(po, 'flogT', b, 1)
                h0 = 4 * hg
                ims.append(dict(qT=_c(q_c[h0 * 64:(h0 + 4) * 64]), kT=_c(k_c[h0 * 64:(h0 + 4) * 64]), v=_c(v_c[:, h0 * 64:(h0 + 4) * 64]),
                                sogT=_c(so_c[h0 * 64:(h0 + 4) * 64]), flogT=_c(fl_c[h0:h0 + 4])))
        mo = _run(build_M(kind), ims)
        ims = []
        for c in range(8):
            b, sc = c // 4, c % 4
            mixT = np.concatenate([np.asarray(mo[b * 4 + hg]['mixT'])[:, sc * TC:(sc + 1) * TC] for hg in range(4)], axis=0)
            ims.append(dict(mixT=_c(mixT), memoT=_c(np.asarray(po[c]['memoT'])), xT=xT[c],
                            w_out=np.asarray(w_out[li], f32), w_up=np.asarray(w_up[li], f32), w_down=np.asarray(w_down[li], f32),
                            l1g=np.asarray(ln1_g[li], f32), l1b=np.asarray(ln1_b[li], f32), l2g=np.asarray(ln2_g[li], f32), l2b=np.asarray(ln2_b[li], f32)))
        fo = _run(build_F(), ims)
        xT = [_c(np.asarray(fo[c]['yT'])) for c in range(8)]
    out = np.empty((2, SEQ, 1024), f32)
    for c in range(8):
        out[c // 4, (c % 4) * TC:(c % 4 + 1) * TC, :] = xT[c].T
    return out
```

```python
from contextlib import ExitStack
import numpy as np
import concourse.bass as bass
import concourse.mybir as mybir

F32 = mybir.dt.float32
BF16 = mybir.dt.bfloat16
AF = mybir.ActivationFunctionType
ALU = mybir.AluOpType
AX = mybir.AxisListType

ENGS = ['pe', 'act', 'dve', 'pool', 'sp']
SYNC_SAME_ENGINE = True
NDSEM = 32


class Buf:
    __slots__ = ('name', 'last_w', 'readers', 'sem', 'cnt')

    def __init__(self, name):
        self.name = name
        self.last_w = None
        self.readers = []
        self.sem = None
        self.cnt = 0


class Op:
    __slots__ = ('eng', 'fn', 'deps', 'is_dma', 'tok', 'signal', 'seq', 'pe_buf', 'inc')

    def __init__(self, eng, fn):
        self.eng = eng
        self.fn = fn
        self.deps = []
        self.is_dma = False
        self.tok = None
        self.signal = False
        self.seq = None
        self.pe_buf = None
        self.inc = 1


class Prog:
    def __init__(self, nc, stack):
        self.nc = nc
        self.stack = stack
        self.ops = {e: [] for e in ENGS}
        self.esem = {e: stack.enter_context(nc.semaphore('es_' + e)) for e in ENGS}
        self.ecnt = {e: 0 for e in ENGS}
        self.waited = {e: {} for e in ENGS}
        self.bufs = []
        self.dsem = [stack.enter_context(nc.semaphore('ds_%d' % i)) for i in range(NDSEM)]
        self.dlast = [None] * NDSEM
        self.dcount = 0
        self.nops = 0
        self.serial = False
        self.prev = None

    def buf(self, name):
        b = Buf(name)
        self.bufs.append(b)
        return b

    def _deps(self, op, reads, writes):
        if self.serial:
            if self.prev is not None:
                op.deps.append(self.prev)
            self.prev = op
        for b in reads:
            if b.last_w is not None:
                op.deps.append(b.last_w)
        for b in writes:
            if b.last_w is not None:
                op.deps.append(b.last_w)
            op.deps.extend(b.readers)
        for b in reads:
            b.readers.append(op)
        for b in writes:
            b.last_w = op
            b.readers = []

    def op(self, eng, fn, reads=(), writes=(), inc=1):
        o = Op(eng, fn)
        o.inc = inc
        self._deps(o, reads, writes)
        self.ops[eng].append(o)
        self.nops += 1
        return o

    def mm(self, fn, reads, out):
        o = Op('pe', fn)
        o.pe_buf = out
        self._deps(o, reads, [out])
        o.deps = [d for d in o.deps if not (d.eng == 'pe' and d.pe_buf is out)]
        self.ops['pe'].append(o)
        self.nops += 1
        return o

    def dma(self, queue, out, in_, reads, wbuf, **kw):
        n = self.dcount
        self.dcount += 1
        i = n % NDSEM
        sem = self.dsem[i]
        val = 16 * (n // NDSEM + 1)

        def fn(eng, out=out, in_=in_, kw=kw):
            return eng.dma_start(out=out, in_=in_, **kw)
        o = Op(queue, fn)
        o.is_dma = True
        o.tok = (sem, val)
        if self.dlast[i] is not None:
            o.deps.append(self.dlast[i])
        self.dlast[i] = o
        if not isinstance(wbuf, (list, tuple)):
            wbuf = [wbuf]
        self._deps(o, reads, wbuf)
        self.ops[queue].append(o)
        self.nops += 1
        return o

    def barrier(self):
        lasts = []
        for e in ENGS:
            if self.ops[e]:
                for o in reversed(self.ops[e]):
                    if not o.is_dma:
                        lasts.append(o)
                        break
        dl = [o for o in self.dlast if o is not None]
        for e in ENGS:
            o = Op(e, lambda eng: eng.nop())
            o.deps = list(lasts) + dl
            self.ops[e].append(o)
        for b in self.bufs:
            b.last_w = None
            b.readers = []

    def emit(self):
        nc = self.nc
        for e in ENGS:
            for o in self.ops[e]:
                for d in o.deps:
                    if not d.is_dma:
                        if d.eng == e and not SYNC_SAME_ENGINE:
                            continue
                        d.signal = True
        for e in ENGS:
            for o in self.ops[e]:
                if o.signal and not o.is_dma and o.seq is None:
                    self.ecnt[e] += o.inc
                    o.seq = self.ecnt[e]
        engobj = {'pe': nc.tensor, 'act': nc.scalar, 'dve': nc.vector, 'pool': nc.gpsimd, 'sp': nc.sync}
        stats = {'waits': 0}

        def run(e, eng):
            waited = self.waited[e]
            for o in self.ops[e]:
                need = {}
                for d in o.deps:
                    if d.is_dma:
                        sem, val = d.tok
                    else:
                        if d.eng == e and not SYNC_SAME_ENGINE:
                            continue
                        sem, val = self.esem[d.eng], d.seq
                    k = id(sem)
                    if waited.get(k, (None, 0))[1] >= val:
                        continue
                    if k not in need or need[k][1] < val:
                        need[k] = (sem, val)
                for k, (sem, val) in need.items():
                    eng.wait_ge(sem, val)
                    waited[k] = (sem, val)
                    stats['waits'] += 1
                ins = o.fn(eng)
                if o.is_dma:
                    ins.then_inc(o.tok[0], 16)
                elif o.signal:
                    ins.then_inc(self.esem[e], o.inc)
            self.ops[e] = []

        with nc.Block() as block:
            @block.tensor
            def _(eng):
                run('pe', eng)

            @block.scalar
            def _(eng):
                run('act', eng)

            @block.vector
            def _(eng):
                run('dve', eng)

            @block.gpsimd
            def _(eng):
                run('pool', eng)

            @block.sync
            def _(eng):
                run('sp', eng)
        return stats


D = 1024
DFF = 4096
ALPHA = 8 ** 0.25
LN_EPS = 1e-5
NORM_EPS = 1e-6
TT = 512


class Ctx:
    def __init__(self, nc, st, P):
        self.nc, self.st, self.P = nc, st, P
        self.ps = []
        for i in range(8):
            t = st.enter_context(nc.psum_tensor("ps%d" % i, [128, 512], F32))
            self.ps.append((t, P.buf("ps%d" % i)))
        self.psi = 0
        self.nring = 8
        self.cache = {}
        self.pcache = {}
        self.pools = {}
        self.sst = st
        self.stage_no = 0
        self.wr = []
        self.wi = 0
        self.ones32 = self.sbp("ones32", [128, 128], F32)
        self.onesb = self.sbp("onesb", [128, 128], BF16)
        b = P.buf("consts")
        self.bconst = b
        P.op('pool', lambda e: e.memset(self.ones32[:], 1.0), [], [b])
        P.op('pool', lambda e: e.memset(self.onesb[:], 1.0), [], [b])
        self.eps_ln = self.sbp('eps_ln', [128, 1], F32)
        self.eps_n = self.sbp('eps_n', [128, 1], F32)
        P.op('pool', lambda e: e.memset(self.eps_ln[:], LN_EPS), [], [b])
        P.op('pool', lambda e: e.memset(self.eps_n[:], NORM_EPS), [], [b])

    def sb(self, name, shape, dt):
        if name in self.pcache:
            return self.pcache[name]
        if name not in self.cache:
            self.cache[name] = self.sst.enter_context(self.nc.sbuf_tensor("%s_s%d" % (name, self.stage_no), shape, dt))
        return self.cache[name]

    def sbp(self, name, shape, dt):
        if name not in self.pcache:
            self.pcache[name] = self.st.enter_context(self.nc.sbuf_tensor(name, shape, dt))
        return self.pcache[name]

    def stage_begin(self):
        self.stage_no += 1
        self.sst = ExitStack()
        self.cache = {}
        self.wr = []
        self.wi = 0

    def stage_end(self):
        self.P.barrier()
        self.P.emit()
        self.sst.close()
        self.sst = self.st
        self.cache = {}
        self.wr = []

    def buf(self, name):
        k = 'buf:' + name
        if k not in self.cache:
            self.cache[k] = self.P.buf(name)
        return self.cache[k]

    def next_ps(self, pool=None):
        if pool is not None:
            banks, st = self.pools[pool]
            t, b = self.ps[banks[st[0] % len(banks)]]
            st[0] += 1
            return t, b
        t, b = self.ps[self.psi % self.nring]
        self.psi += 1
        return t, b

    def set_pool(self, name, banks):
        self.pools[name] = (list(banks), [0])

    def dbg(self, name, ap, bufs):
        d = getattr(self, 'dbgs', None)
        if d and name in d and name not in self.cache:
            self.cache[name] = True
            self.P.dma('sp', d[name], ap, list(bufs), self.buf('dbg_out'))

    def acc(self, i):
        return self.ps[self.nring + i]

    def load_w(self, w_ap, r0, nrows, c0, ncols):
        kc = nrows // 128
        if not self.wr:
            for i in range(4):
                self.wr.append((self.sb("wr%d" % i, [128, 4096], BF16), self.buf("wr%d" % i)))
        t, b = self.wr[self.wi % 4]
        self.wi += 1
        view = t[:, 0:kc * ncols].rearrange("p (k c) -> p k c", k=kc)
        src = w_ap[r0:r0 + nrows, c0:c0 + ncols].rearrange("(k p) c -> p k c", p=128)
        self.P.dma('pool', view, src, [], b)
        return view, b

    def ring(self, name, n, shape, dt):
        slots = [(self.sb("%s%d" % (name, i), shape, dt), self.buf("%s%d" % (name, i))) for i in range(n)]
        k = 'ring:' + name
        if k not in self.cache:
            self.cache[k] = {'i': 0}
        st = self.cache[k]

        def nxt():
            s = slots[st['i'] % n]
            st['i'] += 1
            return s
        return nxt


def layer_norm_fm(C, x32, bx, xb, bxb, g_t, b_t, bgb, ntok_tiles, t0, sq_ring, st_ring):
    P = C.P
    for tt in range(ntok_tiles):
        sl = slice(t0 + tt * TT, t0 + (tt + 1) * TT)
        s1, bs1 = C.next_ps()
        s2, bs2 = C.next_ps()
        for k in range(8):
            zb, bzb = C.ring("LN_zb", 2, [128, TT], BF16)()
            I(P, 'act', 'copy', [bx[k][tt]], [bzb], out=zb[:], in_=x32[:, k, sl])
            MM(P, s1[:], C.onesb[:], zb[:], k == 0, k == 7, [bzb, C.bconst], bs1)
            sqb, bsqb = C.ring("LN_sqb", 2, [128, TT], BF16)()
            I(P, 'act', 'activation', [bx[k][tt]], [bsqb], out=sqb[:], in_=x32[:, k, sl], func=AF.Square)
            MM(P, s2[:], C.onesb[:], sqb[:], k == 0, k == 7, [bsqb, C.bconst], bs2)
        m, bm = st_ring()
        msq, bmsq = st_ring()
        A, bA = st_ring()
        Bc, bBc = st_ring()
        P.op('dve', lambda e, m=m, s1=s1, sl=sl: e.tensor_scalar(out=m[:], in0=s1[:], scalar1=1.0 / D, scalar2=None, op0=ALU.mult), [bs1], [bm])
        P.op('dve', lambda e, m=m, msq=msq, sl=sl: e.tensor_tensor(out=msq[:], in0=m[:], in1=m[:], op=ALU.mult), [bm], [bmsq])
        P.op('dve', lambda e, msq=msq, s2=s2, sl=sl: e.scalar_tensor_tensor(out=msq[:], in0=s2[:], scalar=1.0 / D, in1=msq[:], op0=ALU.mult, op1=ALU.subtract),
             [bs2, bmsq], [bmsq])
        P.op('act', lambda e, msq=msq, A=A, sl=sl: e.activation(out=A[:], in_=msq[:], func=AF.Ln, bias=C.eps_ln[:, 0:1], scale=1.0), [bmsq, C.bconst], [bA])
        P.op('act', lambda e, A=A, sl=sl: e.activation(out=A[:], in_=A[:], func=AF.Exp, scale=-0.5), [bA], [bA])
        P.op('dve', lambda e, m=m, A=A, Bc=Bc, sl=sl: e.scalar_tensor_tensor(out=Bc[:], in0=m[:], scalar=-1.0, in1=A[:], op0=ALU.mult, op1=ALU.mult),
             [bm, bA], [bBc])
        for k in range(8):
            u, bu = sq_ring()
            P.op('dve', lambda e, k=k, u=u, A=A, sl=sl: e.scalar_tensor_tensor(out=u[:], in0=x32[:, k, sl], scalar=g_t[:, k:k + 1], in1=A[:], op0=ALU.mult, op1=ALU.mult),
                 [bx[k][tt], bA, bgb], [bu])
            P.op('dve', lambda e, k=k, u=u, Bc=Bc, sl=sl: e.scalar_tensor_tensor(out=u[:], in0=Bc[:], scalar=g_t[:, k:k + 1], in1=u[:], op0=ALU.mult, op1=ALU.add),
                 [bBc, bu, bgb], [bu])
            P.op('act', lambda e, k=k, u=u, sl=sl: e.activation(out=x32[:, k, sl], in_=u[:], func=AF.Identity, bias=b_t[:, k:k + 1], scale=1.0),
                 [bu, bgb], [bx[k][tt]])
            P.op('act', lambda e, k=k, u=u, sl=sl: e.activation(out=xb[:, k, sl], in_=u[:], func=AF.Identity, bias=b_t[:, k:k + 1], scale=1.0),
                 [bu, bgb], [bxb[k][tt]])


def stage_F(C, T, mixT, memoT, xT_in, xT_out, w_out, ln1g, ln1b, w_up, w_down, ln2g, ln2b, bdram_in, bdram_out, dbg=None):
    P = C.P
    nc = C.nc
    TH = min(T, 1024)
    ntt = TH // TT
    x32 = C.sb("F_x32", [128, 8, TH], F32)
    xb = C.sb("F_xb", [128, 8, TH], BF16)
    cat = C.sb("F_cat", [128, 12, TH], BF16)
    h = C.sb("F_h", [128, 32, TH], BF16)
    lnp = C.sb("F_lnp", [128, 4, 8], F32)
    blnp = C.buf("lnp")
    for i, v in enumerate([ln1g, ln1b, ln2g, ln2b]):
        P.dma("sp", lnp[:, i, :], v.rearrange("(k p) -> p k", p=128), [], blnp, allow_slow_non_contiguous=True)
    sq_ring = C.ring("F_sq", 2, [128, TT], F32)
    st_ring = C.ring("F_st", 4, [128, TT], F32)
    relu_ring = C.ring("F_relu", 3, [128, TT], F32)
    bx = [[C.buf("x%d_%d" % (k, t)) for t in range(ntt)] for k in range(8)]
    bxb = [[C.buf("xb%d_%d" % (k, t)) for t in range(ntt)] for k in range(8)]
    bcat = [[C.buf("cat%d_%d" % (k, t)) for t in range(ntt)] for k in range(12)]
    bh = [[C.buf("h%d_%d" % (k, t)) for t in range(ntt)] for k in range(32)]
    for half in range(T // TH):
        t0 = half * TH
        for k in range(12):
            src = mixT[k * 128:(k + 1) * 128, t0:t0 + TH] if k < 8 else memoT[(k - 8) * 128:(k - 7) * 128, t0:t0 + TH]
            for tt in range(ntt):
                P.dma('sp', cat[:, k, tt * TT:(tt + 1) * TT], src[:, tt * TT:(tt + 1) * TT], [bdram_in], bcat[k][tt])
        for k in range(8):
            for tt in range(ntt):
                P.dma('sp', x32[:, k, tt * TT:(tt + 1) * TT], xT_in[k * 128:(k + 1) * 128, t0 + tt * TT:t0 + (tt + 1) * TT], [bdram_in], bx[k][tt])
        for og in range(4):
            wt, bw = C.load_w(w_out, 0, 1536, og * 256, 256)
            for oc in range(2):
                o = og * 2 + oc
                for tt in range(ntt):
                    sl = slice(tt * TT, (tt + 1) * TT)
                    ps, bps = C.next_ps()
                    for k in range(12):
                        MM(P, ps[:], wt[:, k, oc * 128:(oc + 1) * 128], cat[:, k, sl], k == 0, k == 11, [bw, bcat[k][tt]], bps)
                    I(P, 'dve', 'scalar_tensor_tensor', [bps, bx[o][tt]], [bx[o][tt]], out=x32[:, o, sl], in0=x32[:, o, sl], scalar=ALPHA, in1=ps[:], op0=ALU.mult, op1=ALU.add)
        def dump(name):
            if dbg and name in dbg:
                for k in range(8):
                    for tt in range(ntt):
                        P.dma('sp', dbg[name][k * 128:(k + 1) * 128, t0 + tt * TT:t0 + (tt + 1) * TT], x32[:, k, tt * TT:(tt + 1) * TT], [bx[k][tt]], bdram_out)
        dump('z1')
        layer_norm_fm(C, x32, bx, xb, bxb, lnp[:, 0, :], lnp[:, 1, :], blnp, ntt, 0, sq_ring, st_ring)
        dump('x1')
        for hg in range(8):
            wt, bw = C.load_w(w_up, 0, 1024, hg * 512, 512)
            for oc in range(4):
                hc = hg * 4 + oc
                for tt in range(ntt):
                    sl = slice(tt * TT, (tt + 1) * TT)
                    ps, bps = C.next_ps()
                    for k in range(8):
                        P.mm(lambda e, ps=ps, wt=wt, k=k, oc=oc, sl=sl: e.matmul(ps[:], wt[:, k, oc * 128:(oc + 1) * 128], xb[:, k, sl], start=(k == 0), stop=(k == 7)),
                             [bw, bxb[k][tt]], bps)
                    r, br = relu_ring()
                    I(P, 'act', 'activation', [bps], [br], out=r[:], in_=ps[:], func=AF.Relu)
                    I(P, 'dve', 'tensor_tensor', [br, bps], [bh[hc][tt]], out=h[:, hc, sl], in0=r[:], in1=ps[:], op=ALU.mult)
        for og in range(4):
            wts = [C.load_w(w_down, kh * 2048, 2048, og * 256, 256) for kh in range(2)]
            for oc in range(2):
                o = og * 2 + oc
                for tt in range(ntt):
                    sl = slice(tt * TT, (tt + 1) * TT)
                    ps, bps = C.next_ps()
                    for k in range(32):
                        wt, bw = wts[k // 16]
                        MM(P, ps[:], wt[:, k % 16, oc * 128:(oc + 1) * 128], h[:, k, sl], k == 0, k == 31, [bw, bh[k][tt]], bps)
                    I(P, 'dve', 'scalar_tensor_tensor', [bps, bx[o][tt]], [bx[o][tt]], out=x32[:, o, sl], in0=x32[:, o, sl], scalar=ALPHA, in1=ps[:], op0=ALU.mult, op1=ALU.add)
        layer_norm_fm(C, x32, bx, xb, bxb, lnp[:, 2, :], lnp[:, 3, :], blnp, ntt, 0, sq_ring, st_ring)
        for k in range(8):
            for tt in range(ntt):
                P.dma('sp', xT_out[k * 128:(k + 1) * 128, t0 + tt * TT:t0 + (tt + 1) * TT], x32[:, k, tt * TT:(tt + 1) * TT], [bx[k][tt]], bdram_out)


def I(P, eng, name, reads, writes, *args, **kw):
    return P.op(eng, lambda e, name=name, args=args, kw=kw: getattr(e, name)(*args, **kw), reads, writes)


def MM(P, out, lhsT, rhs, start, stop, reads, obuf):
    return P.mm(lambda e, out=out, lhsT=lhsT, rhs=rhs, start=start, stop=stop: e.matmul(out, lhsT, rhs, start=start, stop=stop), reads, obuf)


MEM_SCALE = 128 ** -0.5


def stage_P(C, T, kind, xT_in, memT, w_in, w_kv, prm, outs, bdram_in, bdram_out):
    P = C.P
    ntt = T // TT
    pi = C.cache.setdefault("P_parity", [0])
    pi[0] ^= 1
    xb = C.sb("P_xb%d" % pi[0], [128, 8, T], BF16)
    memb = C.sb("P_memb", [128, 8, 256], BF16)
    kmT = C.sb("P_kmT", [128, 4, 256], BF16)
    vm = C.sb("P_vm", [128, 2, 512], BF16)
    bxb = [C.buf("Pxb%d_%d" % (pi[0], k)) for k in range(8)]
    bmemb, bkmT, bvm = C.buf("memb"), C.buf("kmT"), C.buf("vm")
    for k in range(8):
        P.dma('pool', xb[:, k, :], xT_in[k * 128:(k + 1) * 128, :], [bdram_in], bxb[k])
    P.dma('pool', memb[:], memT.rearrange("(k p) m -> p k m", p=128), [], bmemb)
    wt, bw = C.load_w(w_kv, 0, 1024, 0, 512)
    for hd in range(4):
        ps, bps = C.next_ps()
        for k in range(8):
            MM(P, ps[:, 0:256], wt[:, k, hd * 128:(hd + 1) * 128], memb[:, k, :], k == 0, k == 7, [bw, bmemb], bps)
        I(P, 'act', 'copy', [bps], [bkmT], out=kmT[:, hd, :], in_=ps[:, 0:256])
    wt, bw = C.load_w(w_kv, 0, 1024, 512, 512)
    for mt in range(2):
        ps, bps = C.next_ps()
        for k in range(8):
            MM(P, ps[:], memb[:, k, mt * 128:(mt + 1) * 128], wt[:, k, :], k == 0, k == 7, [bw, bmemb], bps)
        I(P, 'act', 'copy', [bps], [bvm], out=vm[:, mt, :], in_=ps[:])

    stg32 = C.ring("P_s32", 3, [128, TT], F32)
    stgb = C.ring("P_sb", 3, [128, TT], BF16)
    scr32 = C.ring("P_c32", 3, [128, TT], F32)

    def proj_fm(c0, ncols, post):
        done = 0
        while done < ncols:
            g = min(512, ncols - done)
            wt, bw = C.load_w(w_in, 0, 1024, c0 + done, g)
            for oc in range((g + 127) // 128):
                nrow = min(128, g - oc * 128)
                for tt in range(ntt):
                    ps, bps = C.next_ps()
                    for k in range(8):
                        MM(P, ps[0:nrow, :], wt[:, k, oc * 128:oc * 128 + nrow], xb[:, k, tt * TT:(tt + 1) * TT], k == 0, k == 7, [bw, bxb[k]], bps)
                    post(ps, bps, (done + oc * 128) // 128, nrow, tt)
            done += g

    def proj_tm(c0, ncols, dst):
        for g0 in range(0, ncols, 512):
            wt, bw = C.load_w(w_in, 0, 1024, c0 + g0, 512)
            for t in range(T // 128):
                ps, bps = C.next_ps()
                for k in range(8):
                    MM(P, ps[:], xb[:, k, t * 128:(t + 1) * 128], wt[:, k, :], k == 0, k == 7, [bw, bxb[k]], bps)
                s, bs = stgb()
                I(P, 'act', 'copy', [bps], [bs], out=s[:], in_=ps[:])
                P.dma('sp', dst[t * 128:(t + 1) * 128, g0:g0 + 512], s[:], [bs], bdram_out)

    def store_fm(dst, dt_ring):
        def post(ps, bps, ci, nrow, tt, func=None):
            s, bs = dt_ring()
            I(P, 'act', 'activation', [bps], [bs], out=s[0:nrow, :], in_=ps[0:nrow, :], func=(func or AF.Copy))
            P.dma('sp', dst[ci * 128:ci * 128 + nrow, tt * TT:(tt + 1) * TT], s[0:nrow, :], [bs], bdram_out)
        return post

    def post_act(dst, func, ring):
        base = store_fm(dst, ring)
        return lambda ps, bps, ci, nrow, tt: base(ps, bps, ci, nrow, tt, func=func)

    gp = C.sb("P_gp", [16, 4], F32)
    bgp = C.buf("gp")

    def gates_post(spec):
        def post(ps, bps, ci, nrow, tt):
            spec(ps, bps, tt)
        return post

    if kind == 'gdn':
        proj_fm(0, 3072, store_fm(outs['qkvT'], stg32))
        proj_fm(3072, 1024, post_act(outs['szT'], AF.Silu, stg32))
        P.dma('sp', gp[0:8, 0:1], prm['a_log'].rearrange("(h o) -> h o", o=1), [], bgp)
        P.dma('sp', gp[0:8, 1:2], prm['dt_bias'].rearrange("(h o) -> h o", o=1), [], bgp)
        I(P, 'act', 'activation', [bgp], [bgp], out=gp[0:8, 2:3], in_=gp[0:8, 0:1], func=AF.Exp)
        I(P, 'dve', 'tensor_scalar', [bgp], [bgp], out=gp[0:8, 2:3], in0=gp[0:8, 2:3], scalar1=-1.0, scalar2=None, op0=ALU.mult)

        def gspec(ps, bps, tt):
            s, bs = scr32()
            I(P, 'act', 'activation', [bps, bgp], [bs], out=s[0:8, :], in_=ps[0:8, :], func=AF.Exp, bias=gp[0:8, 1:2], scale=1.0)
            I(P, 'act', 'activation', [bs], [bs], out=s[0:8, :], in_=s[0:8, :], func=AF.Ln, bias=1.0, scale=1.0)
            I(P, 'dve', 'tensor_scalar', [bs, bgp], [bs], out=s[0:8, :], in0=s[0:8, :], scalar1=gp[0:8, 2:3], scalar2=None, op0=ALU.mult)
            P.dma('sp', outs['gT'][:, tt * TT:(tt + 1) * TT], s[0:8, :], [bs], bdram_out)
            s2, bs2 = scr32()
            I(P, 'act', 'activation', [bps], [bs2], out=s2[0:16, :], in_=ps[0:16, :], func=AF.Sigmoid)
            P.dma('sp', outs['betaT'][:, tt * TT:(tt + 1) * TT], s2[8:16, :], [bs2], bdram_out)
        proj_fm(4096, 16, gates_post(gspec))
        memq0 = 4112
    elif kind == 'mlstm':
        proj_fm(0, 512, store_fm(outs['qT'], stg32))
        proj_fm(512, 512, store_fm(outs['kT'], stg32))
        proj_tm(1024, 1024, outs['v'])
        proj_fm(2048, 1024, post_act(outs['sogT'], AF.Sigmoid, stg32))
        P.dma('sp', gp[0:8, 0:1], prm['b_gate'][0, :].rearrange("(h o) -> h o", o=1), [], bgp)
        P.dma('sp', gp[8:16, 0:1], prm['b_gate'][1, :].rearrange("(h o) -> h o", o=1), [], bgp)
        I(P, 'dve', 'tensor_scalar', [bgp], [bgp], out=gp[0:16, 1:2], in0=gp[0:16, 0:1], scalar1=-1.0, scalar2=None, op0=ALU.mult)

        def gspec(ps, bps, tt):
            s, bs = scr32()
            I(P, 'act', 'activation', [bps, bgp], [bs], out=s[0:16, :], in_=ps[0:16, :], func=AF.Identity, bias=gp[0:16, 0:1], scale=1.0)
            P.dma('sp', outs['ilogT'][:, tt * TT:(tt + 1) * TT], s[0:8, :], [bs], bdram_out)
            s2, bs2 = scr32()
            I(P, 'act', 'activation', [bps, bgp], [bs2], out=s2[0:16, :], in_=ps[0:16, :], func=AF.Exp, bias=gp[0:16, 1:2], scale=-1.0)
            I(P, 'act', 'activation', [bs2], [bs2], out=s2[0:16, :], in_=s2[0:16, :], func=AF.Ln, bias=1.0, scale=1.0)
            I(P, 'dve', 'tensor_scalar', [bs2], [bs2], out=s2[0:16, :], in0=s2[0:16, :], scalar1=-1.0, scalar2=None, op0=ALU.mult)
            P.dma('sp', outs['flogT'][:, tt * TT:(tt + 1) * TT], s2[8:16, :], [bs2], bdram_out)
        proj_fm(3072, 16, gates_post(gspec))
        memq0 = 3088
    else:
        blk = C.sb("P_blk", [128, 128], F32)
        qkg = C.sb("P_qkg", [128, 2], F32)
        bblk = C.buf("blk")
        I(P, 'pool', 'memset', [], [bblk], blk[:], 0.0)
        I(P, 'pool', 'memset', [bblk], [bblk], blk[0:64, 0:64], 1.0 / 64)
        I(P, 'pool', 'memset', [bblk], [bblk], blk[64:128, 64:128], 1.0 / 64)
        for j in range(2):
            for hh in range(2):
                P.dma('sp', qkg[hh * 64:(hh + 1) * 64, j:j + 1], prm['qk_g'][j, :].rearrange("(d o) -> d o", o=1), [], bblk)
        I(P, 'dve', 'tensor_scalar', [bblk], [bblk], out=qkg[:, 0:1], in0=qkg[:, 0:1], scalar1=64 ** -0.5, scalar2=None, op0=ALU.mult)

        def rms_post(dst, j):
            def post(ps, bps, ci, nrow, tt):
                raw, braw = scr32()
                sq, bsq = scr32()
                I(P, 'act', 'copy', [bps], [braw], out=raw[:], in_=ps[:])
                I(P, 'act', 'activation', [bps], [bsq], out=sq[:], in_=ps[:], func=AF.Square)
                ps2, bps2 = C.next_ps()
                MM(P, ps2[:], blk[:], sq[:], True, True, [bblk, bsq], bps2)
                I(P, 'act', 'activation', [bps2, C.bconst], [bsq], out=sq[:], in_=ps2[:], func=AF.Ln, bias=C.eps_n[:, 0:1], scale=1.0)
                I(P, 'act', 'activation', [bsq], [bsq], out=sq[:], in_=sq[:], func=AF.Exp, scale=-0.5)
                s, bs = stgb()
                I(P, 'dve', 'scalar_tensor_tensor', [braw, bsq, bblk], [bs], out=s[:], in0=raw[:], scalar=qkg[:, j:j + 1], in1=sq[:], op0=ALU.mult, op1=ALU.mult)
                P.dma('sp', dst[ci * 128:(ci + 1) * 128, tt * TT:(tt + 1) * TT], s[:], [bs], bdram_out)
            return post
        proj_fm(0, 1024, rms_post(outs['qT'], 0))
        proj_fm(1024, 1024, rms_post(outs['kT'], 1))
        proj_tm(2048, 1024, outs['v'])
        proj_fm(3072, 1024, post_act(outs['sogT'], AF.Sigmoid, stg32))
        P.dma('sp', gp[0:16, 0:1], prm['b_f'].rearrange("(h o) -> h o", o=1), [], bgp)
        I(P, 'dve', 'tensor_scalar', [bgp], [bgp], out=gp[0:16, 1:2], in0=gp[0:16, 0:1], scalar1=-1.0, scalar2=None, op0=ALU.mult)

        def gspec(ps, bps, tt):
            s2, bs2 = scr32()
            I(P, 'act', 'activation', [bps, bgp], [bs2], out=s2[0:16, :], in_=ps[0:16, :], func=AF.Exp, bias=gp[0:16, 1:2], scale=-1.0)
            I(P, 'act', 'activation', [bs2], [bs2], out=s2[0:16, :], in_=s2[0:16, :], func=AF.Ln, bias=1.0, scale=1.0)
            I(P, 'dve', 'tensor_scalar', [bs2], [bs2], out=s2[0:16, :], in0=s2[0:16, :], scalar1=-1.0, scalar2=None, op0=ALU.mult)
            P.dma('sp', outs['flogT'][:, tt * TT:(tt + 1) * TT], s2[0:16, :], [bs2], bdram_out)
        proj_fm(4096, 16, gates_post(gspec))
        memq0 = 4112

    pT = C.ring("P_pT", 2, [128, 2, TT], BF16)

    for hd in range(4):
        wt, bw = C.load_w(w_in, 0, 1024, memq0 + hd * 128, 128)
        for tt in range(ntt):
            ps, bps = C.next_ps()
            for k in range(8):
                MM(P, ps[:], wt[:, k, :], xb[:, k, tt * TT:(tt + 1) * TT], k == 0, k == 7, [bw, bxb[k]], bps)
            q, bq = stgb()
            I(P, 'act', 'copy', [bps], [bq], out=q[:], in_=ps[:])
            p, bp = pT()
            for mt in range(2):
                ps2, bps2 = C.next_ps()
                MM(P, ps2[:], kmT[:, hd, mt * 128:(mt + 1) * 128], q[:], True, True, [bkmT, bq], bps2)
                I(P, 'act', 'activation', [bps2], [bp], out=p[:, mt, :], in_=ps2[:], func=AF.Exp, scale=MEM_SCALE)
            po, bpo = C.next_ps()
            pd, bpd = C.next_ps()
            for mt in range(2):
                MM(P, po[:], vm[:, mt, hd * 128:(hd + 1) * 128], p[:, mt, :], mt == 0, mt == 1, [bvm, bp], bpo)
            for mt in range(2):
                MM(P, pd[:], C.onesb[:], p[:, mt, :], mt == 0, mt == 1, [C.bconst, bp], bpd)
            r, br = scr32()
            I(P, 'dve', 'reciprocal', [bpd], [br], out=r[:], in_=pd[:])
            s, bs = stgb()
            I(P, 'dve', 'tensor_tensor', [bpo, br], [bs], out=s[:], in0=po[:], in1=r[:], op=ALU.mult)
            P.dma('sp', outs['memoT'][hd * 128:(hd + 1) * 128, tt * TT:(tt + 1) * TT], s[:], [bs], bdram_out)


def consts_attn(C):
    P = C.P
    if 'tri01' in C.pcache:
        return
    tri = C.sbp("tri01", [128, 128], BF16)
    onesrow = C.sbp("onesrow", [1, 1024], F32)
    negone = C.sbp("negone", [1, 1], F32)
    b = C.bconst
    I(P, 'pool', 'memset', [], [b], tri[:], 1.0)
    I(P, 'pool', 'affine_select', [b], [b], out=tri[:], in_=tri[:], pattern=[[1, 128]], compare_op=ALU.is_ge, fill=0.0, base=0, channel_multiplier=-1)
    I(P, 'pool', 'memset', [], [b], onesrow[:], 1.0)
    I(P, 'pool', 'memset', [], [b], negone[:], -1.0)
    C.tri01, C.onesrow, C.negone = tri, onesrow, negone


def cumsum_row(C, dst_row, src_row, n, rb, wb):
    I(C.P, 'dve', 'tensor_tensor_scan', rb + [C.bconst], wb, out=dst_row, data0=C.onesrow[0:1, 0:n], data1=src_row, initial=0.0, op0=ALU.mult, op1=ALU.add)


def cumsum_dram_row(C, src, S, crow, bcrow, stg_r, bdram_in):
    P = C.P
    PC = min(1024, S)
    for pc in range(S // PC):
        sg, bsg = stg_r()
        P.dma('sp', sg[0:1, 0:PC], src[:, pc * PC:(pc + 1) * PC], [bdram_in], bsg)
        init = 0.0 if pc == 0 else crow[0:1, pc * PC - 1:pc * PC]
        I(P, 'dve', 'tensor_tensor_scan', [bsg, bcrow, C.bconst], [bcrow], out=crow[0:1, pc * PC:(pc + 1) * PC], data0=C.onesrow[0:1, 0:PC], data1=sg[0:1, 0:PC],
          initial=init, op0=ALU.mult, op1=ALU.add)


def attn_loop(C, S, KA, bKA, QA, bQA, kdim, VA, bVA, vcols, negc, E, bside, nacc, den_ones, epilogue, LA=2, ED=3, score_banks=(0, 1, 2)):
    P = C.P
    nq = S // 512
    pT = C.ring("A_pT", 5, [128, 512], BF16)
    C.set_pool('score', list(score_banks))
    blocks = []
    for qi in range(nq):
        nk = 4 * qi + 4
        for kt in range(nk):
            blocks.append((qi, kt, nk))
    sc = {}

    def rec_score(n):
        qi, kt, nk = blocks[n]
        j = kt - 4 * qi
        c0 = 128 * j if j > 0 else 0
        ps, bps = C.next_ps('score')
        MM(P, ps[:, c0:512], KA[0:kdim, kt * 128:(kt + 1) * 128], QA[0:kdim, qi * 512 + c0:(qi + 1) * 512], True, True, [bKA, bQA], bps)
        sc[n] = (ps, bps, c0, j)

    pend = []
    for n in range(min(LA, len(blocks))):
        rec_score(n)
    for n in range(len(blocks)):
        if n + LA < len(blocks):
            rec_score(n + LA)
        qi, kt, nk = blocks[n]
        ps, bps, c0, j = sc.pop(n)
        accs = [C.acc((qi % 2) * nacc + i) for i in range(nacc)]
        p, bp = pT()
        if negc is not None:
            I(P, 'act', 'activation', [bps, bside], [bp], out=p[:, c0:512], in_=ps[:, c0:512], func=AF.Exp, bias=negc[:, kt:kt + 1], scale=1.0)
        else:
            I(P, 'act', 'activation', [bps, bside], [bp], out=p[:, c0:512], in_=ps[:, c0:512], func=AF.Copy, scale=E[:, qi * (S // 128) + kt:qi * (S // 128) + kt + 1])
        if j >= 0:
            I(P, 'pool', 'tensor_tensor', [bp, C.bconst], [bp], out=p[:, c0:c0 + 128], in0=p[:, c0:c0 + 128], in1=C.tri01[:], op=ALU.mult)
        MM(P, accs[0][0][0:vcols, c0:512], VA[:, kt, :], p[:, c0:512], kt == 0, kt == nk - 1, [bVA, bp], accs[0][1])
        if den_ones:
            MM(P, accs[1][0][:, c0:512], C.onesb[:], p[:, c0:512], kt == 0, kt == nk - 1, [C.bconst, bp], accs[1][1])
        for e in pend:
            e[0] -= 1
        while pend and pend[0][0] <= 0:
            g = pend.pop(0)[1]
            for _ in g:
                pass
        if kt == nk - 1:
            g = epilogue(qi, accs)
            next(g)
            pend.append([ED, g])
    for e in pend:
        for _ in e[1]:
            pass


def mixer_fox(C, S, nh, qT, kT, v, sogT, flogT, mixT, bdram_in, bdram_out):
    P = C.P
    consts_attn(C)
    C.nring = 4
    C.set_pool('misc', [6, 7])
    KA = [C.sb("X_KA%d" % i, [65, S], BF16) for i in range(2)]
    QA = [C.sb("X_QA%d" % i, [65, S], BF16) for i in range(2)]
    VA = [C.sb("X_VA%d" % i, [128, S // 128, 65], BF16) for i in range(2)]
    crow = C.sb("X_crow", [1, S], F32)
    cb = C.sb("X_cb", [1, S], BF16)
    negc = [C.sb("X_negc%d" % i, [128, S // 128], F32) for i in range(2)]
    bKA = [C.buf("X_KA%d" % i) for i in range(2)]
    bQA = [C.buf("X_QA%d" % i) for i in range(2)]
    bVA = [C.buf("X_VA%d" % i) for i in range(2)]
    bcrow = C.buf("X_crow")
    bneg = [C.buf("X_negc%d" % i) for i in range(2)]
    stg_r = C.ring("X_stg", 2, [1, 1024], F32)
    sog_r = C.ring("X_sog", 3, [64, 512], F32)
    o_r = C.ring("X_o", 3, [65, 512], F32)
    ob_r = C.ring("X_ob", 2, [64, 512], BF16)
    for h in range(nh):
        i = h % 2
        P.dma('sp', KA[i][0:64, :], kT[h * 64:(h + 1) * 64, :], [bdram_in], bKA[i])
        I(P, 'pool', 'memset', [bKA[i]], [bKA[i]], KA[i][64:65, :], 1.0)
        P.dma('sp', QA[i][0:64, :], qT[h * 64:(h + 1) * 64, :], [bdram_in], bQA[i])
        P.dma('sp', VA[i][:, :, 0:64], v[:, h * 64:(h + 1) * 64].rearrange("(t p) d -> p t d", p=128), [bdram_in], bVA[i])
        I(P, 'pool', 'memset', [bVA[i]], [bVA[i]], VA[i][:, :, 64:65], 1.0)
        cumsum_dram_row(C, flogT[h:h + 1, :], S, crow, bcrow, stg_r, bdram_in)
        I(P, 'dve', 'tensor_copy', [bcrow], [bcrow], out=cb[:], in_=crow[:])
        P.dma('sp', QA[i][64:65, :], cb[:], [bcrow], bQA[i])
        ps, bps = C.next_ps()
        for t in range(S // 128):
            MM(P, ps[:, t:t + 1], crow[0:1, t * 128:(t + 1) * 128], C.negone[0:1, 0:1], True, True, [bcrow, C.bconst], bps)
        I(P, 'dve', 'tensor_copy', [bps], [bneg[i]], out=negc[i][:], in_=ps[:, 0:S // 128])

        def epi(qi, accs, h=h, i=i):
            po, bpo = accs[0]
            o, bo = o_r()
            I(P, 'dve', 'reciprocal', [bpo], [bo], out=o[64:65, :], in_=po[64:65, :])
            I(P, 'act', 'copy', [bpo], [bo], out=o[0:64, :], in_=po[0:64, :])
            sg, bsg = sog_r()
            P.dma('sp', sg[:], sogT[h * 64:(h + 1) * 64, qi * 512:(qi + 1) * 512], [bdram_in], bsg)
            yield
            pb, bpb = C.next_ps('misc')
            MM(P, pb[0:64, :], C.ones32[64:65, 0:64], o[64:65, :], True, True, [C.bconst, bo], bpb)
            I(P, 'dve', 'tensor_tensor', [bo, bpb], [bo], out=o[0:64, :], in0=o[0:64, :], in1=pb[0:64, :], op=ALU.mult)
            ob, bob = ob_r()
            I(P, 'dve', 'tensor_tensor', [bo, bsg], [bob], out=ob[:], in0=o[0:64, :], in1=sg[:], op=ALU.mult)
            P.dma('sp', mixT[h * 64:(h + 1) * 64, qi * 512:(qi + 1) * 512], ob[:], [bob], bdram_out)
            yield
        attn_loop(C, S, KA[i], bKA[i], QA[i], bQA[i], 65, VA[i], bVA[i], 65, negc[i], None, bneg[i], 1, False, epi, LA=3, ED=4, score_banks=(0, 1, 2, 3))
    C.nring = 8


def mixer_mlstm(C, S, nh, qT, kT, v, sogT, ilogT, flogT, ng, mixT, bdram_in, bdram_out):
    P = C.P
    consts_attn(C)
    C.nring = 4
    C.set_pool('misc', [3])
    nq, nkb = S // 512, S // 128
    KA = [C.sb("X_KA%d" % i, [65, S], BF16) for i in range(2)]
    QA = [C.sb("X_QA%d" % i, [65, S], BF16) for i in range(2)]
    VA = [C.sb("M_VA%d" % i, [128, nkb, 128], BF16) for i in range(2)]
    Brow = C.sb("M_Brow", [1, S], F32)
    xrow = C.sb("M_xrow", [1, 1024], F32)
    bBrow = C.buf("M_Brow")
    stg_r = C.ring("M_stg", 2, [1, 1024], F32)
    f_r = C.ring("M_f", 4, [1, 512], F32)
    refs = [C.sb("M_refs%d" % i, [1, 128], F32) for i in range(2)]
    E = [C.sb("M_E%d" % i, [128, 16 * 64], F32) for i in range(2)]
    gcol = C.sb("M_g", [128, 8], F32)
    avg = C.sb("M_avg", [128, 128], F32)
    bKA = [C.buf("X_KA%d" % i) for i in range(2)]
    bQA = [C.buf("X_QA%d" % i) for i in range(2)]
    bVA = [C.buf("M_VA%d" % i) for i in range(2)]
    bE = [C.buf("M_E%d" % i) for i in range(2)]
    bg = C.buf("M_g")
    I(P, 'pool', 'memset', [], [bg], avg[:], 1.0 / 128)
    P.dma('sp', gcol[:, 0:nh], ng.rearrange("(h d) -> d h", d=128), [], bg, allow_slow_non_contiguous=True)
    ld_r = C.ring("M_ld", 2, [64, 512], F32)
    sog_r = C.ring("M_sog", 3, [128, 512], F32)
    w_r = C.ring("M_w", 5, [128, 512], F32)
    ob_r = C.ring("M_ob", 2, [128, 512], BF16)
    for h in range(nh):
        i = h % 2
        cumsum_dram_row(C, flogT[h:h + 1, :], S, Brow, bBrow, stg_r, bdram_in)
        B3q = Brow[0:1, :].rearrange("p (t c) -> p t c", c=512)
        B3k = Brow[0:1, :].rearrange("p (t c) -> p t c", c=128)
        brf = bE[i]
        I(P, 'dve', 'tensor_copy', [bBrow], [brf], out=refs[i][0:1, 0:nq], in_=B3q[:, :, 0])
        I(P, 'dve', 'tensor_copy', [bBrow], [brf], out=refs[i][0:1, 64:64 + nkb], in_=B3k[:, :, 0])
        X3 = xrow[0:1, 0:nq * nkb].rearrange("p (a b) -> p a b", b=nkb)
        I(P, 'dve', 'tensor_tensor', [brf], [bBrow], out=X3, in0=refs[i][0:1, 0:nq].unsqueeze(2).to_broadcast([1, nq, nkb]),
          in1=refs[i][0:1, 64:64 + nkb].unsqueeze(1).to_broadcast([1, nq, nkb]), op=ALU.subtract)
        I(P, 'dve', 'tensor_scalar', [bBrow], [bBrow], out=xrow[0:1, 0:nq * nkb], in0=xrow[0:1, 0:nq * nkb], scalar1=60.0, scalar2=None, op0=ALU.min)
        for c in range((nq * nkb + 511) // 512):
            n = min(512, nq * nkb - c * 512)
            ps, bps = C.next_ps()
            MM(P, ps[:, 0:n], C.ones32[0:1, :], xrow[0:1, c * 512:c * 512 + n], True, True, [C.bconst, bBrow], bps)
            I(P, 'act', 'activation', [bps], [bE[i]], out=E[i][:, c * 512:c * 512 + n], in_=ps[:, 0:n], func=AF.Exp)
        for c in range(S // 512):
            fq, bfq = f_r()
            I(P, 'dve', 'tensor_scalar', [bBrow, brf], [bfq], out=fq[:], in0=Brow[0:1, c * 512:(c + 1) * 512], scalar1=refs[i][0:1, c:c + 1], scalar2=None, op0=ALU.subtract)
            I(P, 'act', 'activation', [bfq], [bfq], out=fq[:], in_=fq[:], func=AF.Exp)
            fk, bfk = f_r()
            P.dma('sp', fk[:], ilogT[h:h + 1, c * 512:(c + 1) * 512], [bdram_in], bfk)
            I(P, 'dve', 'tensor_tensor', [bfk, bBrow], [bfk], out=fk[:], in0=fk[:], in1=Brow[0:1, c * 512:(c + 1) * 512], op=ALU.subtract)
            I(P, 'dve', 'tensor_tensor', [bfk, brf], [bfk], out=fk[:].rearrange("p (t c) -> p t c", c=128), in0=fk[:].rearrange("p (t c) -> p t c", c=128),
              in1=refs[i][0:1, 64 + c * 4:64 + c * 4 + 4].unsqueeze(2).to_broadcast([1, 4, 128]), op=ALU.add)
            I(P, 'act', 'activation', [bfk], [bfk], out=fk[:], in_=fk[:], func=AF.Exp)
            for (src, dstA, bdst, frow, bfrow, sc) in ((qT, QA[i], bQA[i], fq, bfq, 0.125), (kT, KA[i], bKA[i], fk, bfk, 1.0)):
                ld, bld = ld_r()
                P.dma('sp', ld[:], src[h * 64:(h + 1) * 64, c * 512:(c + 1) * 512], [bdram_in], bld)
                ps, bps = C.next_ps()
                MM(P, ps[0:64, :], C.ones32[0:1, 0:64], frow[:], True, True, [C.bconst, bfrow], bps)
                I(P, 'dve', 'scalar_tensor_tensor', [bld, bps], [bdst], out=dstA[0:64, c * 512:(c + 1) * 512], in0=ld[:], scalar=sc, in1=ps[0:64, :], op0=ALU.mult, op1=ALU.mult)
        P.dma('sp', VA[i][:], v[:, h * 128:(h + 1) * 128].rearrange("(t p) d -> p t d", p=128), [bdram_in], bVA[i])

        def epi(qi, accs, h=h, i=i):
            (pn, bpn), (pd, bpd) = accs
            r, br = w_r()
            I(P, 'act', 'activation', [bpd], [br], out=r[:], in_=pd[:], func=AF.Abs)
            I(P, 'dve', 'tensor_scalar', [br], [br], out=r[:], in0=r[:], scalar1=1.0, scalar2=None, op0=ALU.max)
            I(P, 'dve', 'reciprocal', [br], [br], out=r[:], in_=r[:])
            hh, bh = w_r()
            I(P, 'dve', 'tensor_tensor', [bpn, br], [bh], out=hh[:], in0=pn[:], in1=r[:], op=ALU.mult)
            sq, bsq = w_r()
            I(P, 'act', 'activation', [bh], [bsq], out=sq[:], in_=hh[:], func=AF.Square)
            sg, bsg = sog_r()
            P.dma('sp', sg[:], sogT[h * 128:(h + 1) * 128, qi * 512:(qi + 1) * 512], [bdram_in], bsg)
            yield
            pm, bpm = C.next_ps('misc')
            MM(P, pm[:, 0:256], avg[:], hh[:, 0:256], True, True, [bg, bh], bpm)
            MM(P, pm[:, 256:512], avg[:], hh[:, 256:512], True, True, [bg, bh], bpm)
            mean, bmean = w_r()
            I(P, 'act', 'copy', [bpm], [bmean], out=mean[:], in_=pm[:])
            pv, bpv = C.next_ps('misc')
            MM(P, pv[:], avg[:], sq[:], True, True, [bg, bsq], bpv)
            I(P, 'act', 'activation', [bmean], [br], out=r[:], in_=mean[:], func=AF.Square)
            I(P, 'dve', 'tensor_tensor', [bpv, br], [br], out=r[:], in0=pv[:], in1=r[:], op=ALU.subtract)
            I(P, 'act', 'activation', [br, C.bconst], [br], out=r[:], in_=r[:], func=AF.Ln, bias=C.eps_n[:, 0:1], scale=1.0)
            I(P, 'act', 'activation', [br], [br], out=r[:], in_=r[:], func=AF.Exp, scale=-0.5)
            I(P, 'dve', 'tensor_tensor', [bh, bmean], [bh], out=hh[:], in0=hh[:], in1=mean[:], op=ALU.subtract)
            I(P, 'dve', 'scalar_tensor_tensor', [bh, br, bg], [bh], out=hh[:], in0=hh[:], scalar=gcol[:, h:h + 1], in1=r[:], op0=ALU.mult, op1=ALU.mult)
            ob, bob = ob_r()
            I(P, 'dve', 'tensor_tensor', [bh, bsg], [bob], out=ob[:], in0=hh[:], in1=sg[:], op=ALU.mult)
            P.dma('sp', mixT[h * 128:(h + 1) * 128, qi * 512:(qi + 1) * 512], ob[:], [bob], bdram_out)
            yield
        attn_loop(C, S, KA[i], bKA[i], QA[i], bQA[i], 64, VA[i], bVA[i], 128, None, E[i], bE[i], 2, True, epi)
    C.nring = 8


GC = 64
GB = 8


def consts_gdn(C):
    P = C.P
    if 'g_tri' in C.pcache:
        return
    b = C.bconst
    tri = C.sbp("g_tri", [64, 64], F32)
    idn = C.sbp("g_idn", [64, 64], F32)
    off = C.sbp("g_off", [64, 64], F32)
    mS = C.sbp("g_mS", [64, GB, 64], F32)
    mI = C.sbp("g_mI", [64, GB, 64], F32)
    idb = C.sbp("g_idb", [128, 128], BF16)
    offb = C.sbp("g_offb", [64, 64], BF16)
    mSb = C.sbp("g_mSb", [64, GB, 64], BF16)
    mIb = C.sbp("g_mIb", [64, GB, 64], BF16)
    avgb = C.sbp("g_avgb", [128, 128], BF16)
    avg = C.sbp("g_avg", [128, 128], F32)
    for t, op in ((tri, ALU.is_ge), (idn, ALU.is_equal), (off, ALU.not_equal)):
        I(P, 'pool', 'memset', [], [b], t[:], 1.0)
        I(P, 'pool', 'affine_select', [b], [b], out=t[:], in_=t[:], pattern=[[1, 64]], compare_op=op, fill=0.0, base=0, channel_multiplier=-1)
    I(P, 'pool', 'memset', [], [b], mS[:], 0.0)
    I(P, 'pool', 'affine_select', [b], [b], out=mS[:], in_=mS[:], pattern=[[0, GB], [-1, 64]], compare_op=ALU.is_gt, fill=-1.0e4, base=0, channel_multiplier=1)
    I(P, 'pool', 'memset', [], [b], mI[:], 0.0)
    I(P, 'pool', 'affine_select', [b], [b], out=mI[:], in_=mI[:], pattern=[[0, GB], [1, 64]], compare_op=ALU.is_ge, fill=-1.0e4, base=0, channel_multiplier=-1)
    I(P, 'pool', 'memset', [], [b], idb[:], 1.0)
    I(P, 'pool', 'affine_select', [b], [b], out=idb[:], in_=idb[:], pattern=[[1, 128]], compare_op=ALU.is_equal, fill=0.0, base=0, channel_multiplier=-1)
    I(P, 'pool', 'memset', [], [b], avg[:], 1.0 / 128)
    I(P, 'pool', 'tensor_copy', [b], [b], out=offb[:], in_=off[:])
    I(P, 'pool', 'tensor_copy', [b], [b], out=mSb[:], in_=mS[:])
    I(P, 'pool', 'tensor_copy', [b], [b], out=mIb[:], in_=mI[:])
    I(P, 'pool', 'memset', [], [b], avgb[:], 1.0 / 128)
    C.g_tri, C.g_idn, C.g_off, C.g_mS, C.g_mI, C.g_idb, C.g_avg = tri, idn, off, mS, mI, idb, avg
    C.g_offb, C.g_mSb, C.g_mIb, C.g_avgb = offb, mSb, mIb, avgb


def mixer_gdn(C, S, heads, qkvT, conv_w, szT, gT, betaT, norm_g, mixT, bdram_in, bdram_out):
    P = C.P
    consts_gdn(C)
    C.nring = 4
    NCH = S // GC
    NB = S // (GC * GB)
    nh = len(heads)
    CW = min(1024, S)
    ngl = C.sb("G_ng", [128, 1], F32)
    bng = C.buf("G_ng")
    P.dma('sp', ngl[:], norm_g.rearrange("(d o) -> d o", o=1), [], bng)
    cst = C.ring("G_cst", 2, [128, CW + 3], F32)
    cac = C.ring("G_cac", 2, [128, CW], F32)
    csq = C.ring("G_csq", 1, [128, 512], F32)
    grow = C.ring("G_grow", 1, [1, 512], F32)
    HT = []
    for hi in range(2):
        d = {}
        for nm in ('qb', 'kb', 'vb'):
            d[nm] = C.sb("G_%s%d" % (nm, hi), [128, S], BF16)
            d['b' + nm] = C.buf("G_%s%d" % (nm, hi))
        d['cw'] = C.sb("G_cw%d" % hi, [128, 3, 4], F32)
        d['tm'] = C.sb("G_tm%d" % hi, [64, 8, NCH], F32)
        d['dl'] = C.sb("G_dl%d" % hi, [128, NCH], F32)
        d['S32'] = C.sb("G_S32_%d" % hi, [128, 128], F32)
        d['Sb'] = C.sb("G_Sb_%d" % hi, [128, 128], BF16)
        for nm in ('cw', 'tm', 'dl', 'S32', 'Sb'):
            d['b' + nm] = C.buf("G_%s%d" % (nm, hi))
        HT.append(d)

    def phaseA(hidx):
        hi = hidx % 2
        (q0, k0, v0, gr, o0) = heads[hidx]
        d = HT[hi]
        for xi, r0 in enumerate((q0, k0, v0)):
            P.dma('sp', d['cw'][:, xi, :], conv_w[:, r0:r0 + 128].rearrange("j c -> c j"), [], d['bcw'], allow_slow_non_contiguous=True)
        for xi, (r0, nm) in enumerate(((q0, 'qb'), (k0, 'kb'), (v0, 'vb'))):
            dst, bdst = d[nm], d['b' + nm]
            for cc in range(S // CW):
                st, bst = cst()
                if cc == 0:
                    I(P, 'pool', 'memset', [bst], [bst], st[:, 0:3], 0.0)
                    P.dma('sp', st[:, 3:3 + CW], qkvT[r0:r0 + 128, 0:CW], [bdram_in], bst)
                else:
                    P.dma('sp', st[:, 0:3 + CW], qkvT[r0:r0 + 128, cc * CW - 3:(cc + 1) * CW], [bdram_in], bst)
                ac, bac = cac()
                I(P, 'dve', 'tensor_scalar', [bst, d['bcw']], [bac], out=ac[:], in0=st[:, 3:3 + CW], scalar1=d['cw'][:, xi, 3:4], scalar2=None, op0=ALU.mult)
                for j in range(3):
                    I(P, 'dve', 'scalar_tensor_tensor', [bst, bac, d['bcw']], [bac], out=ac[:], in0=st[:, j:j + CW], scalar=d['cw'][:, xi, j:j + 1], in1=ac[:], op0=ALU.mult, op1=ALU.add)
                I(P, 'act', 'activation', [bac], [bac], out=ac[:], in_=ac[:], func=AF.Silu)
                if nm == 'vb':
                    I(P, 'act', 'copy', [bac], [bdst], out=dst[:, cc * CW:(cc + 1) * CW], in_=ac[:])
                else:
                    for t in range(CW // 512):
                        sq, bsq = csq()
                        I(P, 'act', 'activation', [bac], [bsq], out=sq[:], in_=ac[:, t * 512:(t + 1) * 512], func=AF.Square)
                        ps, bps = C.next_ps()
                        MM(P, ps[:], C.ones32[:], sq[:], True, True, [C.bconst, bsq], bps)
                        I(P, 'act', 'activation', [bps, C.bconst], [bsq], out=sq[:], in_=ps[:], func=AF.Ln, bias=C.eps_n[:, 0:1], scale=1.0)
                        I(P, 'act', 'activation', [bsq], [bsq], out=sq[:], in_=sq[:], func=AF.Exp, scale=-0.5)
                        sc = (128 ** -0.5) if nm == 'qb' else 1.0
                        I(P, 'dve', 'scalar_tensor_tensor', [bac, bsq], [bdst], out=dst[:, cc * CW + t * 512:cc * CW + (t + 1) * 512], in0=ac[:, t * 512:(t + 1) * 512], scalar=sc, in1=sq[:], op0=ALU.mult, op1=ALU.mult)
                yield
        tm = d['tm']
        PC = 512
        for w, src in enumerate((gT, betaT)):
            ps, bps = C.next_ps()
            for pc in range(S // PC):
                sg, bsg = grow()
                P.dma('sp', sg[0:1, 0:PC], src[gr:gr + 1, pc * PC:(pc + 1) * PC], [bdram_in], bsg)
                for nn in range(PC // 64):
                    col = pc * (PC // 64) + nn
                    MM(P, ps[0:64, col:col + 1], sg[0:1, nn * 64:(nn + 1) * 64], C.ones32[0:1, 0:1], True, True, [bsg, C.bconst], bps)
            I(P, 'dve', 'tensor_copy', [bps], [d['btm']], out=tm[:, w, :], in_=ps[0:64, 0:NCH])
        ps, bps = C.next_ps()
        MM(P, ps[0:64, 0:NCH], C.g_tri[:], tm[:, 0, :], True, True, [C.bconst, d['btm']], bps)
        I(P, 'dve', 'tensor_copy', [bps], [d['btm']], out=tm[:, 2, :], in_=ps[0:64, 0:NCH])
        ps2, bps2 = C.next_ps()
        MM(P, ps2[:, 0:NCH], C.ones32[0:64, :], tm[:, 0, :], True, True, [C.bconst, d['btm']], bps2)
        I(P, 'act', 'activation', [bps2], [d['bdl']], out=d['dl'][:], in_=ps2[:, 0:NCH], func=AF.Exp)
        I(P, 'act', 'activation', [d['btm']], [d['btm']], out=tm[:, 3, :], in_=tm[:, 2, :], func=AF.Exp)
        I(P, 'dve', 'tensor_scalar', [d['btm']], [d['btm']], out=tm[:, 5, :], in0=tm[:, 1, :], scalar1=-1.0, scalar2=None, op0=ALU.mult)
        I(P, 'dve', 'tensor_tensor', [d['btm']], [d['btm']], out=tm[:, 7, :], in0=tm[:, 3, :], in1=tm[:, 5, :], op=ALU.mult)
        I(P, 'dve', 'tensor_tensor', [d['btm'], bps2], [d['btm']], out=tm[:, 4, :], in0=ps2[0:64, 0:NCH], in1=tm[:, 2, :], op=ALU.subtract)
        I(P, 'act', 'activation', [d['btm']], [d['btm']], out=tm[:, 4, :], in_=tm[:, 4, :], func=AF.Exp)
        I(P, 'pool', 'memset', [], [d['bS32']], d['S32'][:], 0.0)
        yield
        if hidx == 0:
            C.dbg('d_qb', d['qb'][:, 0:512], [d['bqb']]); C.dbg('d_kb', d['kb'][:, 0:512], [d['bkb']]); C.dbg('d_vb', d['vb'][:, 0:512], [d['bvb']])
            C.dbg('d_tm', d['tm'][:, :, 0:8], [d['btm']]); C.dbg('d_dl', d['dl'][:, 0:8], [d['bdl']])

    f32r = C.ring("G_f32", 4, [64, 512], F32)
    b16r = C.ring("G_b16", 8, [64, 512], BF16)
    Xr = C.ring("G_X", 2, [64, 512], F32)
    gtr = C.ring("G_gtr", 2, [64, 512], F32)
    gtrb = C.ring("G_gtrb", 2, [64, 512], BF16)
    Rr = C.ring("G_R", 2, [64, GB, 256], BF16)
    kgr = C.ring("G_kg", 2, [64, GB, 128], BF16)
    UWr = C.ring("G_UW", 2, [64, GB, 256], BF16)
    atr = C.ring("G_at", 2, [64, 512], BF16)
    NTr = C.ring("G_NT", 2, [128, GB, 128], BF16)
    Qpr = C.ring("G_Qp", 2, [128, 512], BF16)
    q32r = C.ring("G_q32", 1, [128, 512], F32)
    o32r = C.ring("G_o32", 2, [128, 512], F32)
    sqbr = C.ring("G_sqb", 1, [128, 512], BF16)
    obr = C.ring("G_ob", 1, [128, 512], BF16)
    szr = C.ring("G_sz", 1, [128, 512], F32)
    Xbr = C.ring("G_Xb", 2, [64, 512], BF16)

    def bc(t, w, n0):
        return t[:, w, n0:n0 + GB].unsqueeze(2).to_broadcast([64, GB, 64])

    def v3(t):
        return t[:, :].rearrange("p (n x) -> p n x", x=64)

    def pre(hidx, b):
        hi = hidx % 2
        d = HT[hi]
        tm, btm = d['tm'], d['btm']
        n0 = b * GB
        tok = slice(b * 512, (b + 1) * 512)
        gtri, bgtri = gtr()
        I(P, 'dve', 'tensor_tensor', [btm, C.bconst], [bgtri], out=v3(gtri), in0=C.g_tri[:, :].unsqueeze(1).to_broadcast([64, GB, 64]), in1=bc(tm, 0, n0), op=ALU.mult)
        p1, bp1 = C.next_ps()
        MM(P, p1[0:64, :], C.ones32[0:64, 0:64], gtri[:], True, False, [C.bconst, bgtri], bp1)
        MM(P, p1[0:64, :], C.g_idb[0:64, 0:64], C.g_mIb[:].rearrange("p n x -> p (n x)"), False, True, [C.bconst], bp1)
        E2, bE2 = f32r()
        I(P, 'dve', 'tensor_tensor', [bp1, btm], [bE2], out=v3(E2), in0=p1[0:64, :].rearrange("p (n x) -> p n x", x=64), in1=bc(tm, 2, n0), op=ALU.subtract)
        I(P, 'act', 'activation', [bE2], [bE2], out=E2[:], in_=E2[:], func=AF.Exp)
        yield
        ngt, bngt = gtr()
        I(P, 'dve', 'tensor_scalar', [bgtri], [bngt], out=ngt[:], in0=gtri[:], scalar1=-1.0, scalar2=None, op0=ALU.mult)
        p2, bp2 = C.next_ps()
        MM(P, p2[0:64, :], C.ones32[0:64, 0:64], ngt[:], True, False, [C.bconst, bngt], bp2)
        MM(P, p2[0:64, :], C.g_idb[0:64, 0:64], C.g_mSb[:].rearrange("p n x -> p (n x)"), False, True, [C.bconst], bp2)
        E1, bE1 = f32r()
        I(P, 'dve', 'tensor_tensor', [bp2, btm], [bE1], out=v3(E1), in0=p2[0:64, :].rearrange("p (n x) -> p n x", x=64), in1=bc(tm, 2, n0), op=ALU.add)
        I(P, 'act', 'activation', [bE1], [bE1], out=E1[:], in_=E1[:], func=AF.Exp)
        yield
        bdg, bbdg = gtrb()
        I(P, 'dve', 'tensor_tensor', [btm, C.bconst], [bbdg], out=v3(bdg), in0=C.g_idn[:, :].unsqueeze(1).to_broadcast([64, GB, 64]), in1=bc(tm, 5, n0), op=ALU.mult)
        p3, bp3 = C.next_ps()
        MM(P, p3[0:64, :], C.g_offb[:], bdg[:], True, True, [C.bconst, bbdg], bp3)
        pk, bpk = C.next_ps()
        pq, bpq = C.next_ps()
        for n in range(GB):
            cs = slice(b * 512 + n * 64, b * 512 + (n + 1) * 64)
            MM(P, pk[0:64, n * 64:(n + 1) * 64], d['kb'][:, cs], d['kb'][:, cs], True, True, [d['bkb']], bpk)
        for n in range(GB):
            cs = slice(b * 512 + n * 64, b * 512 + (n + 1) * 64)
            MM(P, pq[0:64, n * 64:(n + 1) * 64], d['kb'][:, cs], d['qb'][:, cs], True, True, [d['bkb'], d['bqb']], bpq)
        N0t, bN0t = f32r()
        I(P, 'dve', 'tensor_tensor', [bpk, bE1], [bN0t], out=N0t[:], in0=pk[0:64, :], in1=E1[:], op=ALU.mult)
        N0, bN0 = b16r()
        I(P, 'dve', 'tensor_tensor', [bN0t, btm], [bN0], out=v3(N0), in0=v3(N0t), in1=bc(tm, 5, n0), op=ALU.mult)
        NT0t, bNT0t = f32r()
        I(P, 'dve', 'tensor_tensor', [bpk, bE2], [bNT0t], out=NT0t[:], in0=pk[0:64, :], in1=E2[:], op=ALU.mult)
        X, bX = Xr()
        I(P, 'dve', 'tensor_tensor', [bNT0t, bp3], [bX], out=X[:], in0=NT0t[:], in1=p3[0:64, :], op=ALU.mult)
        NT0, bNT0 = b16r()
        I(P, 'act', 'copy', [bX], [bNT0], out=NT0[:], in_=X[:])
        at, bat = atr()
        I(P, 'dve', 'tensor_tensor', [bpq, bE2], [bat], out=at[:], in0=pq[0:64, :], in1=E2[:], op=ALU.mult)
        if hi == 0 and b == 0:
            C.dbg('d_E1', E1[:], [bE1]); C.dbg('d_E2', E2[:], [bE2]);  C.dbg('d_at', at[:], [bat])
        yield
        I(P, 'dve', 'tensor_tensor', [bX, C.bconst], [bX], out=v3(X), in0=v3(X), in1=C.g_idn[:, :].unsqueeze(1).to_broadcast([64, GB, 64]), op=ALU.add)
        Xs, bXs = Xbr()
        I(P, 'act', 'copy', [bX], [bXs], out=Xs[:], in_=X[:])
        Pj, bPj, PTj, bPTj = N0, bN0, NT0, bNT0
        for lvl in range(1, 6):
            pa, bpa = C.next_ps()
            for n in range(GB):
                sl = slice(n * 64, (n + 1) * 64)
                MM(P, pa[0:64, sl], PTj[:, sl], Pj[:, sl], True, True, [bPTj, bPj], bpa)
            Pn, bPn = b16r()
            I(P, 'act', 'copy', [bpa], [bPn], out=Pn[:], in_=pa[0:64, :])
            if lvl < 5:
                pb_, bpb_ = C.next_ps()
                for n in range(GB):
                    sl = slice(n * 64, (n + 1) * 64)
                    MM(P, pb_[0:64, sl], Pj[:, sl], PTj[:, sl], True, True, [bPTj, bPj], bpb_)
                PTn, bPTn = b16r()
                I(P, 'act', 'copy', [bpb_], [bPTn], out=PTn[:], in_=pb_[0:64, :])
            px, bpx = C.next_ps()
            for n in range(GB):
                sl = slice(n * 64, (n + 1) * 64)
                MM(P, px[0:64, sl], Pn[:, sl], Xs[:, sl], True, True, [bPn, bXs], bpx)
            I(P, 'dve', 'tensor_tensor', [bpx, bX], [bX], out=X[:], in0=X[:], in1=px[0:64, :], op=ALU.add)
            Xs, bXs = Xbr()
            I(P, 'act', 'copy', [bX], [bXs], out=Xs[:], in_=X[:])
            Pj, bPj = Pn, bPn
            if lvl < 5:
                PTj, bPTj = PTn, bPTn
            yield
        if hi == 0 and b == 0:
            C.dbg('d_X', X[:], [bX])
        Xb, bXb = Xs, bXs
        Rt, bRt = Rr()
        kg, bkg = kgr()
        for (src, bsrc, which) in ((d['kb'], d['bkb'], 'k'), (d['vb'], d['bvb'], 'v')):
            for half in range(2):
                pt, bpt = C.next_ps()
                ptb = pt[:].bitcast(BF16)
                for n in range(4):
                    nn = half * 4 + n
                    cs = slice(b * 512 + nn * 64, b * 512 + (nn + 1) * 64)
                    P.mm(lambda e, o=ptb[0:64, n * 128:(n + 1) * 128], i_=src[:, cs]: e.transpose(o, i_, C.g_idb[:]), [bsrc, C.bconst], bpt)
                pv3 = ptb[0:64, 0:512].rearrange("p (n x) -> p n x", x=128)
                hs = slice(half * 4, half * 4 + 4)

                def bc4(w):
                    return tm[:, w, n0 + half * 4:n0 + half * 4 + 4].unsqueeze(2).to_broadcast([64, 4, 128])
                if which == 'k':
                    I(P, 'dve', 'tensor_tensor', [bpt, btm], [bRt], out=Rt[:, hs, 128:256], in0=pv3, in1=bc4(7), op=ALU.mult)
                    I(P, 'dve', 'tensor_tensor', [bpt, btm], [bkg], out=kg[:, hs, :], in0=pv3, in1=bc4(4), op=ALU.mult)
                else:
                    I(P, 'dve', 'tensor_tensor', [bpt, btm], [bRt], out=Rt[:, hs, 0:128], in0=pv3, in1=bc4(1), op=ALU.mult)
            yield
        UW, bUW = UWr()
        for pr in range(4):
            pu, bpu = C.next_ps()
            for n2 in range(2):
                n = pr * 2 + n2
                MM(P, pu[0:64, n2 * 256:(n2 + 1) * 256], Xb[:, n * 64:(n + 1) * 64], Rt[:, n, :], True, True, [bXb, bRt], bpu)
            I(P, 'act', 'copy', [bpu], [bUW], out=UW[:, pr * 2:pr * 2 + 2, :].rearrange("p n x -> p (n x)"), in_=pu[0:64, :])
        yield
        NT, bNT = NTr()
        for pr in range(2):
            pn, bpn = C.next_ps()
            for n4 in range(4):
                n = pr * 4 + n4
                MM(P, pn[:, n4 * 128:(n4 + 1) * 128], UW[:, n, 128:256], kg[:, n, :], True, True, [bUW, bkg], bpn)
            I(P, 'act', 'copy', [bpn], [bNT], out=NT[:, pr * 4:pr * 4 + 4, :].rearrange("p n x -> p (n x)"), in_=pn[:])
        yield
        edd, bedd = gtrb()
        I(P, 'dve', 'tensor_tensor', [btm, C.bconst], [bedd], out=v3(edd), in0=C.g_idn[:, :].unsqueeze(1).to_broadcast([64, GB, 64]), in1=bc(tm, 3, n0), op=ALU.mult)
        pe_, bpe_ = C.next_ps()
        MM(P, pe_[:], C.onesb[0:64, :], edd[:], True, True, [C.bconst, bedd], bpe_)
        q32, bq32 = q32r()
        I(P, 'dve', 'tensor_tensor', [d['bqb'], bpe_], [bq32], out=q32[:], in0=d['qb'][:, tok], in1=pe_[:], op=ALU.mult)
        pw, bpw = C.next_ps()
        for n in range(GB):
            MM(P, pw[:, n * 64:(n + 1) * 64], UW[:, n, 128:256], at[:, n * 64:(n + 1) * 64], True, True, [bUW, bat], bpw)
        Qp, bQp = Qpr()
        I(P, 'dve', 'tensor_tensor', [bq32, bpw], [bQp], out=Qp[:], in0=q32[:], in1=pw[:], op=ALU.add)
        if hi == 0 and b == 0:
            C.dbg('d_R', Rt[:].rearrange("p n x -> p (n x)"), [bRt]); C.dbg('d_kg', kg[:].rearrange("p n x -> p (n x)"), [bkg])
            C.dbg('d_UW', UW[:].rearrange("p n x -> p (n x)"), [bUW]); C.dbg('d_NT', NT[:].rearrange("p n x -> p (n x)"), [bNT]); C.dbg('d_Qp', Qp[:], [bQp])
        d['cur'] = dict(UW=UW, bUW=bUW, kg=kg, bkg=bkg, at=at, bat=bat, NT=NT, bNT=bNT, Qp=Qp, bQp=bQp)
        yield

    def chain(hidx, b, cur):
        hi = hidx % 2
        d = HT[hi]
        po, bpo = C.acc_o[hi]
        for n in range(GB):
            gn = b * GB + n
            first = (gn == 0)
            osl = po[:, n * 64:(n + 1) * 64]
            if not first:
                MM(P, osl, d['Sb'][:], cur['Qp'][:, n * 64:(n + 1) * 64], True, False, [d['bSb'], cur['bQp']], bpo)
            MM(P, osl, cur['UW'][:, n, 0:128], cur['at'][:, n * 64:(n + 1) * 64], first, True, [cur['bUW'], cur['bat']], bpo)
            ps, bps = C.acc_s[hi]
            if not first:
                MM(P, ps[:, 0:128], cur['NT'][:, n, :], d['Sb'][:], True, False, [cur['bNT'], d['bSb']], bps)
            MM(P, ps[:, 0:128], cur['kg'][:, n, :], cur['UW'][:, n, 0:128], first, True, [cur['bkg'], cur['bUW']], bps)
            I(P, 'dve', 'scalar_tensor_tensor', [bps, d['bS32'], d['bdl']], [d['bS32']], out=d['S32'][:], in0=d['S32'][:], scalar=d['dl'][:, gn:gn + 1], in1=ps[:, 0:128], op0=ALU.mult, op1=ALU.add)
            I(P, 'act', 'copy', [d['bS32']], [d['bSb']], out=d['Sb'][:], in_=d['S32'][:])
            yield
        (q0, k0, v0, gr, o0) = heads[hidx]
        o32, bo32 = o32r()
        I(P, 'act', 'copy', [bpo], [bo32], out=o32[:], in_=po[:])
        sqb, bsqb = sqbr()
        I(P, 'act', 'activation', [bpo], [bsqb], out=sqb[:], in_=po[:], func=AF.Square)
        pm, bpm = C.next_ps()
        MM(P, pm[:], C.g_avgb[:], sqb[:], True, True, [C.bconst, bsqb], bpm)
        sq, bsq = o32r()
        I(P, 'act', 'activation', [bpm, C.bconst], [bsq], out=sq[:], in_=pm[:], func=AF.Ln, bias=C.eps_n[:, 0:1], scale=1.0)
        I(P, 'act', 'activation', [bsq], [bsq], out=sq[:], in_=sq[:], func=AF.Exp, scale=-0.5)
        I(P, 'dve', 'scalar_tensor_tensor', [bo32, bsq, bng], [bo32], out=o32[:], in0=o32[:], scalar=ngl[:, 0:1], in1=sq[:], op0=ALU.mult, op1=ALU.mult)
        sz, bsz = szr()
        P.dma('sp', sz[:], szT[o0:o0 + 128, b * 512:(b + 1) * 512], [bdram_in], bsz)
        ob, bob = obr()
        I(P, 'dve', 'tensor_tensor', [bo32, bsz], [bob], out=ob[:], in0=o32[:], in1=sz[:], op=ALU.mult)
        P.dma('sp', mixT[o0:o0 + 128, b * 512:(b + 1) * 512], ob[:], [bob], bdram_out)
        yield

    C.acc_o = [C.acc(0), C.acc(1)]
    C.acc_s = [C.acc(2), C.acc(3)]
    def headB(hidx):
        hi = hidx % 2
        for _ in pre(hidx, 0):
            yield
        for b in range(NB):
            cg = chain(hidx, b, HT[hi]['cur'])
            pg = pre(hidx, b + 1) if b + 1 < NB else iter(())
            c_alive = p_alive = True
            while c_alive or p_alive:
                if c_alive:
                    try:
                        next(cg)
                    except StopIteration:
                        c_alive = False
                if p_alive:
                    for _ in range(2):
                        try:
                            next(pg)
                        except StopIteration:
                            p_alive = False
                            break
                yield

    for _ in phaseA(0):
        pass
    for hidx in range(nh):
        bg = phaseA(hidx + 1) if hidx + 1 < nh else None
        tick = 0
        for _ in headB(hidx):
            tick += 1
            if bg is not None and tick % 3 == 0:
                try:
                    next(bg)
                except StopIteration:
                    bg = None
        if bg is not None:
            for _ in bg:
                pass
    C.nring = 8


import ml_dtypes
from concourse.bass_utils import run_bass_kernel_spmd

KINDS = ['gdn', 'mlstm', 'fox', 'gdn']
NCOLS = {'gdn': 4624, 'mlstm': 3600, 'fox': 4624}
TC = 2048
SEQ = 8192
_BF = ml_dtypes.bfloat16
_progs = {}


def _p_out_specs(kind, T):
    if kind == 'gdn':
        return dict(qkvT=([3072, T], F32), szT=([1024, T], F32), gT=([8, T], F32), betaT=([8, T], F32), memoT=([512, T], BF16))
    if kind == 'mlstm':
        return dict(qT=([512, T], F32), kT=([512, T], F32), v=([T, 1024], BF16), sogT=([1024, T], F32), ilogT=([8, T], F32), flogT=([8, T], F32), memoT=([512, T], BF16))
    return dict(qT=([1024, T], BF16), kT=([1024, T], BF16), v=([T, 1024], BF16), sogT=([1024, T], F32), flogT=([16, T], F32), memoT=([512, T], BF16))


def _p_prm_specs(kind):
    if kind == 'gdn':
        return dict(a_log=[8], dt_bias=[8])
    if kind == 'mlstm':
        return dict(b_gate=[2, 8])
    return dict(b_f=[16], qk_g=[2, 64])


def _new_nc():
    return bass.Bass("TRN2", target_bir_lowering=False)


def build_P(kind):
    key = ('P', kind)
    if key in _progs:
        return _progs[key]
    nc = _new_nc()
    di = lambda n, s, dt=F32: nc.dram_tensor(n, s, dt, kind="ExternalInput").ap()
    xT = di("xT", [1024, TC]); memT = di("memT", [1024, 256]); w_in = di("w_in", [1024, NCOLS[kind]]); w_kv = di("w_kv", [1024, 1024])
    prm = {k: di(k, s) for k, s in _p_prm_specs(kind).items()}
    outs = {k: nc.dram_tensor(k, s, dt, kind="ExternalOutput").ap() for k, (s, dt) in _p_out_specs(kind, TC).items()}
    with ExitStack() as st:
        P = Prog(nc, st)
        C = Ctx(nc, st, P)
        stage_P(C, TC, kind, xT, memT, w_in, w_kv, prm, outs, P.buf("din"), P.buf("dout"))
        P.barrier()
        P.emit()
    _progs[key] = nc
    return nc


def build_F():
    key = ('F',)
    if key in _progs:
        return _progs[key]
    nc = _new_nc()
    di = lambda n, s, dt=F32: nc.dram_tensor(n, s, dt, kind="ExternalInput").ap()
    mixT = di("mixT", [1024, TC], BF16); memoT = di("memoT", [512, TC], BF16); xT = di("xT", [1024, TC])
    w_out = di("w_out", [1536, 1024]); w_up = di("w_up", [1024, 4096]); w_down = di("w_down", [4096, 1024])
    l1g = di("l1g", [1024]); l1b = di("l1b", [1024]); l2g = di("l2g", [1024]); l2b = di("l2b", [1024])
    yT = nc.dram_tensor("yT", [1024, TC], F32, kind="ExternalOutput").ap()
    with ExitStack() as st:
        P = Prog(nc, st)
        C = Ctx(nc, st, P)
        stage_F(C, TC, mixT, memoT, xT, yT, w_out, l1g, l1b, w_up, w_down, l2g, l2b, P.buf("din"), P.buf("dout"))
        P.barrier()
        P.emit()
    _progs[key] = nc
    return nc


def build_M(kind):
    key = ('M', kind)
    if key in _progs:
        return _progs[key]
    nc = _new_nc()
    S = SEQ
    di = lambda n, s, dt=F32: nc.dram_tensor(n, s, dt, kind="ExternalInput").ap()
    with ExitStack() as st:
        P = Prog(nc, st)
        C = Ctx(nc, st, P)
        bin_, bout = P.buf("din"), P.buf("dout")
        mixT = nc.dram_tensor("mixT", [256, S], BF16, kind="ExternalOutput").ap()
        if kind == 'gdn':
            qkvT = di("qkvT", [768, S]); conv_w = di("conv_w", [4, 768]); szT = di("szT", [256, S]); gT = di("gT", [2, S]); betaT = di("betaT", [2, S]); ng = di("ng", [128])
            heads = [(h * 128, 256 + h * 128, 512 + h * 128, h, h * 128) for h in range(2)]
            mixer_gdn(C, S, heads, qkvT, conv_w, szT, gT, betaT, ng, mixT, bin_, bout)
        elif kind == 'mlstm':
            qT = di("qT", [128, S]); kT = di("kT", [128, S]); v = di("v", [S, 256], BF16)
            sogT = di("sogT", [256, S]); flogT = di("flogT", [2, S]); ilogT = di("ilogT", [2, S]); ng = di("ng", [256])
            mixer_mlstm(C, S, 2, qT, kT, v, sogT, ilogT, flogT, ng, mixT, bin_, bout)
        else:
            qT = di("qT", [256, S], BF16); kT = di("kT", [256, S], BF16); v = di("v", [S, 256], BF16)
            sogT = di("sogT", [256, S]); flogT = di("flogT", [4, S])
            mixer_fox(C, S, 4, qT, kT, v, sogT, flogT, mixT, bin_, bout)
        P.barrier()
        P.emit()
    _progs[key] = nc
    return nc


FUSED = False
_W_SPECS = dict(gdn_w_in=[2, 1024, 4624], gdn_conv_w=[2, 4, 3072], gdn_a_log=[2, 8], gdn_dt_bias=[2, 8], gdn_norm_g=[2, 128],
                mlstm_w_in=[1, 1024, 3600], mlstm_b_gate=[1, 2, 8], mlstm_norm_g=[1, 1024], fox_w_in=[1, 1024, 4624], fox_b_f=[1, 16],
                fox_qk_g=[1, 2, 64], mem_w_kv=[4, 1024, 1024], w_out=[4, 1536, 1024], ln1_g=[4, 1024], ln1_b=[4, 1024],
                w_up=[4, 1024, 4096], w_down=[4, 4096, 1024], ln2_g=[4, 1024], ln2_b=[4, 1024])


def build_fused(nlayers=4):
    key = ('fused', nlayers)
    if key in _progs:
        return _progs[key]
    nc = _new_nc()
    S = SEQ
    di = lambda n, s, dt=F32: nc.dram_tensor(n, s, dt, kind="ExternalInput").ap()
    sc = lambda n, s, dt=F32: nc.dram_tensor(n, s, dt).ap()
    xT0 = di("xT", [1024, S]); memT = di("memT", [1024, 256])
    W = {k: di(k, shp) for k, shp in _W_SPECS.items()}
    yT = nc.dram_tensor("yT", [1024, S], F32, kind="ExternalOutput").ap()
    xa, xb_ = sc("xa", [1024, S]), sc("xb", [1024, S])
    scr = {}
    for kind in ('gdn', 'mlstm', 'fox'):
        scr[kind] = {k: sc("%s_%s" % (kind, k), shp, dt) for k, (shp, dt) in _p_out_specs(kind, S).items()}
    mixT = sc("mixT", [1024, S], BF16)
    with ExitStack() as st:
        P = Prog(nc, st)
        C = Ctx(nc, st, P)
        cur = xT0
        for li in range(nlayers):
            kind, j = KINDS[li], li // 3
            nxt = yT if li == nlayers - 1 else (xa if li % 2 == 0 else xb_)
            o = scr[kind]
            if kind == 'gdn':
                w_in = W['gdn_w_in'][j]
                prm = dict(a_log=W['gdn_a_log'][j], dt_bias=W['gdn_dt_bias'][j])
            elif kind == 'mlstm':
                w_in = W['mlstm_w_in'][j]
                prm = dict(b_gate=W['mlstm_b_gate'][j])
            else:
                w_in = W['fox_w_in'][j]
                prm = dict(b_f=W['fox_b_f'][j], qk_g=W['fox_qk_g'][j])
            C.stage_begin()
            for tc in range(S // TC):
                tok = slice(tc * TC, (tc + 1) * TC)
                outs = {k: (a[tok, :] if k == 'v' else a[:, tok]) for k, a in o.items()}
                stage_P(C, TC, kind, cur[:, tok], memT, w_in, W['mem_w_kv'][li], prm, outs, C.buf("din"), C.buf("dout"))
            C.stage_end()
            C.stage_begin()
            bi, bo = C.buf("din"), C.buf("dout")
            if kind == 'gdn':
                heads = [(h * 128, 1024 + h * 128, 2048 + h * 128, h, h * 128) for h in range(8)]
                mixer_gdn(C, S, heads, o['qkvT'], W['gdn_conv_w'][j], o['szT'], o['gT'], o['betaT'], W['gdn_norm_g'][j], mixT, bi, bo)
            elif kind == 'mlstm':
                mixer_mlstm(C, S, 8, o['qT'], o['kT'], o['v'], o['sogT'], o['ilogT'], o['flogT'], W['mlstm_norm_g'][j], mixT, bi, bo)
            else:
                mixer_fox(C, S, 16, o['qT'], o['kT'], o['v'], o['sogT'], o['flogT'], mixT, bi, bo)
            C.stage_end()
            C.stage_begin()
            for tc in range(S // TC):
                tok = slice(tc * TC, (tc + 1) * TC)
                stage_F(C, TC, mixT[:, tok], o['memoT'][:, tok], cur[:, tok], nxt[:, tok], W['w_out'][li], W['ln1_g'][li], W['ln1_b'][li],
                        W['w_up'][li], W['w_down'][li], W['ln2_g'][li], W['ln2_b'][li], C.buf("din"), C.buf("dout"))
            C.stage_end()
            cur = nxt
    _progs[key] = nc
    return nc


def kernel_fused(inputs):
    f32 = np.float32
    x = np.asarray(inputs['x'], f32)
    mem = np.asarray(inputs['mem'], f32)
    wts = {}
    for k, shp in _W_SPECS.items():
        wts[k] = _c(np.asarray(inputs[k], f32).reshape(shp))
    ims = []
    for c in range(8):
        b = c % 2
        ims.append(dict(xT=_c(x[b].T), memT=_c(mem[b].T), **wts))
    res = _run(build_fused(), ims)
    out = np.empty((2, SEQ, 1024), f32)
    for b in range(2):
        out[b] = np.asarray(res[b]['yT']).T
    return out


def _run(nc, in_maps):
    res = run_bass_kernel_spmd(nc, in_maps, core_ids=list(range(8)))
    return res.results


def _cat_tok(outs, name, b, axis):
    return np.concatenate([np.asarray(outs[b * 4 + c][name]) for c in range(4)], axis=axis)


def _c(a):
    return np.ascontiguousarray(a)


def kernel(x, mem, gdn_w_in, gdn_conv_w, gdn_a_log, gdn_dt_bias, gdn_norm_g,
           mlstm_w_in, mlstm_b_gate, mlstm_norm_g, fox_w_in, fox_b_f, fox_qk_g,
           mem_w_kv, w_out, ln1_g, ln1_b, w_up, w_down, ln2_g, ln2_b):
    if FUSED:
        return kernel_fused(dict(x=x, mem=mem, gdn_w_in=gdn_w_in, gdn_conv_w=gdn_conv_w, gdn_a_log=gdn_a_log, gdn_dt_bias=gdn_dt_bias,
                                 gdn_norm_g=gdn_norm_g, mlstm_w_in=mlstm_w_in, mlstm_b_gate=mlstm_b_gate, mlstm_norm_g=mlstm_norm_g,
                                 fox_w_in=fox_w_in, fox_b_f=fox_b_f, fox_qk_g=fox_qk_g, mem_w_kv=mem_w_kv, w_out=w_out, ln1_g=ln1_g,
                                 ln1_b=ln1_b, w_up=w_up, w_down=w_down, ln2_g=ln2_g, ln2_b=ln2_b))
    f32 = np.float32
    x = np.asarray(x, f32)
    mem = np.asarray(mem, f32)
    xT = [_c(x[c // 4, (c % 4) * TC:(c % 4 + 1) * TC, :].T) for c in range(8)]
    memT = [_c(mem[b].T) for b in range(2)]
    for li in range(4):
        kind, j = KINDS[li], li // 3
        if kind == 'gdn':
            w_in = np.asarray(gdn_w_in[j], f32)
            prm = dict(a_log=np.asarray(gdn_a_log[j], f32), dt_bias=np.asarray(gdn_dt_bias[j], f32))
        elif kind == 'mlstm':
            w_in = np.asarray(mlstm_w_in[j], f32)
            prm = dict(b_gate=np.asarray(mlstm_b_gate[j], f32))
        else:
            w_in = np.asarray(fox_w_in[j], f32)
            prm = dict(b_f=np.asarray(fox_b_f[j], f32), qk_g=np.asarray(fox_qk_g[j], f32))
        wkv = np.asarray(mem_w_kv[li], f32)
        ims = [dict(xT=xT[c], memT=memT[c // 4], w_in=w_in, w_kv=wkv, **prm) for c in range(8)]
        po = _run(build_P(kind), ims)
        ims = []
        for jc in range(8):
            b, hg = jc // 4, jc % 4
            if kind == 'gdn':
                qkv = _cat_tok(po, 'qkvT', b, 1) if hg == 0 else qkv_cache
                qkv_cache = qkv
                hs = [2 * hg, 2 * hg + 1]
                rows = np.concatenate([np.arange(o + h * 128, o + (h + 1) * 128) for o in (0, 1024, 2048) for h in hs])
                if hg == 0:
                    sz_c = _cat_tok(po, 'szT', b, 1); g_c = _cat_tok(po, 'gT', b, 1); be_c = _cat_tok(po, 'betaT', b, 1)
                ims.append(dict(qkvT=_c(qkv[rows]), conv_w=_c(np.asarray(gdn_conv_w[j], f32)[:, rows]), szT=_c(sz_c[hs[0] * 128:(hs[1] + 1) * 128]),
                                gT=_c(g_c[hs[0]:hs[1] + 1]), betaT=_c(be_c[hs[0]:hs[1] + 1]), ng=np.asarray(gdn_norm_g[j], f32)))
            elif kind == 'mlstm':
                if hg == 0:
                    q_c = _cat_tok(po, 'qT', b, 1); k_c = _cat_tok(po, 'kT', b, 1); v_c = _cat_tok(po, 'v', b, 0)
                    so_c = _cat_tok(po, 'sogT', b, 1); il_c = _cat_tok(po, 'ilogT', b, 1); fl_c = _cat_tok(po, 'flogT', b, 1)
                h0 = 2 * hg
                ims.append(dict(qT=_c(q_c[h0 * 64:(h0 + 2) * 64]), kT=_c(k_c[h0 * 64:(h0 + 2) * 64]), v=_c(v_c[:, h0 * 128:(h0 + 2) * 128]),
                                sogT=_c(so_c[h0 * 128:(h0 + 2) * 128]), ilogT=_c(il_c[h0:h0 + 2]), flogT=_c(fl_c[h0:h0 + 2]),
                                ng=_c(np.asarray(mlstm_norm_g[j], f32)[h0:h0 + 2].reshape(-1))))
            else:
                if hg == 0:
                    q_c = _cat_tok(po, 'qT', b, 1); k_c = _cat_tok(po, 'kT', b, 1); v_c = _cat_tok(po, 'v', b, 0)
                    so_c = _cat_tok(po, 'sogT', b, 1); fl_c = _cat_tok(po, 'flogT', b, 1)
                h0 = 4 * hg
                ims.append(dict(qT=_c(q_c[h0 * 64:(h0 + 4) * 64]), kT=_c(k_c[h0 * 64:(h0 + 4) * 64]), v=_c(v_c[:, h0 * 64:(h0 + 4) * 64]),
                                sogT=_c(so_c[h0 * 64:(h0 + 4) * 64]), flogT=_c(fl_c[h0:h0 + 4])))
        mo = _run(build_M(kind), ims)
        ims = []
        for c in range(8):
            b, sc = c // 4, c % 4
            mixT = np.concatenate([np.asarray(mo[b * 4 + hg]['mixT'])[:, sc * TC:(sc + 1) * TC] for hg in range(4)], axis=0)
            ims.append(dict(mixT=_c(mixT), memoT=_c(np.asarray(po[c]['memoT'])), xT=xT[c],
                            w_out=np.asarray(w_out[li], f32), w_up=np.asarray(w_up[li], f32), w_down=np.asarray(w_down[li], f32),
                            l1g=np.asarray(ln1_g[li], f32), l1b=np.asarray(ln1_b[li], f32), l2g=np.asarray(ln2_g[li], f32), l2b=np.asarray(ln2_b[li], f32)))
        fo = _run(build_F(), ims)
        xT = [_c(np.asarray(fo[c]['yT'])) for c in range(8)]
    out = np.empty((2, SEQ, 1024), f32)
    for c in range(8):
        out[c // 4, (c % 4) * TC:(c % 4 + 1) * TC, :] = xT[c].T
    return out
```
